# Optimizing a Trainium2 kernel written in Bass

```python
import math
import jax
import jax.numpy as jnp
from jax import lax
import numpy as np

D_MODEL = 1024
BATCH = 2
SEQ = 8192
DEPTH = 1
DEC_BATCH = 32
DEC_SEQ = 1
PAST_LEN = 8192
PAGE_SIZE = 128

N_MEM = 256
NSA_HEADS = 8
NSA_KV_HEADS = 2
NSA_GROUP = NSA_HEADS // NSA_KV_HEADS
HEAD_DIM = 64
CMP_BLOCK = 32
CMP_STRIDE = 16
SEL_BLOCK = 64
SEL_TOP = 16
WINDOW = 512
Q_BLOCK = 128
N_KV_SLOTS = 4
FORCE_BONUS = 1e4
GLA_HEADS = 4
GLA_DK = 64
GLA_DV = 128
GLA_GATE_RANK = 16
GLA_GATE_TAU = 16.0
GLA_CHUNK = 64
X_HEADS = 4
X_HEAD_DIM = 128
N_BUCKETS = 32
MAX_DISTANCE = 128
D_FF = 2816
CONV_W = 3
N_BRANCH = 3
EPS = 1e-6
NEG_INF = -1e30
TINY = 1e-30

NSA_QW = NSA_HEADS * HEAD_DIM
NSA_KVW = 6 * NSA_KV_HEADS * HEAD_DIM
NSA_GW = 3 * NSA_HEADS
GLA_KW = GLA_HEADS * GLA_DK
GLA_VW = GLA_HEADS * GLA_DV
X_W = X_HEADS * X_HEAD_DIM
IN_SIZES = (NSA_QW, NSA_KVW, NSA_GW, GLA_KW, GLA_KW, GLA_VW, GLA_GATE_RANK, GLA_VW, X_W, N_BRANCH * D_MODEL)
D_IN = sum(IN_SIZES)

kernel_name = 'nsa_gla_memory_hybrid_step'


def _rmsnorm(x, g):
    xf = x.astype(jnp.float32)
    y = xf * lax.rsqrt(jnp.mean(xf * xf, axis=-1, keepdims=True) + EPS)
    return (y * g.astype(jnp.float32)).astype(x.dtype)


def _t5_bucket(rel):
    n = jnp.maximum(rel, 0)
    max_exact = N_BUCKETS // 2
    nf = jnp.maximum(n, 1).astype(jnp.float32)
    large = max_exact + (jnp.log(nf / max_exact) / math.log(MAX_DISTANCE / max_exact)
                         * (N_BUCKETS - max_exact)).astype(jnp.int32)
    return jnp.where(n < max_exact, n, jnp.minimum(large, N_BUCKETS - 1))


def _masked_softmax(s, mask, axis):
    s = jnp.where(mask, s, NEG_INF)
    m = jnp.max(s, axis=axis, keepdims=True)
    p = jnp.where(mask, jnp.exp(s - m), 0.0)
    return p / jnp.maximum(jnp.sum(p, axis=axis, keepdims=True), TINY)


def _compress(rows, pe, w1, w2):
    b, length, nkv, dh = rows.shape
    n_chunks = length // CMP_STRIDE
    n_cmp = (length - CMP_BLOCK) // CMP_STRIDE + 1
    chunks = rows[:, :n_chunks * CMP_STRIDE].reshape(b, n_chunks, CMP_STRIDE, nkv, dh)
    hid = None
    for r in range(CMP_BLOCK // CMP_STRIDE):
        sl = slice(r * CMP_STRIDE, (r + 1) * CMP_STRIDE)
        part = jnp.einsum('bcjkd,jde->bcke', chunks + pe[sl][None, None, :, None, :], w1[sl])[:, r:r + n_cmp]
        hid = part if hid is None else hid + part
    return jax.nn.gelu(hid) @ w2


def _cmp_sel_overlap(n_cmp, n_sel):
    cs = (jnp.arange(n_cmp) * CMP_STRIDE)[:, None]
    ss = (jnp.arange(n_sel) * SEL_BLOCK)[None, :]
    return ((cs < ss + SEL_BLOCK) & (cs + CMP_BLOCK > ss)).astype(jnp.float32)


def _nsa_attend(q, gates, k_cmp, v_cmp, k_sel, v_sel, k_win, v_win, q0, pw0, rel_bias):
    b, tq = q.shape[:2]
    length = k_sel.shape[1]
    lw = k_win.shape[1]
    n_cmp = k_cmp.shape[1]
    n_sel = -(-length // SEL_BLOCK)
    top = min(SEL_TOP, n_sel)
    qb = min(Q_BLOCK, tq)
    nqb = -(-tq // qb)
    tp = nqb * qb
    f32 = jnp.float32
    scale = HEAD_DIM ** -0.5

    def blocks(a):
        a = jnp.pad(a, [(0, 0), (0, tp - tq)] + [(0, 0)] * (a.ndim - 2))
        return jnp.moveaxis(a.reshape((b, nqb, qb) + a.shape[2:]), 1, 0)

    q_blk = blocks(q.reshape(b, tq, NSA_KV_HEADS, NSA_GROUP, HEAD_DIM))
    g_blk = blocks(gates.reshape(b, tq, NSA_KV_HEADS, NSA_GROUP, 3))
    pad_sel = n_sel * SEL_BLOCK - length
    ks_blk = jnp.pad(k_sel, ((0, 0), (0, pad_sel), (0, 0), (0, 0))).reshape(b, n_sel, SEL_BLOCK, NSA_KV_HEADS, HEAD_DIM)
    vs_blk = jnp.pad(v_sel, ((0, 0), (0, pad_sel), (0, 0), (0, 0))).reshape(b, n_sel, SEL_BLOCK, NSA_KV_HEADS, HEAD_DIM)
    kw_pad = jnp.pad(k_win, ((0, 0), (WINDOW, qb), (0, 0), (0, 0)))
    vw_pad = jnp.pad(v_win, ((0, 0), (WINDOW, qb), (0, 0), (0, 0)))
    kc = k_cmp.astype(f32)
    vc = v_cmp.astype(f32)
    cmp_end = jnp.arange(n_cmp) * CMP_STRIDE + (CMP_BLOCK - 1)
    sel_id = jnp.arange(n_sel)
    overlap = _cmp_sel_overlap(n_cmp, n_sel)
    bias_hg = rel_bias.astype(f32).reshape(N_BUCKETS, NSA_KV_HEADS, NSA_GROUP)
    bias_kg = jnp.transpose(bias_hg, (1, 0, 2))
    b_ix = jnp.arange(b)[:, None, None, None]
    k_ix = jnp.arange(NSA_KV_HEADS)[None, None, :, None]
    win_len = WINDOW + qb

    def one_block(args):
        n, qn, gn = args
        qs = q0 + n * qb
        tpos = qs + jnp.arange(qb)
        qf = qn.astype(f32) * scale
        s_c = jnp.einsum('bqkgd,bckd->bqkgc', qf, kc)
        bias_c = bias_hg[_t5_bucket(tpos[:, None] - cmp_end[None, :])]
        s_c = s_c + jnp.transpose(bias_c, (0, 2, 3, 1))[None]
        mask_c = (cmp_end[None, :] <= tpos[:, None])[None, :, None, None, :]
        p_c = _masked_softmax(s_c, mask_c, -1)
        o_c = jnp.einsum('bqkgc,bckd->bqkgd', p_c, vc)
        imp = jnp.einsum('bqkgc,cj->bqkj', p_c, overlap)
        tblk = (tpos // SEL_BLOCK)[:, None]
        valid = sel_id[None, :] * SEL_BLOCK <= tpos[:, None]
        forced = (sel_id[None, :] == 0) | (sel_id[None, :] == tblk) | (sel_id[None, :] == tblk - 1)
        score = jnp.where(valid[None, :, None, :],
                          imp + jnp.where(forced, FORCE_BONUS, 0.0)[None, :, None, :], NEG_INF)
        _, idx = lax.top_k(score, top)
        k_g = ks_blk[b_ix, idx, :, k_ix, :].astype(f32)
        v_g = vs_blk[b_ix, idx, :, k_ix, :].astype(f32)
        pos = idx[..., None] * SEL_BLOCK + jnp.arange(SEL_BLOCK)
        rel = tpos[None, :, None, None, None] - pos
        s_s = jnp.einsum('bqkgd,bqktsd->bqkgts', qf, k_g)
        bias_s = bias_kg[k_ix[..., None], _t5_bucket(rel)]
        s_s = s_s + jnp.moveaxis(bias_s, -1, 3)
        mask_s = ((rel >= 0) & (pos < length))[:, :, :, None]
        p_s = _masked_softmax(s_s, mask_s, (-2, -1))
        o_s = jnp.einsum('bqkgts,bqktsd->bqkgd', p_s, v_g)
        start = qs - pw0
        k_w = lax.dynamic_slice_in_dim(kw_pad, start, win_len, axis=1).astype(f32)
        v_w = lax.dynamic_slice_in_dim(vw_pad, start, win_len, axis=1).astype(f32)
        wpos = qs - WINDOW + jnp.arange(win_len)
        rel_w = tpos[:, None] - wpos[None, :]
        s_w = jnp.einsum('bqkgd,bskd->bqkgs', qf, k_w)
        s_w = s_w + jnp.transpose(bias_hg[_t5_bucket(rel_w)], (0, 2, 3, 1))[None]
        mask_w = ((rel_w >= 0) & (rel_w < WINDOW) & (wpos[None, :] >= pw0)
                  & (wpos[None, :] < pw0 + lw))[None, :, None, None, :]
        p_w = _masked_softmax(s_w, mask_w, -1)
        o_w = jnp.einsum('bqkgs,bskd->bqkgd', p_w, v_w)
        gf = gn.astype(f32)
        o = gf[..., 0:1] * o_c + gf[..., 1:2] * o_s + gf[..., 2:3] * o_w
        return o.astype(q.dtype)

    out = lax.map(one_block, (jnp.arange(nqb), q_blk, g_blk))
    return jnp.moveaxis(out, 0, 1).reshape(b, tp, NSA_HEADS, HEAD_DIM)[:, :tq]


def _gla(q, k, v, log_a, s0):
    b, t, nh, _ = q.shape
    dv = v.shape[-1]
    c = min(GLA_CHUNK, t)
    nc = -(-t // c)
    tp = nc * c

    def prep(a):
        a = jnp.pad(a.astype(jnp.float32), ((0, 0), (0, tp - t), (0, 0), (0, 0)))
        return jnp.transpose(a.reshape(b, nc, c, nh, a.shape[-1]), (1, 0, 3, 2, 4))

    causal = jnp.tril(jnp.ones((c, c), dtype=bool))[None, None, :, :, None]

    def step(s, inp):
        qc, kc, vc, ac = inp
        cb = jnp.cumsum(ac, axis=2)
        diff = cb[:, :, :, None, :] - cb[:, :, None, :, :]
        decay = jnp.exp(jnp.where(causal, diff, -jnp.inf))
        att = jnp.einsum('bhtd,bhsd,bhtsd->bhts', qc, kc, decay)
        o = jnp.einsum('bhtd,bhde->bhte', qc * jnp.exp(cb), s) + jnp.einsum('bhts,bhse->bhte', att, vc)
        last = cb[:, :, -1, :]
        s = jnp.exp(last)[..., None] * s + jnp.einsum('bhsd,bhse->bhde', kc * jnp.exp(last[:, :, None, :] - cb), vc)
        return s, o

    s_new, o = lax.scan(step, s0.astype(jnp.float32), (prep(q), prep(k), prep(v), prep(log_a)))
    o = jnp.transpose(o, (1, 0, 3, 2, 4)).reshape(b, tp, nh, dv)[:, :t]
    return o, s_new


def _conv_ffn(h, conv_past, w_up, conv_w, conv_b, w_down):
    t = h.shape[1]
    u, g = jnp.split(h @ w_up, 2, axis=-1)
    g_ext = jnp.concatenate([conv_past.astype(g.dtype), g], axis=1)
    gc = conv_b + sum(conv_w[j] * g_ext[:, j:j + t] for j in range(CONV_W))
    out = (jax.nn.gelu(gc) * u) @ w_down
    return out, g_ext[:, g_ext.shape[1] - (CONV_W - 1):]


def _memory_kv(mem, g_mem, w_mem_kv, g_x_k):
    b, m, _ = mem.shape
    kv = (_rmsnorm(mem, g_mem) @ w_mem_kv).reshape(b, m, 2, X_HEADS, X_HEAD_DIM)
    return jnp.stack([_rmsnorm(kv[:, :, 0], g_x_k), kv[:, :, 1]], axis=2)


def _layer(x, past_rows, win_past, gla_s0, conv_past, mem_kv, q0, p):
    b, t, _ = x.shape
    f32 = jnp.float32
    h = _rmsnorm(x, p['g_mix'])
    splits = [int(s) for s in np.cumsum(IN_SIZES)[:-1]]
    (nsa_q, nsa_kv, nsa_g, gla_q, gla_k, gla_v, gla_lr, gla_r, x_q, merge_g) = jnp.split(h @ p['w_in'], splits, axis=-1)

    g_k = p['g_nsa_k']
    q = _rmsnorm(nsa_q.reshape(b, t, NSA_HEADS, HEAD_DIM), p['g_nsa_q'])
    kv = nsa_kv.reshape(b, t, 6, NSA_KV_HEADS, HEAD_DIM)
    new_rows = jnp.stack([kv[:, :, 0], kv[:, :, 1], _rmsnorm(kv[:, :, 2], g_k[1]), kv[:, :, 3]], axis=2)
    new_win = jnp.stack([_rmsnorm(kv[:, :, 4], g_k[2]), kv[:, :, 5]], axis=2)
    rows = jnp.concatenate([past_rows.astype(x.dtype), new_rows], axis=1)
    win_all = jnp.concatenate([win_past.astype(x.dtype), new_win], axis=1)
    k_cmp = _rmsnorm(_compress(rows[:, :, 0], p['cmp_k_pe'], p['cmp_k_w1'], p['cmp_k_w2']), g_k[0])
    v_cmp = _compress(rows[:, :, 1], p['cmp_v_pe'], p['cmp_v_w1'], p['cmp_v_w2'])
    nsa_gates = jax.nn.sigmoid(nsa_g.reshape(b, t, NSA_HEADS, 3))
    o_nsa = _nsa_attend(q, nsa_gates, k_cmp, v_cmp, rows[:, :, 2], rows[:, :, 3],
                        win_all[:, :, 0], win_all[:, :, 1], q0, q0 - win_past.shape[1], p['rel_bias'])
    new_win_state = win_all[:, win_all.shape[1] - min(WINDOW, q0 + t):]

    gq = gla_q.reshape(b, t, GLA_HEADS, GLA_DK) * (GLA_DK ** -0.5)
    gk = gla_k.reshape(b, t, GLA_HEADS, GLA_DK)
    gv = gla_v.reshape(b, t, GLA_HEADS, GLA_DV)
    log_a = jax.nn.log_sigmoid((gla_lr @ p['w_gla_gate'] + p['b_gla_gate']).astype(f32)) / GLA_GATE_TAU
    o_gla, s_new = _gla(gq, gk, gv, log_a.reshape(b, t, GLA_HEADS, GLA_DK), gla_s0)
    o_gla = _rmsnorm(o_gla.astype(x.dtype), p['g_gla_o']).reshape(b, t, GLA_VW) * jax.nn.silu(gla_r)

    xq = _rmsnorm(x_q.reshape(b, t, X_HEADS, X_HEAD_DIM), p['g_x_q'])
    s_x = jnp.einsum('bthd,bmhd->bhtm', xq.astype(f32) * (X_HEAD_DIM ** -0.5), mem_kv[:, :, 0].astype(f32))
    o_x = jnp.einsum('bhtm,bmhd->bthd', jax.nn.softmax(s_x, axis=-1), mem_kv[:, :, 1].astype(f32))
    o_x = o_x.astype(x.dtype).reshape(b, t, X_W)

    mg = jax.nn.sigmoid(merge_g.reshape(b, t, N_BRANCH, D_MODEL))
    merged = (mg[:, :, 0] * (o_nsa.reshape(b, t, NSA_QW) @ p['w_nsa_out'])
              + mg[:, :, 1] * (o_gla @ p['w_gla_out'])
              + mg[:, :, 2] * (o_x @ p['w_x_out']))
    x1 = x + merged @ p['w_o']

    ffn, conv_state = _conv_ffn(_rmsnorm(x1, p['g_ffn']), conv_past, p['w_up'], p['conv_w'], p['conv_b'], p['w_down'])
    y = x1 + ffn
    return y, new_rows, new_win_state, s_new.astype(gla_s0.dtype), conv_state


def setup_inputs(seed: int = 0) -> dict:
    key = jax.random.key(seed)
    keys = iter(jax.random.split(key, 48))
    f32 = jnp.float32

    def nrm(shape, scale):
        return jax.random.normal(next(keys), shape, f32) * scale

    def gain(shape):
        return 1.0 + 0.05 * jax.random.normal(next(keys), shape, f32)

    n_pages = PAST_LEN // PAGE_SIZE
    n_used = DEC_BATCH * n_pages
    n_phys = n_used + max(1, n_used // 4)
    win_buf = min(WINDOW, PAST_LEN)
    d = D_MODEL
    x_prompt = nrm((BATCH, SEQ, d), 1.0)
    x_sample = nrm((DEC_BATCH, DEC_SEQ, d), 1.0)
    cache_kv = nrm((n_phys, PAGE_SIZE, N_KV_SLOTS, NSA_KV_HEADS, HEAD_DIM), 1.0)
    cache_win = nrm((DEC_BATCH, win_buf, 2, NSA_KV_HEADS, HEAD_DIM), 1.0)
    state_gla = nrm((DEC_BATCH, GLA_HEADS, GLA_DK, GLA_DV), 1.0)
    state_conv = nrm((DEC_BATCH, CONV_W - 1, D_FF), 1.0)
    cache_mem = nrm((DEC_BATCH, N_MEM, 2, X_HEADS, X_HEAD_DIM), 1.0)
    page_table = jax.random.permutation(next(keys), n_phys)[:n_used].reshape(DEC_BATCH, n_pages).astype(jnp.int32)
    mem_prompt = nrm((BATCH, N_MEM, d), 1.0)
    return {
        'x_prompt': x_prompt, 'x_sample': x_sample, 'cache_kv': cache_kv, 'cache_win': cache_win,
        'state_gla': state_gla, 'state_conv': state_conv, 'cache_mem': cache_mem,
        'page_table': page_table, 'mem_prompt': mem_prompt,
        'g_mix': gain((d,)),
        'w_in': nrm((d, D_IN), d ** -0.5),
        'g_nsa_q': gain((HEAD_DIM,)),
        'g_nsa_k': gain((3, HEAD_DIM)),
        'cmp_k_pe': nrm((CMP_BLOCK, HEAD_DIM), 0.1),
        'cmp_k_w1': nrm((CMP_BLOCK, HEAD_DIM, HEAD_DIM), (CMP_BLOCK * HEAD_DIM) ** -0.5),
        'cmp_k_w2': nrm((HEAD_DIM, HEAD_DIM), HEAD_DIM ** -0.5),
        'cmp_v_pe': nrm((CMP_BLOCK, HEAD_DIM), 0.1),
        'cmp_v_w1': nrm((CMP_BLOCK, HEAD_DIM, HEAD_DIM), (CMP_BLOCK * HEAD_DIM) ** -0.5),
        'cmp_v_w2': nrm((HEAD_DIM, HEAD_DIM), HEAD_DIM ** -0.5),
        'rel_bias': nrm((N_BUCKETS, NSA_HEADS), 0.5),
        'w_gla_gate': nrm((GLA_GATE_RANK, GLA_KW), GLA_GATE_RANK ** -0.5),
        'b_gla_gate': nrm((GLA_KW,), 0.1),
        'g_gla_o': gain((GLA_DV,)),
        'g_mem': gain((d,)),
        'w_mem_kv': nrm((d, 2 * X_W), d ** -0.5),
        'g_x_q': gain((X_HEAD_DIM,)),
        'g_x_k': gain((X_HEAD_DIM,)),
        'w_nsa_out': nrm((NSA_QW, d), NSA_QW ** -0.5),
        'w_gla_out': nrm((GLA_VW, d), GLA_VW ** -0.5),
        'w_x_out': nrm((X_W, d), X_W ** -0.5),
        'w_o': nrm((d, d), d ** -0.5),
        'g_ffn': gain((d,)),
        'w_up': nrm((d, 2 * D_FF), d ** -0.5),
        'conv_w': nrm((CONV_W, D_FF), CONV_W ** -0.5),
        'conv_b': nrm((D_FF,), 0.02),
        'w_down': nrm((D_FF, d), D_FF ** -0.5),
    }


def reference(x_prompt, x_sample, cache_kv, cache_win, state_gla, state_conv, cache_mem, page_table, mem_prompt,
              g_mix, w_in, g_nsa_q, g_nsa_k, cmp_k_pe, cmp_k_w1, cmp_k_w2, cmp_v_pe, cmp_v_w1, cmp_v_w2,
              rel_bias, w_gla_gate, b_gla_gate, g_gla_o, g_mem, w_mem_kv, g_x_q, g_x_k,
              w_nsa_out, w_gla_out, w_x_out, w_o, g_ffn, w_up, conv_w, conv_b, w_down):
    p = dict(g_mix=g_mix, w_in=w_in, g_nsa_q=g_nsa_q, g_nsa_k=g_nsa_k,
             cmp_k_pe=cmp_k_pe, cmp_k_w1=cmp_k_w1, cmp_k_w2=cmp_k_w2,
             cmp_v_pe=cmp_v_pe, cmp_v_w1=cmp_v_w1, cmp_v_w2=cmp_v_w2,
             rel_bias=rel_bias, w_gla_gate=w_gla_gate, b_gla_gate=b_gla_gate, g_gla_o=g_gla_o,
             g_x_q=g_x_q, w_nsa_out=w_nsa_out, w_gla_out=w_gla_out, w_x_out=w_x_out, w_o=w_o,
             g_ffn=g_ffn, w_up=w_up, conv_w=conv_w, conv_b=conv_b, w_down=w_down)
    dt = x_prompt.dtype
    bp = x_prompt.shape[0]
    db = x_sample.shape[0]

    mem_kv_p = _memory_kv(mem_prompt, g_mem, w_mem_kv, g_x_k)
    y_p, rows_p, win_p, gla_p, conv_p = _layer(
        x_prompt,
        jnp.zeros((bp, 0, N_KV_SLOTS, NSA_KV_HEADS, HEAD_DIM), dt),
        jnp.zeros((bp, 0, 2, NSA_KV_HEADS, HEAD_DIM), dt),
        jnp.zeros((bp, GLA_HEADS, GLA_DK, GLA_DV), dt),
        jnp.zeros((bp, CONV_W - 1, D_FF), dt),
        mem_kv_p, 0, p)

    n_pages = page_table.shape[1]
    past_len = n_pages * cache_kv.shape[1]
    past_rows = cache_kv[page_table].reshape((db, past_len) + cache_kv.shape[2:])
    y_s, rows_s, win_s, gla_s, conv_s = _layer(
        x_sample, past_rows, cache_win, state_gla, state_conv, cache_mem, past_len, p)

    return (y_p, y_s, rows_p, win_p, gla_p, conv_p, mem_kv_p, rows_s, win_s, gla_s, conv_s)
```

```python
import numpy as np
import ml_dtypes
from contextlib import ExitStack
import concourse.bass as bass
import concourse.mybir as mybir
from concourse.bass_utils import run_bass_kernel_spmd

F32 = mybir.dt.float32
BF16 = mybir.dt.bfloat16
I32 = mybir.dt.int32
AF = mybir.ActivationFunctionType
ALU = mybir.AluOpType
AX = mybir.AxisListType

D = 1024
DFF = 2816
NFF = 22
EPS = 1e-6
TINY = 1e-30
NEG = -30000.0
NSLOT = 65
DMIN = -160
DMAX = 700
ND = DMAX - DMIN + 1
C0, C1, C2, C3, C4, C5, C6, CMG = 0, 512, 1024, 1320, 1832, 2344, 2856, 3368
W_IN_PERM = np.concatenate([np.arange(0, 512), np.arange(512, 1024), np.arange(1024, 1280),
                            np.arange(2328, 2344), np.arange(1280, 1304), np.arange(1304, 1816),
                            np.arange(1816, 2328), np.arange(2344, 2856), np.arange(2856, 3368),
                            np.arange(3368, 6440)])
FIRST_Q = 47
PREFIX_WIN_FROM = 43


class Buf:
    __slots__ = ('w', 'r', 'excl')

    def __init__(self):
        self.w = None
        self.r = {}
        self.excl = False


class T:
    def __init__(self, t):
        self.t = t
        self.b = Buf()

    def __getitem__(self, k):
        return self.t[k]


class KB:
    def __init__(self, nds=24):
        self.nc = bass.Bass('TRN2', target_bir_lowering=False)
        nc = self.nc
        self.es = ExitStack()
        self.eng = dict(pe=nc.tensor, act=nc.scalar, dve=nc.vector, pool=nc.gpsimd, sp=nc.sync)
        self.sem = {e: self.es.enter_context(nc.semaphore('s_' + e)) for e in ('pe', 'act', 'dve', 'pool')}
        self.cnt = dict.fromkeys(self.sem, 0)
        self.NDS = nds
        self.dsem = [self.es.enter_context(nc.semaphore('d%d' % i)) for i in range(nds)]
        self.dcnt = [0] * nds
        self.dnext = 0
        self.dnext_sw = 0
        self.waited = {e: {} for e in self.eng}
        self.nbuf = 0
        self.ninstr = 0

    def sb(self, shape, dt=F32):
        self.nbuf += 1
        return T(self.es.enter_context(self.nc.sbuf_tensor('sb%d' % self.nbuf, list(shape), dt)))

    def ps(self, shape, dt=F32):
        self.nbuf += 1
        t = T(self.es.enter_context(self.nc.psum_tensor('ps%d' % self.nbuf, list(shape), dt)))
        t.b.excl = True
        return t

    def din(self, name, shape, dt=F32):
        return self.nc.dram_tensor(name, list(shape), dt, kind="ExternalInput").ap()

    def dout(self, name, shape, dt=F32):
        return self.nc.dram_tensor(name, list(shape), dt, kind="ExternalOutput").ap()

    def dint(self, name, shape, dt=F32):
        return self.nc.dram_tensor(name, list(shape), dt, kind="Internal").ap()

    def _wait(self, e, dep):
        k, v = dep
        w = self.waited[e]
        if w.get(k, 0) >= v:
            return
        w[k] = v
        sem = self.sem[k] if isinstance(k, str) else self.dsem[k]
        self.eng[e].wait_ge(sem, v)
        self.ninstr += 1

    def _deps(self, e, reads, writes):
        deps = set()
        for b in reads:
            if b.w is not None:
                deps.add(b.w)
        for b in writes:
            if b.w is not None:
                deps.add(b.w)
            deps.update(b.r.values())
        for d in deps:
            if e == 'pe' and d[0] == 'pe':
                continue
            self._wait(e, d)

    def _mark(self, me, key, reads, writes):
        for b in reads:
            b.r[key] = me
        for b in writes:
            b.w = me
            b.r = {}

    def op(self, e, fn, reads=(), writes=()):
        reads = [x.b if isinstance(x, T) else x for x in reads]
        writes = [x.b if isinstance(x, T) else x for x in writes]
        writes = writes + [b for b in reads if b.excl and b not in writes]
        self._deps(e, reads, writes)
        ins = fn(self.eng[e])
        self.cnt[e] += 1
        self.ninstr += 1
        ins.then_inc(self.sem[e], 1)
        me = (e, self.cnt[e])
        self._mark(me, e, reads, writes)
        return me

    def dma(self, q, out, in_, reads=(), writes=(), fn=None, **kw):
        reads = [x.b if isinstance(x, T) else x for x in reads]
        writes = [x.b if isinstance(x, T) else x for x in writes]
        if q == 'pool':
            i = self.NDS - 8 + self.dnext_sw
            self.dnext_sw = (self.dnext_sw + 1) % 8
        else:
            i = self.dnext
            self.dnext = (i + 1) % (self.NDS - 8)
        if self.dcnt[i] > 0:
            self._wait(q, (i, self.dcnt[i]))
        self._deps(q, reads, writes)
        if fn is None:
            ins = self.eng[q].dma_start(out=out, in_=in_, **kw)
        else:
            ins = fn(self.eng[q])
        self.dcnt[i] += 16
        self.ninstr += 1
        ins.then_inc(self.dsem[i], 16)
        me = (i, self.dcnt[i])
        self._mark(me, i, reads, writes)
        return me

    def finish(self):
        for i in range(self.NDS):
            if self.dcnt[i] > 0:
                self._wait('sp', (i, self.dcnt[i]))
        for e in self.sem:
            if self.cnt[e] > 0:
                self._wait('sp', (e, self.cnt[e]))
        self.es.close()
        return self.nc


class MK:
    def __init__(self, n_prefix=47, n_own=17, n_samp=4, do_sample=True):
        self.k = k = KB()
        self.n_prefix, self.n_own, self.n_samp, self.do_sample = n_prefix, n_own, n_samp, do_sample
        import os
        self.stop = int(os.environ.get('MK_STOP', '99'))
        self.declare_io()
        self.alloc()
        self.setup()
        if self.stop >= 2:
            self.prompt_phase()
        if do_sample and self.stop >= 90:
            self.sample_phase()
        self.nc = k.finish()

    def mm(self, out, lhsT, rhs, start, stop, reads, writes):
        self.k.op('pe', lambda e: e.matmul(out, lhsT=lhsT, rhs=rhs, start=start, stop=stop, skip_group_check=True), reads, writes)

    def tr(self, out, in_, f32, reads, writes):
        idt = self.ident_f if f32 else self.ident_b
        self.k.op('pe', lambda e: e.transpose(out=out, in_=in_, identity=idt.t[:]), list(reads) + [idt], writes)

    def act(self, out, in_, func, reads, writes, **kw):
        self.k.op('act', lambda e: e.activation(out=out, in_=in_, func=func, **kw), reads, writes)

    def tt(self, out, in0, in1, op, reads, writes, eng='dve'):
        self.k.op(eng, lambda e: e.tensor_tensor(out=out, in0=in0, in1=in1, op=op), reads, writes)

    def ts(self, out, in0, s1, s2, op0, op1, reads, writes, eng='dve'):
        if op1 is None:
            self.k.op(eng, lambda e: e.tensor_scalar(out=out, in0=in0, scalar1=s1, scalar2=None, op0=op0), reads, writes)
        else:
            self.k.op(eng, lambda e: e.tensor_scalar(out=out, in0=in0, scalar1=s1, scalar2=s2, op0=op0, op1=op1), reads, writes)

    def stt(self, out, in0, scalar, in1, op0, op1, reads, writes):
        self.k.op('dve', lambda e: e.scalar_tensor_tensor(out=out, in0=in0, scalar=scalar, in1=in1, op0=op0, op1=op1), reads, writes)

    def cp(self, out, in_, reads, writes, eng='dve'):
        self.k.op(eng, lambda e: e.tensor_copy(out=out, in_=in_), reads, writes)

    def recip(self, ap, t):
        self.k.op('dve', lambda e: e.reciprocal(out=ap, in_=ap), [t], [t])

    def memset(self, ap, val, writes, eng='pool'):
        self.k.op(eng, lambda e: e.memset(ap, val), (), writes)

    def pa_next(self):
        self._pai = (self._pai + 1) % len(self.pa)
        return self.pa[self._pai]

    def pb_next(self):
        self._pbi = (self._pbi + 1) % len(self.pb)
        return self.pb[self._pbi]

    def rstd_from_ss(self, ss_ap, n, cnt, reads_t, out_t):
        self.act(out_t[:, 0:n], ss_ap, AF.Sqrt, list(reads_t) + [self.eps_t], [out_t], scale=1.0 / cnt, bias=self.eps_t[:, 0:1])
        self.recip(out_t[:, 0:n], out_t)

    def headnorm(self, src, H, Dh, reads):
        sq, ss, rs = self.hn_sq, self.hn_ss, self.hn_rs
        n = H * Dh
        self.act(sq[:, 0:n], src, AF.Square, reads, [sq])
        self.k.op('dve', lambda e: e.tensor_reduce(out=ss[:, 0:H], in_=sq[:, 0:n].rearrange('p (h d) -> p h d', h=H), axis=AX.X, op=ALU.add), [sq], [ss])
        self.rstd_from_ss(ss[:, 0:H], H, Dh, [ss], rs)
        self.tt(sq[:, 0:n].rearrange('p (h d) -> p h d', h=H), src.rearrange('p (h d) -> p h d', h=H),
                rs[:, 0:H].unsqueeze(2).to_broadcast([128, H, Dh]), ALU.mult, list(reads) + [rs], [sq])
        return sq

    def gelu_tanh(self, out, x, tmp, reads, writes, tmp_t):
        self.tt(tmp, x, x, ALU.mult, reads, [tmp_t])
        self.ts(tmp, tmp, 0.044715, 1.0, ALU.mult, ALU.add, [tmp_t], [tmp_t])
        self.tt(tmp, tmp, x, ALU.mult, list(reads) + [tmp_t], [tmp_t])
        self.act(tmp, tmp, AF.Sigmoid, [tmp_t], [tmp_t], scale=1.5957691216057308)
        self.tt(out, tmp, x, ALU.mult, list(reads) + [tmp_t], writes)

    def declare_io(self):
        k = self.k
        ns_ = self.n_samp
        i = {}
        i['xloc'] = k.din('xloc', [64 * 128, D])
        i['xs'] = k.din('xs', [ns_, D])
        i['cache_kv'] = k.din('cache_kv', [(2560 if self.do_sample else 2) * 128, 512])
        i['cache_win'] = k.din('cache_win', [ns_, 512, 256])
        i['state_gla'] = k.din('state_gla', [ns_, 4, 64, 128])
        i['state_conv'] = k.din('state_conv', [ns_, 2, DFF])
        i['cache_mem'] = k.din('cache_mem', [ns_, 256, 1024])
        i['page_table'] = k.din('page_table', [ns_, 64], I32)
        i['mem'] = k.din('mem', [256, D])
        i['g_mix'] = k.din('g_mix', [D]); i['g_mem'] = k.din('g_mem', [D]); i['g_ffn'] = k.din('g_ffn', [D])
        i['w_in'] = k.din('w_in', [D, 6440])
        i['g_nsa_q'] = k.din('g_nsa_q', [64]); i['g_nsa_k'] = k.din('g_nsa_k', [3, 64])
        for nm in ('k', 'v'):
            i['cmp_%s_pe' % nm] = k.din('cmp_%s_pe' % nm, [32, 64])
            i['cmp_%s_w1' % nm] = k.din('cmp_%s_w1' % nm, [32, 64, 64])
            i['cmp_%s_w2' % nm] = k.din('cmp_%s_w2' % nm, [64, 64])
        i['rel_bias'] = k.din('rel_bias', [32, 8])
        i['w_gla_gate'] = k.din('w_gla_gate', [16, 256]); i['b_gla_gate'] = k.din('b_gla_gate', [1, 256])
        i['g_gla_o'] = k.din('g_gla_o', [128]); i['g_x_q'] = k.din('g_x_q', [128]); i['g_x_k'] = k.din('g_x_k', [128])
        i['w_mem_kv'] = k.din('w_mem_kv', [D, 1024])
        i['w_nsa_out'] = k.din('w_nsa_out', [512, D]); i['w_gla_out'] = k.din('w_gla_out', [512, D]); i['w_x_out'] = k.din('w_x_out', [512, D])
        i['w_o'] = k.din('w_o', [D, D])
        i['w_up'] = k.din('w_up', [D, 2 * DFF])
        i['conv_w'] = k.din('conv_w', [3, DFF]); i['conv_b'] = k.din('conv_b', [1, DFF])
        i['w_down'] = k.din('w_down', [DFF, D])
        i['c_ident'] = k.din('c_ident', [128, 128]); i['c_U'] = k.din('c_U', [128, 128]); i['c_L'] = k.din('c_L', [128, 128])
        i['c_onesbd'] = k.din('c_onesbd', [128, 128])
        i['c_ohdr'] = k.din('c_ohdr', [33, ND]); i['c_ohda'] = k.din('c_ohda', [33, ND])
        i['c_relb'] = k.din('c_relb', [128, 3]); i['c_edge'] = k.din('c_edge', [128, 128], BF16)
        i['c_e0'] = k.din('c_e0', [128, 2])
        i['m_slotbias'] = k.din('m_slotbias', [2, 128, NSLOT])
        i['m_pm'] = k.din('m_pm', [2, 128, 136])
        i['m_cmask'] = k.din('m_cmask', [2, 1, 528])
        i['m_fbabs'] = k.din('m_fbabs', [2, 128, 136]); i['m_hs'] = k.din('m_hs', [128, 1])
        self.i = i
        o = {}
        o['y_p'] = k.dout('y_p', [16 * 128, D])
        o['rows_p'] = k.dout('rows_p', [16 * 128, 512])
        o['win_p'] = k.dout('win_p', [512, 256])
        o['gla_p'] = k.dout('gla_p', [4, 64, 128])
        o['conv_p'] = k.dout('conv_p', [2, DFF])
        o['memkv_p'] = k.dout('memkv_p', [256, 1024])
        o['y_s'] = k.dout('y_s', [ns_, D])
        o['rows_s'] = k.dout('rows_s', [ns_, 512])
        o['win_s'] = k.dout('win_s', [ns_, 512, 256])
        o['gla_s'] = k.dout('gla_s', [ns_, 4, 64, 128])
        o['conv_s'] = k.dout('conv_s', [ns_, 2, DFF])
        self.o = o
        s = {}
        s['w_in'] = k.dint('s_w_in', [128, 8, 6440], BF16)
        s['w_mem_kv'] = k.dint('s_w_mem_kv', [128, 8, 1024], BF16)
        s['w_nsa_out'] = k.dint('s_w_nsa_out', [128, 4, D], BF16)
        s['w_gla_out'] = k.dint('s_w_gla_out', [128, 4, D], BF16)
        s['w_x_out'] = k.dint('s_w_x_out', [128, 4, D], BF16)
        s['w_o'] = k.dint('s_w_o', [128, 8, D], BF16)
        s['w_up'] = k.dint('s_w_up', [128, 8, 2 * DFF], BF16)
        s['w_down'] = k.dint('s_w_down', [128, NFF, D], BF16)
        self.s = s
        self.sbuf_w = {nm: Buf() for nm in s}

    def alloc(self):
        k = self.k
        sb, ps = k.sb, k.ps
        self.pa = [ps([128, 512]) for _ in range(4)]
        self.pacc = [ps([128, 512]) for _ in range(2)]
        self.pb = [ps([128, 1024], BF16) for _ in range(2)]
        self._pai = self._pbi = 0
        self.ident_f = sb([128, 128]); self.ident_b = sb([128, 128], BF16)
        self.U = sb([128, 128]); self.L = sb([128, 128]); self.onesbd = sb([128, 128])
        self.I4 = sb([128, 4, 128], BF16)
        self.rbp = sb([33, 8]); self.rb31 = sb([32, 8])
        self.relb = sb([128, 3]); self.e0 = sb([128, 2])
        self.ones_row = sb([1, 128], BF16)
        self.ones_col = sb([128, 1])
        self.eps_t = sb([128, 1])
        self.slotbias = [sb([128, NSLOT]) for _ in range(2)]
        self.pm = [sb([128, 136]) for _ in range(2)]
        self.cmask = [sb([1, 528], BF16) for _ in range(2)]
        self.cmask_f = sb([1, 528])
        self.fbabs = [sb([128, 136]) for _ in range(2)]; self.hs = sb([128, 1])
        self.BT = sb([128, 2, 2, 512])
        self.BT512T = sb([128, 128], BF16)
        self.BC = sb([128, 8, 16])
        self.gains = Buf()
        self.gcol = sb([128, 3, 8])
        self.gq = sb([128, 64]); self.gk = sb([128, 3, 64]); self.ggo = sb([128, 128]); self.gxq = sb([128, 128]); self.gxk = sb([128, 128])
        self.gk0col = sb([128, 1])
        self.convw = sb([128, NFF, 3]); self.convb = sb([128, NFF])
        self.w1bd = [sb([128, 32, 128], BF16) for _ in range(2)]
        self.w2bd = [sb([128, 128], BF16) for _ in range(2)]
        self.b1 = [sb([128, 1]) for _ in range(2)]
        self.peT = sb([128, 2, 32], BF16)
        self.wg17 = sb([17, 256], BF16)
        self.stage = sb([128, 64])
        self.k_selT = sb([128, NSLOT * 128], BF16); self.b_ksel = [Buf() for _ in range(NSLOT)]
        self.v_sel = sb([128, NSLOT, 2, 65], BF16); self.b_vsel = [Buf() for _ in range(NSLOT)]
        self.k_winT = sb([128, 8 * 128], BF16); self.v_win = sb([128, 8, 2, 65], BF16); self.b_win = [Buf() for _ in range(8)]
        self.k_cmpT = sb([128, 528], BF16); self.b_kcmp = Buf()
        self.hgvT = sb([128, 640], BF16)
        self.v_cmp = sb([128, 5, 2, 64], BF16); self.b_vcmp = Buf()
        self.rawT = [sb([128, 16 + 512], BF16) for _ in range(2)]
        self.k_memT = sb([128, 4, 256], BF16); self.v_mem = sb([128, 2, 4, 129], BF16)
        self.S = sb([64, 4, 128]); self.Sb = sb([64, 4, 128], BF16)
        self.xt = [sb([128, D]) for _ in range(1)]
        self.big = sb([128, 2048])
        def view(ap):
            t = T(ap); t.b = self.big.b
            return t
        self.pc = view(self.big.t[:, :].rearrange('p (a b) -> p a b', a=4))
        self.x1 = view(self.big.t[:, 0:1024])
        self.hn_sq = view(self.big.t[:, 1024:2048])
        self.yt = self.hn_sq; self.memrows = self.hn_sq
        self.mtmp = view(self.big.t[:, 1536:2048].rearrange('p (a b) -> p a b', a=4))
        self.BTs = view(self.big.t[:, 0:512].rearrange('p (k o n) -> p k o n', k=2, o=2))
        self.qTs = sb([128, 128], BF16); self.I4s = sb([128, 128], BF16)
        self.hn_ss = sb([128, 8]); self.hn_rs = sb([128, 8])
        self.hb = sb([128, D], BF16)
        self.hT = sb([128, 8, 128], BF16)
        self.rows = sb([128, 4, 128]); self.winrows = sb([128, 2, 128])
        self.qn = sb([128, 4, 2, 64], BF16); self.qT = sb([128, 4, 128], BF16)
        self.gates = sb([128, 24])
        self.lrn = sb([128, 16], BF16); self.lrT = sb([17, 128], BF16)
        self.gqk = sb([128, 512], BF16); self.gqkT = sb([64, 8, 128], BF16)
        self.gv = sb([128, 512], BF16); self.gr = sb([128, 512], BF16)
        self.xqn = sb([128, 512], BF16); self.xqT = sb([128, 4, 128], BF16)
        self.mgT = sb([128, 24, 128], BF16)
        self.la = sb([128, 256]); self.e1 = sb([128, 256])
        self.ecb = sb([64, 4, 128]); self.encb = sb([64, 4, 128]); self.ecbl = sb([64, 4])
        self.qeT = sb([64, 4, 128], BF16); self.keT = sb([64, 4, 128], BF16)
        self.erc = sb([128, 256]); self.kl = sb([128, 256], BF16)
        self.Am = sb([128, 4, 128], BF16)
        self.ogla = sb([128, 512], BF16); self.oglaT = sb([128, 4, 128], BF16)
        self.psum8 = sb([128, 8]); self.prc8 = sb([128, 8])
        self.pg = sb([128, 2, 528]); self.imp = sb([128, 2, 136]); self.m16 = sb([128, 2, 16]); self.wk = sb([128, 136])
        self.neg = sb([128, 2, 136], BF16)
        self.negx = [[sb([128, 512], BF16) for _ in range(2)] for _ in range(2)]
        self.pcT = sb([128, 4, 128], BF16)
        self.pT = [sb([128, 512], BF16) for _ in range(4)]; self._pTi = 0
        self.oc = sb([128, 512]); self.osel = sb([128, 8, 64], BF16); self.owin = sb([128, 8, 64], BF16); self.den = sb([128, 8])
        self.onsa = sb([128, 512], BF16); self.onsaT = sb([128, 4, 128], BF16)
        self.pxT = sb([128, 2, 4, 128], BF16); self.ox = sb([128, 512], BF16); self.oxT = sb([128, 4, 128], BF16)
        self.mrg = view(self.big.t[:, 0:1024].rearrange('p (a b) -> p a b', a=8)); self.mrgT = sb([128, 8, 128], BF16)
        self.h2T = self.hT
        self.hist = sb([128, NFF, 2]); self.gx = [sb([128, 2, 130]) for _ in range(2)]; self.ub = [sb([128, 2, 128], BF16) for _ in range(2)]
        self.cva = sb([128, 2, 128]); self.cvb = sb([128, 2, 128]); self.g0col = sb([128, NFF])
        self.actT = sb([128, NFF, 128], BF16)
        self.hg = sb([128, 32], BF16); self.hx = sb([128, 32]); self.hx2 = sb([128, 32]); self.hxs = [self.hx, sb([128, 32])]; self.hx2s = [self.hx2, sb([128, 32])]; self.kc = sb([128, 32]); self.kc2 = sb([128, 32]); self.kcr = sb([128, 32])
        self.idx = sb([128, self.n_samp * 64], I32); self.ptb = sb([128, self.n_samp * 64], I32); self.iop = sb([128, 1], I32)
        self.page = [sb([128, 512]) for _ in range(2)]
        pcf = self.big.t[:, :]
        self.ohdr = T(pcf[0:33, 0:ND]); self.ohdr.b = self.pc.b
        self.ohda = T(pcf[0:33, ND:2 * ND]); self.ohda.b = self.pc.b
        self.NW = 4
        self.WAHEAD = 2
        self.wring = [sb([128, 2048], BF16) for _ in range(self.NW)]
        self.wq = []
        self.wissued = 0
        self.wnext = 0

    def wsched(self, tag, src, shape):
        self.wq.append((tag, src, shape))

    def _wview(self, r, shape):
        n = int(np.prod(shape[1:]))
        return r.t[:, 0:n].rearrange('p (a b) -> p a b', a=shape[1])

    def wget(self, tag):
        k = self.k
        i = self.wnext
        while self.stop < 99 and self.wq[i][0] != tag:
            self.wnext += 1; i = self.wnext; self.wissued = max(self.wissued, i)
        assert self.wq[i][0] == tag, (self.wq[i][0], tag)
        upto = min(len(self.wq), i + self.WAHEAD + 1)
        while self.wissued < upto:
            j = self.wissued
            tg, src, shape = self.wq[j]
            r = self.wring[j % self.NW]
            k.dma('sp', self._wview(r, shape), src, reads=[self.sbuf_w[tg.split(':')[0]]], writes=[r])
            self.wissued += 1
        self.wnext += 1
        r = self.wring[i % self.NW]
        return self._wview(r, self.wq[i][2]), r

    def sched_cols(self, nm, tag, c0, c1, nk=8):
        j = 0
        for a in range(c0, c1, 256):
            b = min(c1, a + 256)
            self.wsched('%s:%s:%d' % (nm, tag, j), self.s[nm][:, :, a:b], [128, nk, b - a])
            j += 1

    def sched_tile(self, own):
        if own:
            self.sched_cols('w_in', 'c0', C0, C1)
        self.sched_cols('w_in', 'c1', C1, C2)
        self.sched_cols('w_in', 'c2', C2, C3)
        self.sched_cols('w_in', 'c3', C3, C4)
        self.sched_cols('w_in', 'c4', C4, C5)
        if own:
            self.sched_cols('w_in', 'c5', C5, C6)
            self.sched_cols('w_in', 'c6', C6, CMG)
            self.sched_cols('w_in', 'mg', CMG, 6440)
            for nm in ('w_nsa_out', 'w_gla_out', 'w_x_out'):
                self.sched_cols(nm, 'o', 0, D, nk=4)
            self.sched_cols('w_o', 'o', 0, D)
            self.sched_cols('w_up', 'u', 0, 2 * DFF)
            for j in range(11):
                self.wsched('w_down:d:%d' % j, self.s['w_down'][:, 2 * j:2 * j + 2, :], [128, 2, 1024])

    def setup(self):
        k, i, s = self.k, self.i, self.s

        def castw(nm, src3, nk, ncols):
            for c in range(nk):
                for c0 in range(0, ncols, 2048):
                    c1 = min(ncols, c0 + 2048)
                    k.dma('pool', s[nm][:, c, c0:c1], src3[:, c, c0:c1], writes=[self.sbuf_w[nm]])
        castw('w_mem_kv', i['w_mem_kv'].rearrange('(c p) n -> p c n', p=128), 8, 1024)
        castw('w_in', i['w_in'].rearrange('(c p) n -> p c n', p=128), 8, 6440)
        self.late_casts = []
        def castw_late(nm, src3, nk, ncols):
            for c in range(nk):
                for c0 in range(0, ncols, 2048):
                    c1 = min(ncols, c0 + 2048)
                    self.late_casts.append((nm, s[nm][:, c, c0:c1], src3[:, c, c0:c1]))
        for nm in ('w_nsa_out', 'w_gla_out', 'w_x_out'):
            castw_late(nm, i[nm].rearrange('(c p) n -> p c n', p=128), 4, D)
        castw_late('w_o', i['w_o'].rearrange('(c p) n -> p c n', p=128), 8, D)
        castw_late('w_up', i['w_up'].rearrange('(c p) n -> p c n', p=128), 8, 2 * DFF)
        castw_late('w_down', i['w_down'].rearrange('(c p) n -> p c n', p=128), NFF, D)

        def ld(t, src):
            k.dma('sp', t[:], src, writes=[t])
        ld(self.ident_f, i['c_ident']); ld(self.U, i['c_U']); ld(self.L, i['c_L']); ld(self.onesbd, i['c_onesbd'])
        ld(self.BT512T, i['c_edge'])
        ld(self.ohdr, i['c_ohdr']); ld(self.ohda, i['c_ohda']); ld(self.relb, i['c_relb']); ld(self.e0, i['c_e0'])
        for st in range(2):
            ld(self.slotbias[st], i['m_slotbias'][st]); ld(self.pm[st], i['m_pm'][st]); ld(self.fbabs[st], i['m_fbabs'][st])
            k.dma('sp', self.cmask_f[:], i['m_cmask'][st], writes=[self.cmask_f])
            self.cp(self.cmask[st][:], self.cmask_f[:], [self.cmask_f], [self.cmask[st]])
        ld(self.hs, i['m_hs'])
        self.cp(self.ident_b[:], self.ident_f[:], [self.ident_f], [self.ident_b])
        for g in range(4):
            self.cp(self.I4[:, g, :], self.ident_f[:], [self.ident_f], [self.I4])
        self.memset(self.ones_row[:], 1.0, [self.ones_row])
        self.memset(self.ones_col[:], 1.0, [self.ones_col])
        self.memset(self.eps_t[:], EPS, [self.eps_t])
        G = self.gains
        with k.nc.allow_non_contiguous_dma(reason="small one-time transposed parameter loads"):
            for j, nm in enumerate(('g_mix', 'g_mem', 'g_ffn')):
                k.dma('sp', self.gcol[:, j, :], i[nm].rearrange('(c p) -> p c', p=128), writes=[G])
            for t_ in range(3):
                k.dma('sp', self.convw[:, :, t_], i['conv_w'][t_].rearrange('(j p) -> p j', p=128), writes=[G])
            k.dma('sp', self.convb[:], i['conv_b'].rearrange('o (j p) -> p (o j)', p=128), writes=[G])
            for a, nm in enumerate(('k', 'v')):
                for kv in range(2):
                    k.dma('sp', self.stage[kv * 64:(kv + 1) * 64, a * 32:(a + 1) * 32], i['cmp_%s_pe' % nm].rearrange('j d -> d j'), writes=[self.stage])
            for kv in range(2):
                k.dma('sp', self.gk0col[kv * 64:(kv + 1) * 64, :], i['g_nsa_k'][0].rearrange('(d o) -> d o', o=1), writes=[G])
        for t, nm in ((self.gq, 'g_nsa_q'), (self.ggo, 'g_gla_o'), (self.gxq, 'g_x_q'), (self.gxk, 'g_x_k')):
            k.dma('sp', t[:], i[nm].partition_broadcast(128), writes=[G])
        for j in range(3):
            k.dma('sp', self.gk[:, j, :], i['g_nsa_k'][j].partition_broadcast(128), writes=[G])
        self.ts(self.gq[:], self.gq[:], 0.125, None, ALU.mult, None, [G], [G])
        self.ts(self.gxq[:], self.gxq[:], 128.0 ** -0.5, None, ALU.mult, None, [G], [G])
        self.cp(self.peT[:].rearrange('p a j -> p (a j)'), self.stage[:, 0:64], [self.stage], [self.peT])
        k.dma('sp', self.rbp[0:32, :], i['rel_bias'], writes=[self.rbp])
        k.dma('sp', self.rb31[:], i['rel_bias'][31].partition_broadcast(32), writes=[self.rb31])
        self.tt(self.rbp[0:32, :], self.rbp[0:32, :], self.rb31[:], ALU.subtract, [self.rbp, self.rb31], [self.rbp])
        self.memset(self.rbp[32:33, :], NEG, [self.rbp])
        for a, nm in enumerate(('k', 'v')):
            self.memset(self.w1bd[a][:], 0.0, [self.w1bd[a]])
            self.memset(self.w2bd[a][:], 0.0, [self.w2bd[a]])
            for kv in range(2):
                k.dma('pool', self.w1bd[a][kv * 64:(kv + 1) * 64, :, kv * 64:(kv + 1) * 64], i['cmp_%s_w1' % nm].rearrange('j d e -> d j e'), writes=[self.w1bd[a]])
                k.dma('pool', self.w2bd[a][kv * 64:(kv + 1) * 64, kv * 64:(kv + 1) * 64], i['cmp_%s_w2' % nm], writes=[self.w2bd[a]])
            p = self.pa_next()
            for j in range(32):
                self.mm(p[:, 0:2], self.w1bd[a][:, j, :], self.peT[:, a, j:j + 1].to_broadcast([128, 2]), j == 0, j == 31, [self.w1bd[a], self.peT], [p])
            self.cp(self.b1[a][:], p[:, 0:1], [p], [self.b1[a]])
        k.dma('pool', self.wg17[0:16, :], i['w_gla_gate'], writes=[self.wg17])
        k.dma('pool', self.wg17[16:17, :], i['b_gla_gate'], writes=[self.wg17])
        self.memset(self.lrT[:], 1.0, [self.lrT])
        zb = Buf()
        for t in (self.k_selT, self.k_winT, self.k_cmpT, self.v_sel, self.v_win):
            self.memset(t[:], 0.0, [zb])
        self.memset(self.v_sel[:, :, :, 64:65], 1.0, [zb])
        self.memset(self.v_win[:, :, :, 64:65], 1.0, [zb])
        self.memset(self.v_cmp[:], 0.0, [zb])
        for t in (self.hgvT, self.pg, self.hist, self.S, self.Sb, self.rawT[0], self.rawT[1], self.v_mem, self.neg, self.k_memT, self.pcT, self.negx[0][0], self.negx[0][1], self.negx[1][0], self.negx[1][1]):
            self.memset(t[:], 0.0, [t])
        self.memset(self.v_mem[:, :, :, 128:129], 1.0, [self.v_mem])
        for b in self.b_ksel + self.b_vsel + self.b_win + [self.b_kcmp, self.b_vcmp]:
            b.w = zb.w
        self.build_bias_tiles()

    def build_bias_tiles(self):
        for oi, off in enumerate((0, 128)):
            for half in range(2):
                p = self.pa_next()
                for ql in range(64):
                    q = half * 64 + ql
                    j0 = DMAX - off - q
                    self.mm(p[:, ql * 8:(ql + 1) * 8], self.ohdr[:, j0:j0 + 128], self.rbp[:, :], True, True, [self.ohdr, self.rbp], [p])
                src = p[:, :].rearrange('p (q h) -> p h q', h=8)
                for kv in range(2):
                    self.cp(self.BT[:, kv, oi, :].rearrange('p (g q) -> p g q', g=4)[:, :, half * 64:(half + 1) * 64],
                            src[:, kv * 4:(kv + 1) * 4, :], [p], [self.BT])
        p = self.pa_next()
        for uu in range(16):
            i0 = (113 - 16 * uu) - DMIN
            self.mm(p[:, uu * 8:(uu + 1) * 8], self.ohda[:, i0:i0 + 128], self.rbp[:, :], True, True, [self.ohda, self.rbp], [p])
        self.cp(self.BC[:], p[:, 0:128].rearrange('p (u h) -> p h u', h=8), [p], [self.BC])

    def load_x(self, xt, src):
        self.k.dma('sp', xt[:], src, writes=[xt])

    def norm_T(self, xt, gi, outT):
        sq, ss, rs = self.hn_sq, self.hn_ss, self.hn_rs
        self.act(sq[:], xt[:], AF.Square, [xt], [sq, ss], accum_out=ss[:, 0:1])
        self.rstd_from_ss(ss[:, 0:1], 1, D, [ss], rs)
        self.ts(self.hb[:], xt[:], rs[:, 0:1], None, ALU.mult, None, [xt, rs], [self.hb])
        p = self.pb_next()
        for c in range(8):
            self.tr(p[:, c * 128:(c + 1) * 128], self.hb[:, c * 128:(c + 1) * 128], False, [self.hb], [p])
        self.tt(outT[:], p[:, :].rearrange('p (a b) -> p a b', a=8), self.gcol[:, gi, :].unsqueeze(2).to_broadcast([128, 8, 128]), ALU.mult,
                [p, self.gains], [outT])

    def proj(self, nm, tag, ncols, srcT=None):
        srcT = srcT or self.hT
        p = self.pa_next()
        j = 0
        for a in range(0, ncols, 256):
            b = min(ncols, a + 256)
            w, wb = self.wget('%s:%s:%d' % (nm, tag, j))
            for c in range(8):
                self.mm(p[:, a:b], srcT[:, c, :], w[:, c, :], c == 0, c == 7, [srcT, wb], [p])
            j += 1
        return p

    def a2_rows(self, slot, rows, gi=0):
        p = self.pa_next()
        for j, sl in enumerate((0, 1, 2)):
            self.tr(p[:, j * 128:(j + 1) * 128], rows[:, sl, :], True, [rows], [p])
        for a in range(2):
            self.cp(self.rawT[a][:, 16 + gi * 128:16 + (gi + 1) * 128], p[:, a * 128:(a + 1) * 128], [p], [self.rawT[a]])
        self.cp(self.k_selT[:, slot * 128:(slot + 1) * 128], p[:, 256:384], [p], [self.b_ksel[slot]])
        self.cp(self.v_sel[:, slot, :, 0:64], rows[:, 3, :].rearrange('p (k d) -> p k d', k=2), [rows], [self.b_vsel[slot]], eng='pool')

    def a2_win(self, slot, winrows):
        r = slot % 8
        p = self.pa_next()
        self.tr(p[:, 0:128], winrows[:, 0, :], True, [winrows], [p])
        self.cp(self.k_winT[:, r * 128:(r + 1) * 128], p[:, 0:128], [p], [self.b_win[r]])
        self.cp(self.v_win[:, r, :, 0:64], winrows[:, 1, :].rearrange('p (k d) -> p k d', k=2), [winrows], [self.b_win[r]], eng='pool')

    def compress(self, slot0, G=1):
        cc0 = 8 * slot0
        n8 = 8 * G

        def chain(a):
            hx, hx2 = self.hxs[a], self.hx2s[a]
            p = self.pa_next()
            n = 0
            for r in range(2):
                for j in range(16):
                    st = 16 * r + j
                    self.mm(p[:, 0:n8], self.w1bd[a][:, st, :], self.rawT[a][:, st:st + 16 * (n8 - 1) + 1:16], n == 0, n == 31, [self.w1bd[a], self.rawT[a]], [p])
                    n += 1
            self.cp(self.rawT[a][:, 0:16], self.rawT[a][:, G * 128:G * 128 + 16], [self.rawT[a]], [self.rawT[a]], eng='pool')
            self.act(hx[:, 0:n8], p[:, 0:n8], AF.Identity, [p, self.b1[a]], [hx], bias=self.b1[a][:, 0:1])
            yield
            x, t = hx[:, 0:n8], hx2[:, 0:n8]
            self.tt(t, x, x, ALU.mult, [hx], [hx2])
            yield
            self.ts(t, t, 0.044715, 1.0, ALU.mult, ALU.add, [hx2], [hx2])
            yield
            self.tt(t, t, x, ALU.mult, [hx, hx2], [hx2])
            yield
            self.act(t, t, AF.Sigmoid, [hx2], [hx2], scale=1.5957691216057308)
            yield
            if a == 0:
                self.tt(self.hg[:, 0:n8], t, x, ALU.mult, [hx, hx2], [self.hg])
                yield
                p2 = self.pa_next()
                self.mm(p2[:, 0:n8], self.w2bd[0][:], self.hg[:, 0:n8], True, True, [self.w2bd[0], self.hg], [p2])
                yield
                self.cp(self.kc[:, 0:n8], p2[:, 0:n8], [p2], [self.kc])
                yield
                self.tt(self.kc2[:, 0:n8], self.kc[:, 0:n8], self.kc[:, 0:n8], ALU.mult, [self.kc], [self.kc2])
                yield
                p3 = self.pa_next()
                self.mm(p3[:, 0:n8], self.onesbd[:], self.kc2[:, 0:n8], True, True, [self.onesbd, self.kc2], [p3])
                yield
                self.rstd_from_ss(p3[:, 0:n8], n8, 64, [p3], self.kcr)
                yield
                self.tt(self.kc[:, 0:n8], self.kc[:, 0:n8], self.kcr[:, 0:n8], ALU.mult, [self.kc, self.kcr], [self.kc])
                yield
                self.ts(self.k_cmpT[:, cc0:cc0 + n8], self.kc[:, 0:n8], self.gk0col[:, 0:1], None, ALU.mult, None, [self.kc, self.gains], [self.b_kcmp])
                yield
            else:
                self.tt(self.hgvT[:, cc0:cc0 + n8], t, x, ALU.mult, [hx, hx2], [self.hgvT])
                yield
                for g in range(cc0 // 128, (cc0 + n8 - 1) // 128 + 1):
                    p2 = self.pa_next()
                    self.mm(p2[:, 0:128], self.hgvT[:, g * 128:(g + 1) * 128], self.w2bd[1][:], True, True, [self.hgvT, self.w2bd[1]], [p2])
                    yield
                    self.cp(self.v_cmp[:, g, :, :], p2[:, 0:128].rearrange('p (k d) -> p k d', k=2), [p2], [self.b_vcmp])
                    yield

        gens = [chain(0), chain(1)]
        while gens:
            for g_ in list(gens):
                try:
                    next(g_)
                except StopIteration:
                    gens.remove(g_)

    def la_compute(self, rowmask_col):
        p = self.pb_next()
        self.tr(p[0:16, 0:128], self.lrn[:], False, [self.lrn], [p])
        self.cp(self.lrT[0:16, :], p[0:16, 0:128], [p], [self.lrT])
        pz = self.pa_next()
        self.mm(pz[:, 0:256], self.lrT[:], self.wg17[:], True, True, [self.lrT, self.wg17], [pz])
        self.act(self.e1[:], pz[:, 0:256], AF.Exp, [pz], [self.e1], scale=-1.0)
        self.act(self.e1[:], self.e1[:], AF.Ln, [self.e1], [self.e1], bias=1.0)
        self.ts(self.la[:], self.e1[:], rowmask_col, -1.0 / 16.0, ALU.mult, ALU.mult, [self.e1, self.e0], [self.la])

    def gla_state_update(self, S, Sb, ecbl_ready):
        prc = self.pa_next()
        self.mm(prc[:, 0:256], self.L[:], self.la[:], True, True, [self.L, self.la], [prc])
        self.act(self.erc[:], prc[:, 0:256], AF.Exp, [prc], [self.erc])
        self.tt(self.kl[:], self.gqk[:, 256:512], self.erc[:], ALU.mult, [self.gqk, self.erc], [self.kl])
        if not ecbl_ready:
            pl = self.pa_next()
            for h in range(4):
                self.mm(pl[0:64, 2 * h:2 * h + 2], self.la[:, h * 64:(h + 1) * 64], self.ones_col[:, 0:1].to_broadcast([128, 2]), True, True, [self.la, self.ones_col], [pl])
            self.act(self.ecbl[:], pl[0:64, 0:8:2], AF.Exp, [pl], [self.ecbl])
        pu = self.pa_next()
        for h in range(4):
            self.mm(pu[0:64, h * 128:(h + 1) * 128], self.kl[:, h * 64:(h + 1) * 64], self.gv[:, h * 128:(h + 1) * 128], True, True, [self.kl, self.gv], [pu])
        for h in range(4):
            self.stt(S[:, h, :], S[:, h, :], self.ecbl[:, h:h + 1], pu[0:64, h * 128:(h + 1) * 128], ALU.mult, ALU.add, [S, self.ecbl, pu], [S])
        self.cp(Sb[:], S[:], [S], [Sb])

    def stage_a(self, slot, xt, own, need_win, rows_out=None, win_out=None, gi=0, flush=True):
        k = self.k
        self.norm_T(xt, 0, self.hT)
        if own:
            p = self.proj('w_in', 'c0', 512)
            sq = self.headnorm(p[:, :], 8, 64, [p])
            self.tt(self.qn[:].rearrange('p g k d -> p k g d'), sq[:, 0:512].rearrange('p (k g d) -> p k g d', k=2, g=4),
                    self.gq[:].unsqueeze(1).unsqueeze(1).to_broadcast([128, 2, 4, 64]), ALU.mult, [sq, self.gains], [self.qn])
            pq = self.pb_next()
            for g in range(4):
                self.tr(pq[:, g * 128:(g + 1) * 128], self.qn[:, g, :, :].rearrange('p k d -> p (k d)'), False, [self.qn], [pq])
            self.cp(self.qT[:].rearrange('p a b -> p (a b)'), pq[:, 0:512], [pq], [self.qT])
        p = self.proj('w_in', 'c1', 512)
        self.act(self.rows[:, 0:2, :].rearrange('p a b -> p (a b)'), p[:, 0:256], AF.Copy, [p], [self.rows])
        self.act(self.rows[:, 3, :], p[:, 384:512], AF.Copy, [p], [self.rows])
        sq = self.headnorm(p[:, 256:384], 2, 64, [p])
        self.tt(self.rows[:, 2, :].rearrange('p (h d) -> p h d', h=2), sq[:, 0:128].rearrange('p (h d) -> p h d', h=2),
                self.gk[:, 1, :].unsqueeze(1).to_broadcast([128, 2, 64]), ALU.mult, [sq, self.gains], [self.rows])
        if rows_out is not None:
            k.dma('sp', rows_out[0], self.rows[rows_out[1], :, :].rearrange('p a b -> p (a b)'), reads=[self.rows])
        self.a2_rows(slot, self.rows, gi)
        if flush:
            self.compress(slot - gi, gi + 1)
        p = self.proj('w_in', 'c2', 296)
        if need_win:
            sq = self.headnorm(p[:, 0:128], 2, 64, [p])
            self.tt(self.winrows[:, 0, :].rearrange('p (h d) -> p h d', h=2), sq[:, 0:128].rearrange('p (h d) -> p h d', h=2),
                    self.gk[:, 2, :].unsqueeze(1).to_broadcast([128, 2, 64]), ALU.mult, [sq, self.gains], [self.winrows])
            self.act(self.winrows[:, 1, :], p[:, 128:256], AF.Copy, [p], [self.winrows])
            if win_out is not None:
                k.dma('sp', win_out[0], self.winrows[win_out[1], :, :].rearrange('p a b -> p (a b)'), reads=[self.winrows])
            self.a2_win(slot, self.winrows)
        self.cp(self.lrn[:], p[:, 256:272], [p], [self.lrn])
        if own:
            self.act(self.gates[:], p[:, 272:296], AF.Sigmoid, [p], [self.gates])
        p = self.proj('w_in', 'c3', 512)
        if own:
            self.act(self.gqk[:, 0:256], p[:, 0:256], AF.Copy, [p], [self.gqk], scale=0.125)
        self.cp(self.gqk[:, 256:512], p[:, 256:512], [p], [self.gqk])
        p = self.proj('w_in', 'c4', 512)
        self.act(self.gv[:], p[:, :], AF.Copy, [p], [self.gv])

    def tr4(self, src, dstT):
        p = self.pb_next()
        for j in range(4):
            self.tr(p[:, j * 128:(j + 1) * 128], src[:, j * 128:(j + 1) * 128], False, [src], [p])
        self.cp(dstT[:].rearrange('p a b -> p (a b)'), p[:, 0:512], [p], [dstT])

    def gla_tile(self, S, Sb, rowmask_col):
        self.la_compute(rowmask_col)
        p = self.pb_next()
        for j in range(8):
            self.tr(p[0:64, j * 128:(j + 1) * 128], self.gqk[:, j * 64:(j + 1) * 64], False, [self.gqk], [p])
        self.cp(self.gqkT[:].rearrange('p a b -> p (a b)'), p[0:64, :], [p], [self.gqkT])
        pc = self.pa_next()
        for h in range(4):
            self.mm(pc[0:64, h * 128:(h + 1) * 128], self.la[:, h * 64:(h + 1) * 64], self.U[:], True, True, [self.la, self.U], [pc])
        self.act(self.ecb[:].rearrange('p a b -> p (a b)'), pc[0:64, :], AF.Exp, [pc], [self.ecb])
        self.act(self.encb[:].rearrange('p a b -> p (a b)'), pc[0:64, :], AF.Exp, [pc], [self.encb], scale=-1.0)
        self.cp(self.ecbl[:], self.ecb[:, :, 127], [self.ecb], [self.ecbl])
        self.tt(self.qeT[:], self.gqkT[:, 0:4, :], self.ecb[:], ALU.mult, [self.gqkT, self.ecb], [self.qeT])
        self.tt(self.keT[:], self.gqkT[:, 4:8, :], self.encb[:], ALU.mult, [self.gqkT, self.encb], [self.keT])
        pA = self.pa_next()
        for h in range(4):
            self.mm(pA[:, h * 128:(h + 1) * 128], self.keT[:, h, :], self.qeT[:, h, :], True, True, [self.keT, self.qeT], [pA])
        self.tt(self.Am[:], pA[:, :].rearrange('p (h t) -> p h t', h=4), self.U[:].unsqueeze(1).to_broadcast([128, 4, 128]), ALU.mult, [pA, self.U], [self.Am])
        po = self.pa_next()
        for h in range(4):
            self.mm(po[:, h * 128:(h + 1) * 128], self.Am[:, h, :], self.gv[:, h * 128:(h + 1) * 128], h == 0, False, [self.Am, self.gv], [po])
            self.mm(po[:, h * 128:(h + 1) * 128], self.qeT[:, h, :], Sb[:, h, :], False, h == 3, [self.qeT, Sb], [po])
        sq = self.headnorm(po[:, :], 4, 128, [po])
        self.tt(sq[:, 0:512].rearrange('p (h d) -> p h d', h=4), sq[:, 0:512].rearrange('p (h d) -> p h d', h=4),
                self.ggo[:].unsqueeze(1).to_broadcast([128, 4, 128]), ALU.mult, [sq, self.gains], [sq])
        self.tt(self.ogla[:], sq[:, 0:512], self.gr[:], ALU.mult, [sq, self.gr], [self.ogla])
        self.tr4(self.ogla, self.oglaT)
        self.gla_state_update(S, Sb, True)

    def nsa_tile(self, sq_, mset, nq=128):
        k = self.k
        sq = sq_
        ncc = min(8 * (sq + 1), 512)
        nb = 2 * (sq + 1)
        ngrp = (ncc + 127) // 128
        pacc_c = self.pacc[0]
        for kv in range(2):
            rs = slice(kv * 64, (kv + 1) * 64)
            banks = []
            for g in range(4):
                p = self.pa_next()
                banks.append(p)
                self.mm(p[:, 0:ncc], self.qT[rs, g, :], self.k_cmpT[rs, 0:ncc], True, False, [self.qT, self.b_kcmp], [p])
                self.mm(p[:, 0:ncc], self.ones_row[:, :], self.cmask[mset][:, 0:ncc], False, True, [self.ones_row, self.cmask[mset]], [p])
            w0 = 8 * (sq - 1)
            w1 = min(8 * (sq + 1), ncc)
            for g in range(4):
                h = kv * 4 + g
                p = banks[g]
                self.tt(p[:, w0:w1], p[:, w0:w1], self.BC[:, h, 0:w1 - w0], ALU.add, [p, self.BC], [p])
                self.act(self.pc[:, g, 0:ncc], p[:, 0:ncc], AF.Exp, [p], [self.pc, self.psum8], accum_out=self.psum8[:, h:h + 1])
            hs = slice(kv * 4, kv * 4 + 4)
            self.ts(self.prc8[:, hs], self.psum8[:, hs], TINY, None, ALU.max, None, [self.psum8], [self.prc8])
            self.recip(self.prc8[:, hs], self.prc8)
            self.tt(self.pc[:, :, 0:ncc], self.pc[:, :, 0:ncc], self.prc8[:, hs].unsqueeze(2).to_broadcast([128, 4, ncc]), ALU.mult, [self.pc, self.prc8], [self.pc])
            k.op('dve', lambda e: e.tensor_reduce(out=self.pg[:, kv, 0:ncc], in_=self.pc[:, :, 0:ncc].rearrange('p g c -> p c g'), axis=AX.X, op=ALU.add), [self.pc], [self.pg])
            for g_ in range(ngrp):
                w = min(128, ncc - g_ * 128)
                p = self.pa_next()
                for g in range(4):
                    self.tr(p[0:w, g * 128:(g + 1) * 128], self.pc[:, g, g_ * 128:g_ * 128 + w], True, [self.pc], [p])
                self.cp(self.pcT[0:w, :, :].rearrange('p a b -> p (a b)'), p[0:w, 0:512], [p], [self.pcT])
                for g in range(4):
                    h = kv * 4 + g
                    self.mm(pacc_c[:, h * 64:(h + 1) * 64], self.pcT[0:w, g, :], self.v_cmp[0:w, g_, kv, :], (kv == 0 and g_ == 0 and g == 0),
                            (kv == 1 and g_ == ngrp - 1 and g == 3), [self.pcT, self.b_vcmp], [pacc_c])
        self.cp(self.oc[:], pacc_c[:, :], [pacc_c], [self.oc])
        if ncc < 528:
            self.memset(self.pg[:, :, ncc:528], 0.0, [self.pg], eng='dve')
        k.op('dve', lambda e: e.tensor_reduce(out=self.imp[:, :, 0:nb], in_=self.pg[:, :, 0:4 * nb].rearrange('p k (j f) -> p k j f', f=4), axis=AX.X, op=ALU.add), [self.pg], [self.imp])
        self.tt(self.imp[:, :, 0:nb], self.imp[:, :, 0:nb], self.pg[:, :, 4:4 * nb + 4:4], ALU.add, [self.imp, self.pg], [self.imp])
        self.tt(self.imp[:, :, 0:nb], self.imp[:, :, 0:nb], self.fbabs[mset][:, 0:nb].unsqueeze(1).to_broadcast([128, 2, nb]), ALU.add, [self.imp, self.fbabs[mset]], [self.imp])
        self.tt(self.imp[:, :, nb - 3:nb], self.imp[:, :, nb - 3:nb], self.relb[:].unsqueeze(1).to_broadcast([128, 2, 3]), ALU.add, [self.imp, self.relb], [self.imp])
        for kv in range(2):
            k.op('dve', lambda e: e.max(out=self.m16[:, kv, 0:8], in_=self.imp[:, kv, 0:nb]), [self.imp], [self.m16])
            k.op('dve', lambda e: e.match_replace(out=self.wk[:, 0:nb], in_to_replace=self.m16[:, kv, 0:8], in_values=self.imp[:, kv, 0:nb], imm_value=-3.0e38), [self.imp, self.m16], [self.wk])
            k.op('dve', lambda e: e.max(out=self.m16[:, kv, 8:16], in_=self.wk[:, 0:nb]), [self.wk], [self.m16])
            self.ts(self.wk[:, 0:nb], self.imp[:, kv, 0:nb], self.m16[:, kv, 15:16], -NEG, ALU.is_ge, ALU.mult, [self.imp, self.m16], [self.wk])
            self.tt(self.neg[:, kv, 0:nb], self.wk[:, 0:nb], self.pm[mset][:, 0:nb], ALU.add, [self.wk, self.pm[mset]], [self.neg])
        self.mg_proj()
        NQ = 4 * nq
        if nq == 128:
            qrhs = [self.qT[kv * 64:(kv + 1) * 64, :, :].rearrange('p g q -> p (g q)') for kv in range(2)]
            I4f, I4t = self.I4[:].rearrange('p g q -> p (g q)'), self.I4
            BTv = lambda kv, oi: self.BT[:, kv, oi, :]
            BTt = self.BT
        else:
            self.cp(self.qTs[:, 0:NQ].rearrange('p (g q) -> p g q', g=4), self.qT[:, :, 0:nq], [self.qT], [self.qTs])
            self.cp(self.I4s[:, 0:NQ].rearrange('p (g q) -> p g q', g=4), self.I4[:, :, 0:nq], [self.I4], [self.I4s], eng='pool')
            for kv in range(2):
                for oi in range(2):
                    self.cp(self.BTs[:, kv, oi, 0:NQ].rearrange('p (g q) -> p g q', g=4),
                            self.BT[:, kv, oi, :].rearrange('p (g q) -> p g q', g=4)[:, :, 0:nq], [self.BT], [self.BTs], eng='pool')
            qrhs = [self.qTs[kv * 64:(kv + 1) * 64, 0:NQ] for kv in range(2)]
            I4f, I4t = self.I4s[:, 0:NQ], self.I4s
            BTv = lambda kv, oi: self.BTs[:, kv, oi, 0:NQ]
            BTt = self.BTs
        for br in (1, 0):
            kts = list(range(0, sq + 1)) if br == 0 else [t for t in range(sq - 4, sq + 1) if t >= 0]
            ngx = [None, None]

            def stage1(kt):
                banks = []
                if br == 0 and kt % 4 == 0 and kt < sq:
                    nblk = min(8, 2 * sq - 2 * kt)
                    for kv in range(2):
                        t_ = self.negx[kv][(kt // 4) % 2]
                        ngx[kv] = t_
                        self.cp(t_[:, 0:nblk * 64].rearrange('p (j s) -> p j s', s=64), self.neg[:, kv, 2 * kt:2 * kt + nblk].unsqueeze(2).to_broadcast([128, nblk, 64]),
                                [self.neg], [t_], eng='pool')
                off = 128 * (sq - kt)
                masked = (br == 0 and kt < sq)
                edge = (br == 1 and off == 512)
                for kv in range(2):
                    rs = slice(kv * 64, (kv + 1) * 64)
                    p = self.pa_next()
                    if br == 0:
                        kT, kb = self.k_selT[rs, kt * 128:(kt + 1) * 128], self.b_ksel[kt]
                    else:
                        r = kt % 8
                        kT, kb = self.k_winT[rs, r * 128:(r + 1) * 128], self.b_win[r]
                    self.mm(p[:, 0:NQ], kT, qrhs[kv], True, not (masked or edge), [kb, self.qT, self.qTs], [p])
                    banks.append(p)
                for kv in range(2):
                    p = banks[kv]
                    if masked:
                        t_ = ngx[kv]
                        self.mm(p[:, 0:NQ], t_[:, (kt % 4) * 128:(kt % 4 + 1) * 128], I4f, False, True, [t_, I4t], [p])
                    if edge:
                        self.mm(p[:, 0:NQ], self.BT512T[:, :], I4f, False, True, [self.BT512T, I4t], [p])
                    if off in (0, 128):
                        self.tt(p[:, 0:NQ], p[:, 0:NQ], BTv(kv, off // 128), ALU.add, [p, BTt], [p])
                return banks

            def stage2(n, kt, banks):
                for kv in range(2):
                    p = banks[kv]
                    pT = self.pT[self._pTi]
                    self._pTi = (self._pTi + 1) % len(self.pT)
                    self.act(pT[:, 0:NQ], p[:, 0:NQ], AF.Exp, [p, self.slotbias[mset]], [pT], bias=self.slotbias[mset][:, kt:kt + 1])
                    if br == 0:
                        vv, vb = self.v_sel[:, kt, kv, :], self.b_vsel[kt]
                    else:
                        r = kt % 8
                        vv, vb = self.v_win[:, r, kv, :], self.b_win[r]
                    pacc = self.pacc[kv]
                    for g in range(4):
                        self.mm(pacc[0:nq, g * 65:(g + 1) * 65], pT[:, g * nq:(g + 1) * nq], vv, (n == 0 and g == 0), (n == len(kts) - 1 and g == 3), [pT, vb], [pacc])

            nxt = stage1(kts[0])
            for n, kt in enumerate(kts):
                cur = nxt
                if n + 1 < len(kts):
                    nxt = stage1(kts[n + 1])
                stage2(n, kt, cur)
            dst = self.osel if br == 0 else self.owin
            for kv in range(2):
                pacc = self.pacc[kv]
                acc3 = pacc[0:nq, 0:260].rearrange('p (g d) -> p g d', g=4)
                hs = slice(kv * 4, kv * 4 + 4)
                self.ts(self.den[0:nq, hs], acc3[:, :, 64], TINY, None, ALU.max, None, [pacc], [self.den])
                self.recip(self.den[0:nq, hs], self.den)
                self.tt(dst[0:nq, hs, :], acc3[:, :, 0:64], self.den[0:nq, hs].unsqueeze(2).to_broadcast([nq, 4, 64]), ALU.mult, [pacc, self.den], [dst])
        g3 = self.gates[:].rearrange('p (h i) -> p h i', i=3)
        oc3 = self.oc[:].rearrange('p (h d) -> p h d', h=8)
        bc = lambda i: g3[:, :, i].unsqueeze(2).to_broadcast([128, 8, 64])
        self.tt(oc3, oc3, bc(0), ALU.mult, [self.oc, self.gates], [self.oc])
        self.tt(self.osel[:], self.osel[:], bc(1), ALU.mult, [self.osel, self.gates], [self.osel])
        self.tt(self.owin[:], self.owin[:], bc(2), ALU.mult, [self.owin, self.gates], [self.owin])
        self.tt(oc3, oc3, self.osel[:], ALU.add, [self.oc, self.osel], [self.oc])
        self.tt(self.onsa[:].rearrange('p (h d) -> p h d', h=8), oc3, self.owin[:], ALU.add, [self.oc, self.owin], [self.onsa])
        self.tr4(self.onsa, self.onsaT)

    def xatt_tile(self):
        for mt in range(2):
            p = self.pa_next()
            for h in range(4):
                self.mm(p[:, h * 128:(h + 1) * 128], self.k_memT[:, h, mt * 128:(mt + 1) * 128], self.xqT[:, h, :], True, True, [self.k_memT, self.xqT], [p])
            self.act(self.pxT[:, mt, :, :].rearrange('p a b -> p (a b)'), p[:, :], AF.Exp, [p], [self.pxT])
        for hp in range(2):
            p = self.pa_next()
            for hh in range(2):
                h = hp * 2 + hh
                for mt in range(2):
                    self.mm(p[:, hh * 129:(hh + 1) * 129], self.pxT[:, mt, h, :], self.v_mem[:, mt, h, :], (hh == 0 and mt == 0), (hh == 1 and mt == 1), [self.pxT, self.v_mem], [p])
            a3 = p[:, 0:258].rearrange('p (h d) -> p h d', h=2)
            self.ts(self.den[:, 0:2], a3[:, :, 128], TINY, None, ALU.max, None, [p], [self.den])
            self.recip(self.den[:, 0:2], self.den)
            self.tt(self.ox[:, hp * 256:(hp + 1) * 256].rearrange('p (h d) -> p h d', h=2), a3[:, :, 0:128], self.den[:, 0:2].unsqueeze(2).to_broadcast([128, 2, 128]), ALU.mult, [p, self.den], [self.ox])
        self.tr4(self.ox, self.oxT)

    def own_rest_a(self):
        p = self.proj('w_in', 'c5', 512)
        self.act(self.gr[:], p[:, :], AF.Silu, [p], [self.gr])
        p = self.proj('w_in', 'c6', 512)
        sq = self.headnorm(p[:, :], 4, 128, [p])
        self.tt(self.xqn[:].rearrange('p (h d) -> p h d', h=4), sq[:, 0:512].rearrange('p (h d) -> p h d', h=4),
                self.gxq[:].unsqueeze(1).to_broadcast([128, 4, 128]), ALU.mult, [sq, self.gains], [self.xqn])
        self.tr4(self.xqn, self.xqT)

    def mg_proj(self):
        for j in range(6):
            p = self.pa_next()
            for hf in range(2):
                w, wb = self.wget('w_in:mg:%d' % (2 * j + hf))
                for cc in range(2):
                    o_ = (hf * 2 + cc) * 128
                    for c in range(8):
                        self.mm(p[:, o_:o_ + 128], w[:, c, cc * 128:(cc + 1) * 128], self.hT[:, c, :], c == 0, c == 7, [wb, self.hT], [p])
            self.act(self.mgT[:, 4 * j:4 * j + 4, :].rearrange('p a b -> p (a b)'), p[:, :], AF.Sigmoid, [p], [self.mgT])

    def merge_ffn(self, xt, y_out, conv_out, conv_tok0_out):
        k = self.k
        for br, (oT, nm) in enumerate(((self.onsaT, 'w_nsa_out'), (self.oglaT, 'w_gla_out'), (self.oxT, 'w_x_out'))):
            for half in range(2):
                p = self.pa_next()
                for hf in range(2):
                    w, wb = self.wget('%s:o:%d' % (nm, half * 2 + hf))
                    for cc in range(2):
                        o_ = (hf * 2 + cc) * 128
                        for c in range(4):
                            self.mm(p[:, o_:o_ + 128], w[:, c, cc * 128:(cc + 1) * 128], oT[:, c, :], c == 0, c == 3, [wb, oT], [p])
                p3 = p[:, :].rearrange('p (a b) -> p a b', a=4)
                mg = self.mgT[:, br * 8 + half * 4:br * 8 + half * 4 + 4, :]
                hs = slice(half * 4, half * 4 + 4)
                if br == 0:
                    self.tt(self.mrg[:, hs, :], p3, mg, ALU.mult, [p, self.mgT], [self.mrg])
                else:
                    self.tt(self.mtmp[:], p3, mg, ALU.mult, [p, self.mgT], [self.mtmp])
                    dst = self.mrg if br == 1 else self.mrgT
                    self.tt(dst[:, hs, :], self.mrg[:, hs, :], self.mtmp[:], ALU.add, [self.mrg, self.mtmp], [dst])
        if self.stop < 11:
            return
        for half in range(2):
            p = self.pa_next()
            for hf in range(2):
                w, wb = self.wget('w_o:o:%d' % (half * 2 + hf))
                for c in range(8):
                    self.mm(p[:, hf * 256:(hf + 1) * 256], self.mrgT[:, c, :], w[:, c, :], c == 0, c == 7, [self.mrgT, wb], [p])
            self.tt(self.x1[:, half * 512:(half + 1) * 512], p[:, :], xt[:, half * 512:(half + 1) * 512], ALU.add, [p, xt], [self.x1])
        if self.stop < 12:
            return
        self.norm_T(self.x1, 2, self.h2T)
        if self.stop < 13:
            return
        groups = [(j0, min(2, NFF - j0)) for j0 in range(0, NFF, 2)]
        for gi_, (j0, ng) in enumerate(groups):
            gx, ub = self.gx[gi_ % 2], self.ub[gi_ % 2]
            banks = []
            for jj in range(ng):
                w, wb = self.wget('w_up:u:%d' % (j0 + jj))
                if jj % 2 == 0:
                    p = self.pa_next()
                    banks.append(p)
                for cc in range(2):
                    o_ = ((jj % 2) * 2 + cc) * 128
                    for c in range(8):
                        self.mm(p[:, o_:o_ + 128], w[:, c, cc * 128:(cc + 1) * 128], self.h2T[:, c, :], c == 0, c == 7, [wb, self.h2T], [p])
            self.cp(gx[:, 0:ng, 0:2], self.hist[:, j0:j0 + ng, :], [self.hist], [gx], eng='pool')
            for bi, p in enumerate(banks):
                nb_ = min(2, ng - 2 * bi)
                p4 = p[:, 0:nb_ * 256].rearrange('p (a t b) -> p a t b', t=2, b=128)
                self.act(ub[:, 2 * bi:2 * bi + nb_, :], p4[:, :, 0, :], AF.Copy, [p], [ub])
                self.cp(gx[:, 2 * bi:2 * bi + nb_, 2:130], p4[:, :, 1, :], [p], [gx])
            self.cp(self.hist[:, j0:j0 + ng, :], gx[:, 0:ng, 128:130], [gx], [self.hist], eng='pool')
            if conv_tok0_out is not None:
                self.cp(self.g0col[:, j0:j0 + ng], gx[:, 0:ng, 2], [gx], [self.g0col], eng='pool')
            cw = lambda t: self.convw[:, j0:j0 + ng, t:t + 1].to_broadcast([128, ng, 128])
            a_, b_ = self.cva[:, 0:ng, :], self.cvb[:, 0:ng, :]
            self.tt(a_, gx[:, 0:ng, 0:128], cw(0), ALU.mult, [gx, self.gains], [self.cva])
            self.tt(b_, gx[:, 0:ng, 1:129], cw(1), ALU.mult, [gx, self.gains], [self.cvb], eng='pool')
            self.tt(a_, a_, b_, ALU.add, [self.cva, self.cvb], [self.cva])
            self.tt(b_, gx[:, 0:ng, 2:130], cw(2), ALU.mult, [gx, self.gains], [self.cvb], eng='pool')
            self.tt(a_, a_, b_, ALU.add, [self.cva, self.cvb], [self.cva])
            self.tt(a_, a_, self.convb[:, j0:j0 + ng].unsqueeze(2).to_broadcast([128, ng, 128]), ALU.add, [self.cva, self.gains], [self.cva])
            self.gelu_tanh(b_, a_, b_, [self.cva], [self.cvb], self.cvb)
            self.tt(self.actT[:, j0:j0 + ng, :], b_, ub[:, 0:ng, :], ALU.mult, [self.cvb, ub], [self.actT])
        if self.stop < 14:
            return
        with k.nc.allow_non_contiguous_dma(reason="conv state columns written transposed (<= 2 x 2816 elements)"):
            if conv_out is not None:
                for t_ in range(2):
                    k.dma('sp', conv_out[t_].rearrange('(j p) -> p j', p=128), self.hist[:, :, t_], reads=[self.hist])
            if conv_tok0_out is not None:
                k.dma('sp', conv_tok0_out.rearrange('t (j p) -> p (t j)', p=128), self.g0col[:], reads=[self.g0col])
        if self.stop < 15:
            return
        p0, p1 = self.pa_next(), self.pa_next()
        for j in range(11):
            w, wb = self.wget('w_down:d:%d' % j)
            for c in range(2):
                f = 2 * j + c
                self.mm(p0[:, :], self.actT[:, f, :], w[:, c, 0:512], f == 0, f == NFF - 1, [self.actT, wb], [p0])
                self.mm(p1[:, :], self.actT[:, f, :], w[:, c, 512:1024], f == 0, f == NFF - 1, [self.actT, wb], [p1])
        self.tt(self.yt[:, 0:512], p0[:, :], self.x1[:, 0:512], ALU.add, [p0, self.x1], [self.yt])
        self.tt(self.yt[:, 512:1024], p1[:, :], self.x1[:, 512:1024], ALU.add, [p1, self.x1], [self.yt])
        if y_out is not None:
            k.dma('sp', y_out[0], self.yt[y_out[1], :], reads=[self.yt])

    def q_tile(self, slot, xt, mset, S, Sb, rowmask_col, rows_out, win_out, y_out, conv_out, conv_tok0_out=None):
        mk_ = lambda nm: self.marks.append(('  %d:%s' % (slot, nm), dict(self.k.cnt))) if slot in (48, 63, 64) else None
        self.stage_a(slot, xt, True, True, rows_out, win_out)
        mk_('stage_a')
        if self.stop < 6:
            return
        self.own_rest_a()
        mk_('rest_a')
        if self.stop < 7:
            return
        self.nsa_tile(slot, mset, nq=(32 if slot == 64 else 128))
        mk_('nsa')
        if self.stop < 8:
            return
        self.gla_tile(S, Sb, rowmask_col)
        mk_('gla')
        if self.stop < 9:
            return
        self.xatt_tile()
        mk_('xatt')
        if self.stop < 10:
            return
        self.merge_ffn(xt, y_out, conv_out, conv_tok0_out)
        mk_('merge_ffn')

    def mem_to_resident(self, mt, memrows):
        p = self.pa_next()
        for h in range(4):
            self.tr(p[:, h * 128:(h + 1) * 128], memrows[:, h * 128:(h + 1) * 128], True, [memrows], [p])
        self.cp(self.k_memT[:, :, mt * 128:(mt + 1) * 128], p[:, :].rearrange('p (h m) -> p h m', h=4), [p], [self.k_memT])
        self.cp(self.v_mem[:, mt, :, 0:128], memrows[:, 512:1024].rearrange('p (h d) -> p h d', h=4), [memrows], [self.v_mem], eng='pool')

    def mem_kv_prompt(self):
        k = self.k
        self.sched_cols('w_mem_kv', 'm', 0, 1024)
        self.sched_cols('w_mem_kv', 'm2', 0, 1024)
        for mt in range(2):
            xt = self.xt[mt % len(self.xt)]
            self.load_x(xt, self.i['mem'][mt * 128:(mt + 1) * 128, :])
            self.norm_T(xt, 1, self.hT)
            tg = 'm' if mt == 0 else 'm2'
            for half in range(2):
                p = self.pa_next()
                for hf in range(2):
                    w, wb = self.wget('w_mem_kv:%s:%d' % (tg, half * 2 + hf))
                    for c in range(8):
                        self.mm(p[:, hf * 256:(hf + 1) * 256], self.hT[:, c, :], w[:, c, :], c == 0, c == 7, [self.hT, wb], [p])
                if half == 0:
                    sq = self.headnorm(p[:, :], 4, 128, [p])
                    self.tt(self.memrows[:, 0:512].rearrange('p (h d) -> p h d', h=4), sq[:, 0:512].rearrange('p (h d) -> p h d', h=4),
                            self.gxk[:].unsqueeze(1).to_broadcast([128, 4, 128]), ALU.mult, [sq, self.gains], [self.memrows])
                else:
                    self.act(self.memrows[:, 512:1024], p[:, :], AF.Copy, [p], [self.memrows])
            k.dma('sp', self.o['memkv_p'][mt * 128:(mt + 1) * 128, :], self.memrows[:], reads=[self.memrows])
            self.mem_to_resident(mt, self.memrows)

    def prompt_phase(self):
        k, i, o = self.k, self.i, self.o
        self.mem_kv_prompt()
        if self.stop < 3:
            return
        slots = list(range(FIRST_Q - self.n_prefix, FIRST_Q + self.n_own))
        self.p_slots = slots
        for s in slots:
            self.sched_tile(s >= FIRST_Q)
        if self.do_sample:
            for _ in range(self.n_samp):
                self.sched_tile(True)
        self.load_x(self.xt[0], i['xloc'][slots[0] * 128:(slots[0] + 1) * 128, :])
        self.marks = [('start', dict(self.k.cnt))]
        for n, s in enumerate(slots):
            self.marks.append(('slot%d' % s, dict(self.k.cnt)))
            xt = self.xt[n % len(self.xt)]
            if n + 1 < len(slots):
                s2 = slots[n + 1]
                nxt_x = (self.xt[(n + 1) % len(self.xt)], i['xloc'][s2 * 128:(s2 + 1) * 128, :])
                if len(self.xt) > 1:
                    self.load_x(*nxt_x)
            ncast = -(-len(self.late_casts) // max(1, (FIRST_Q - s))) if s < FIRST_Q else len(self.late_casts)
            for _ in range(ncast):
                nm_, dst_, src_ = self.late_casts.pop(0)
                k.dma('pool', dst_, src_, writes=[self.sbuf_w[nm_]])
            if s < FIRST_Q:
                gi = n % 4
                self.stage_a(s, xt, False, s >= FIRST_Q - 4, gi=gi, flush=(gi == 3 or slots[n + 1] >= FIRST_Q))
                if self.stop < 4:
                    continue
                self.la_compute(self.e0[:, 0:1])
                self.gla_state_update(self.S, self.Sb, False)
                if self.stop < 5:
                    return
            else:
                t = s - 48
                rows_out = (o['rows_p'][t * 128:(t + 1) * 128, :], slice(0, 128)) if t >= 0 else None
                win_out = (o['win_p'][(t - 12) * 128:(t - 11) * 128, :], slice(0, 128)) if t >= 12 else None
                y_out = (o['y_p'][t * 128:(t + 1) * 128, :], slice(0, 128)) if t >= 0 else None
                conv_out = o['conv_p'] if s == slots[-1] else None
                self.q_tile(s, xt, 0, self.S, self.Sb, self.e0[:, 0:1], rows_out, win_out, y_out, conv_out)
                pass
            if len(self.xt) == 1 and n + 1 < len(slots):
                self.load_x(*nxt_x)
            if s == FIRST_Q and self.stop >= 99:
                if True:
                    self.ts(self.hist[:], self.hist[:], self.hs[:, 0:1], None, ALU.mult, None, [self.hist, self.hs], [self.hist])
        k.dma('sp', o['gla_p'].rearrange('h d v -> d h v'), self.S[:], reads=[self.S])

    def sample_phase(self):
        k, i, o = self.k, self.i, self.o
        ns = self.n_samp
        xs_t = self.xt[0]
        k.dma('sp', self.ptb[:], i['page_table'].rearrange('s j -> (s j)').partition_broadcast(128), writes=[self.ptb])
        k.op('pool', lambda e: e.iota(self.iop[:], pattern=[[0, 1]], base=0, channel_multiplier=1), (), [self.iop])
        self.cp(self.e1[:, 0:1], self.iop[:], [self.iop], [self.e1])
        self.cp(self.la[:, 0:ns * 64], self.ptb[:], [self.ptb], [self.la])
        self.ts(self.la[:, 0:ns * 64], self.la[:, 0:ns * 64], 128.0, self.e1[:, 0:1], ALU.mult, ALU.add, [self.la, self.e1], [self.la])
        self.cp(self.idx[:], self.la[:, 0:ns * 64], [self.la], [self.idx])
        self.memset(xs_t[:], 0.0, [xs_t])
        for s in range(ns):
            self.marks.append(('fill%d' % s, dict(self.k.cnt)))
            def gather(j):
                pg = self.page[j % 2]
                col = s * 64 + j
                k.dma('pool', None, None, reads=[self.idx], writes=[pg],
                      fn=lambda e, pg=pg, col=col: e.indirect_dma_start(out=pg[:], out_offset=None, in_=i['cache_kv'],
                                                                        in_offset=bass.IndirectOffsetOnAxis(ap=self.idx[:, col:col + 1], axis=0)))
            gather(0)
            for j in range(64):
                pg = self.page[j % 2]
                if j + 1 < 64:
                    gather(j + 1)
                pgT = T(pg.t[:].rearrange('p (a b) -> p a b', a=4)); pgT.b = pg.b
                self.a2_rows(j, pgT, j % 4)
                if j % 4 == 3:
                    self.compress(j - 3, 4)
            for t in range(4):
                k.dma('sp', self.winrows[:].rearrange('p a b -> p (a b)'), i['cache_win'][s, t * 128:(t + 1) * 128, :], writes=[self.winrows])
                self.a2_win(60 + t, self.winrows)
            k.dma('sp', self.S[:], i['state_gla'][s].rearrange('h d v -> d h v'), writes=[self.S])
            self.cp(self.Sb[:], self.S[:], [self.S], [self.Sb])
            with k.nc.allow_non_contiguous_dma(reason="conv state loaded transposed (2 x 2816 elements)"):
                for t_ in range(2):
                    k.dma('sp', self.hist[:, :, t_], i['state_conv'][s, t_].rearrange('(j p) -> p j', p=128), writes=[self.hist])
            for mt in range(2):
                k.dma('sp', self.memrows[:], i['cache_mem'][s, mt * 128:(mt + 1) * 128, :], writes=[self.memrows])
                self.mem_to_resident(mt, self.memrows)
            k.dma('sp', xs_t[0:1, :], i['xs'][s:s + 1, :], writes=[xs_t])
            k.dma('sp', o['win_s'][s, 0:511, :], i['cache_win'][s, 1:512, :])
            k.dma('sp', o['conv_s'][s, 0:1, :], i['state_conv'][s, 1:2, :])
            self.marks.append(('sq%d' % s, dict(self.k.cnt)))
            self.q_tile(64, xs_t, 1, self.S, self.Sb, self.e0[:, 1:2],
                        (o['rows_s'][s:s + 1, :], slice(0, 1)), (o['win_s'][s, 511:512, :], slice(0, 1)),
                        (o['y_s'][s:s + 1, :], slice(0, 1)), None, conv_tok0_out=o['conv_s'][s, 1:2, :])
            k.dma('sp', o['gla_s'][s].rearrange('h d v -> d h v'), self.S[:], reads=[self.S])


def _t5_bucket_np(d):
    n = np.maximum(d, 0)
    nf = np.maximum(n, 1).astype(np.float32)
    large = 16 + (np.log(nf / np.float32(16)) / np.float32(np.log(128 / 16)) * np.float32(16)).astype(np.int32)
    return np.where(n < 16, n, np.minimum(large, 31))


def _constants():
    c = {}
    c['c_ident'] = np.eye(128, dtype=np.float32)
    s = np.arange(128)
    c['c_U'] = (s[:, None] <= s[None, :]).astype(np.float32)
    c['c_L'] = (s[:, None] > s[None, :]).astype(np.float32)
    c['c_onesbd'] = np.kron(np.eye(2), np.ones((64, 64))).astype(np.float32)
    c['c_edge'] = (NEG * (s[:, None] >= s[None, :])).astype(np.float32).astype(ml_dtypes.bfloat16)
    d = np.arange(DMIN, DMAX + 1)
    oh = np.zeros((33, ND), np.float32)
    valid = (d >= 0) & (d < 512)
    b = _t5_bucket_np(d)
    oh[b[valid], np.nonzero(valid)[0]] = 1.0
    oh[32, ~valid] = 1.0
    c['c_ohda'] = oh
    c['c_ohdr'] = np.ascontiguousarray(oh[:, ::-1])
    relb = np.zeros((128, 3), np.float32)
    relb[:64, 0] = 1e4
    relb[:, 1] = 1e4
    relb[64:, 2] = 1e4
    relb[:64, 2] = -1e30
    c['c_relb'] = relb
    e0 = np.zeros((128, 2), np.float32)
    e0[:, 0] = 1.0
    e0[0, 1] = 1.0
    c['c_e0'] = e0
    return c


def _masks(npre):
    sb = np.zeros((2, 128, NSLOT), np.float32)
    sb[0, :, :npre] = NEG
    pm = np.full((2, 128, 136), NEG, np.float32)
    pm[0, :, :2 * npre] += NEG
    cm = np.zeros((2, 1, 528), np.float32)
    cm[0, 0, :8 * npre + 1] = NEG
    cm[1, 0, 0] = NEG
    fb = np.zeros((2, 128, 136), np.float32)
    fb[0, :, 2 * npre] = 1e4
    fb[1, :, 0] = 1e4
    hs = np.full((128, 1), 0.0 if npre > FIRST_Q else 1.0, np.float32)
    return dict(m_slotbias=sb, m_pm=pm, m_cmask=cm, m_fbabs=fb, m_hs=hs)


_CACHE = {}


def _get_nc(key, **kw):
    if key not in _CACHE:
        _CACHE[key] = MK(**kw).nc
    return _CACHE[key]


def kernel(x_prompt, x_sample, cache_kv, cache_win, state_gla, state_conv, cache_mem, page_table, mem_prompt,
           g_mix, w_in, g_nsa_q, g_nsa_k, cmp_k_pe, cmp_k_w1, cmp_k_w2, cmp_v_pe, cmp_v_w1, cmp_v_w2,
           rel_bias, w_gla_gate, b_gla_gate, g_gla_o, g_mem, w_mem_kv, g_x_q, g_x_k,
           w_nsa_out, w_gla_out, w_x_out, w_o, g_ffn, w_up, conv_w, conv_b, w_down):
    f = lambda a: np.ascontiguousarray(np.asarray(a, dtype=np.float32))
    x_prompt = f(x_prompt); x_sample = f(x_sample).reshape(32, D)
    cache_kv2 = f(cache_kv).reshape(-1, 512)
    cache_win2 = f(cache_win).reshape(32, 512, 256)
    state_gla = f(state_gla); state_conv = f(state_conv)
    cache_mem2 = f(cache_mem).reshape(32, 256, 1024)
    page_table = np.ascontiguousarray(np.asarray(page_table, dtype=np.int32))
    mem_prompt = f(mem_prompt)
    w_up_p = f(w_up).reshape(D, 2, NFF, 128).transpose(0, 2, 1, 3).reshape(D, 2 * DFF)
    shared = dict(
        g_mix=f(g_mix), g_mem=f(g_mem), g_ffn=f(g_ffn), w_in=np.ascontiguousarray(f(w_in)[:, W_IN_PERM]),
        g_nsa_q=f(g_nsa_q), g_nsa_k=f(g_nsa_k), cmp_k_pe=f(cmp_k_pe), cmp_k_w1=f(cmp_k_w1), cmp_k_w2=f(cmp_k_w2),
        cmp_v_pe=f(cmp_v_pe), cmp_v_w1=f(cmp_v_w1), cmp_v_w2=f(cmp_v_w2), rel_bias=f(rel_bias),
        w_gla_gate=f(w_gla_gate), b_gla_gate=f(b_gla_gate).reshape(1, 256), g_gla_o=f(g_gla_o), g_x_q=f(g_x_q), g_x_k=f(g_x_k),
        w_mem_kv=f(w_mem_kv), w_nsa_out=f(w_nsa_out), w_gla_out=f(w_gla_out), w_x_out=f(w_x_out), w_o=f(w_o),
        w_up=np.ascontiguousarray(w_up_p), conv_w=f(conv_w), conv_b=f(conv_b).reshape(1, DFF), w_down=f(w_down),
        cache_kv=cache_kv2)
    shared.update(_constants())
    in_maps = []
    for core in range(8):
        b, c = core // 4, core % 4
        npre = 48 - 16 * c
        xloc = np.zeros((64 * 128, D), np.float32)
        xloc[npre * 128:] = x_prompt[b, :(64 - npre) * 128] if False else x_prompt[b, (16 * (c + 1) - (64 - npre)) * 128:16 * (c + 1) * 128]
        m = dict(shared)
        m.update(_masks(npre))
        sl = slice(4 * core, 4 * core + 4)
        m.update(xloc=xloc, xs=x_sample[sl], cache_win=cache_win2[sl], state_gla=state_gla[sl], state_conv=state_conv[sl],
                 cache_mem=cache_mem2[sl], page_table=page_table[sl], mem=mem_prompt[b])
        in_maps.append(m)
    nc = _get_nc('full')
    res = run_bass_kernel_spmd(nc, in_maps, core_ids=list(range(8))).results
    r = lambda core, nm: np.asarray(res[core][nm], dtype=np.float32)
    y_p = np.stack([np.concatenate([r(4 * b + c, 'y_p') for c in range(4)], 0) for b in range(2)])
    rows_p = np.stack([np.concatenate([r(4 * b + c, 'rows_p') for c in range(4)], 0) for b in range(2)]).reshape(2, 8192, 4, 2, 64)
    win_p = np.stack([r(4 * b + 3, 'win_p') for b in range(2)]).reshape(2, 512, 2, 2, 64)
    gla_p = np.stack([r(4 * b + 3, 'gla_p') for b in range(2)])
    conv_p = np.stack([r(4 * b + 3, 'conv_p') for b in range(2)])
    memkv_p = np.stack([r(4 * b, 'memkv_p') for b in range(2)]).reshape(2, 256, 2, 4, 128)
    cat = lambda nm: np.concatenate([r(core, nm) for core in range(8)], 0)
    y_s = cat('y_s').reshape(32, 1, D)
    rows_s = cat('rows_s').reshape(32, 1, 4, 2, 64)
    win_s = cat('win_s').reshape(32, 512, 2, 2, 64)
    gla_s = cat('gla_s')
    conv_s = cat('conv_s')
    return (y_p, y_s, rows_p, win_p, gla_p, conv_p, memkv_p, rows_s, win_s, gla_s, conv_s)
```

```python
import numpy as np
import ml_dtypes
from contextlib import ExitStack
import concourse.bass as bass
import concourse.mybir as mybir
from concourse.bass_utils import run_bass_kernel_spmd

F32 = mybir.dt.float32
BF16 = mybir.dt.bfloat16
I32 = mybir.dt.int32
AF = mybir.ActivationFunctionType
ALU = mybir.AluOpType
AX = mybir.AxisListType

D = 1024
DFF = 2816
NFF = 22
EPS = 1e-6
TINY = 1e-30
NEG = -30000.0
NSLOT = 65
DMIN = -160
DMAX = 700
ND = DMAX - DMIN + 1
C0, C1, C2, C3, C4, C5, C6, CMG = 0, 512, 1024, 1320, 1832, 2344, 2856, 3368
W_IN_PERM = np.concatenate([np.arange(0, 512), np.arange(512, 1024), np.arange(1024, 1280),
                            np.arange(2328, 2344), np.arange(1280, 1304), np.arange(1304, 1816),
                            np.arange(1816, 2328), np.arange(2344, 2856), np.arange(2856, 3368),
                            np.arange(3368, 6440)])
FIRST_Q = 47
PREFIX_WIN_FROM = 43


class Buf:
    __slots__ = ('w', 'r', 'excl')

    def __init__(self):
        self.w = None
        self.r = {}
        self.excl = False


class T:
    def __init__(self, t):
        self.t = t
        self.b = Buf()

    def __getitem__(self, k):
        return self.t[k]


class KB:
    def __init__(self, nds=24):
        self.nc = bass.Bass('TRN2', target_bir_lowering=False)
        nc = self.nc
        self.es = ExitStack()
        self.eng = dict(pe=nc.tensor, act=nc.scalar, dve=nc.vector, pool=nc.gpsimd, sp=nc.sync)
        self.sem = {e: self.es.enter_context(nc.semaphore('s_' + e)) for e in ('pe', 'act', 'dve', 'pool')}
        self.cnt = dict.fromkeys(self.sem, 0)
        self.NDS = nds
        self.dsem = [self.es.enter_context(nc.semaphore('d%d' % i)) for i in range(nds)]
        self.dcnt = [0] * nds
        self.dnext = 0
        self.dnext_sw = 0
        self.waited = {e: {} for e in self.eng}
        self.nbuf = 0
        self.ninstr = 0

    def sb(self, shape, dt=F32):
        self.nbuf += 1
        return T(self.es.enter_context(self.nc.sbuf_tensor('sb%d' % self.nbuf, list(shape), dt)))

    def ps(self, shape, dt=F32):
        self.nbuf += 1
        t = T(self.es.enter_context(self.nc.psum_tensor('ps%d' % self.nbuf, list(shape), dt)))
        t.b.excl = True
        return t

    def din(self, name, shape, dt=F32):
        return self.nc.dram_tensor(name, list(shape), dt, kind="ExternalInput").ap()

    def dout(self, name, shape, dt=F32):
        return self.nc.dram_tensor(name, list(shape), dt, kind="ExternalOutput").ap()

    def dint(self, name, shape, dt=F32):
        return self.nc.dram_tensor(name, list(shape), dt, kind="Internal").ap()

    def _wait(self, e, dep):
        k, v = dep
        w = self.waited[e]
        if w.get(k, 0) >= v:
            return
        w[k] = v
        sem = self.sem[k] if isinstance(k, str) else self.dsem[k]
        self.eng[e].wait_ge(sem, v)
        self.ninstr += 1

    def _deps(self, e, reads, writes):
        deps = set()
        for b in reads:
            if b.w is not None:
                deps.add(b.w)
        for b in writes:
            if b.w is not None:
                deps.add(b.w)
            deps.update(b.r.values())
        for d in deps:
            if e == 'pe' and d[0] == 'pe':
                continue
            self._wait(e, d)

    def _mark(self, me, key, reads, writes):
        for b in reads:
            b.r[key] = me
        for b in writes:
            b.w = me
            b.r = {}

    def op(self, e, fn, reads=(), writes=()):
        reads = [x.b if isinstance(x, T) else x for x in reads]
        writes = [x.b if isinstance(x, T) else x for x in writes]
        writes = writes + [b for b in reads if b.excl and b not in writes]
        self._deps(e, reads, writes)
        ins = fn(self.eng[e])
        self.cnt[e] += 1
        self.ninstr += 1
        ins.then_inc(self.sem[e], 1)
        me = (e, self.cnt[e])
        self._mark(me, e, reads, writes)
        return me

    def dma(self, q, out, in_, reads=(), writes=(), fn=None, **kw):
        reads = [x.b if isinstance(x, T) else x for x in reads]
        writes = [x.b if isinstance(x, T) else x for x in writes]
        if q == 'pool':
            i = self.NDS - 8 + self.dnext_sw
            self.dnext_sw = (self.dnext_sw + 1) % 8
        else:
            i = self.dnext
            self.dnext = (i + 1) % (self.NDS - 8)
        if self.dcnt[i] > 0:
            self._wait(q, (i, self.dcnt[i]))
        self._deps(q, reads, writes)
        if fn is None:
            ins = self.eng[q].dma_start(out=out, in_=in_, **kw)
        else:
            ins = fn(self.eng[q])
        self.dcnt[i] += 16
        self.ninstr += 1
        ins.then_inc(self.dsem[i], 16)
        me = (i, self.dcnt[i])
        self._mark(me, i, reads, writes)
        return me

    def finish(self):
        for i in range(self.NDS):
            if self.dcnt[i] > 0:
                self._wait('sp', (i, self.dcnt[i]))
        for e in self.sem:
            if self.cnt[e] > 0:
                self._wait('sp', (e, self.cnt[e]))
        self.es.close()
        return self.nc


class MK:
    def __init__(self, n_prefix=47, n_own=17, n_samp=4, do_sample=True):
        self.k = k = KB()
        self.n_prefix, self.n_own, self.n_samp, self.do_sample = n_prefix, n_own, n_samp, do_sample
        import os
        self.stop = int(os.environ.get('MK_STOP', '99'))
        self.declare_io()
        self.alloc()
        self.setup()
        if self.stop >= 2:
            self.prompt_phase()
        if do_sample and self.stop >= 90:
            self.sample_phase()
        self.nc = k.finish()

    def mm(self, out, lhsT, rhs, start, stop, reads, writes):
        self.k.op('pe', lambda e: e.matmul(out, lhsT=lhsT, rhs=rhs, start=start, stop=stop, skip_group_check=True), reads, writes)

    def tr(self, out, in_, f32, reads, writes):
        idt = self.ident_f if f32 else self.ident_b
        self.k.op('pe', lambda e: e.transpose(out=out, in_=in_, identity=idt.t[:]), list(reads) + [idt], writes)

    def act(self, out, in_, func, reads, writes, **kw):
        self.k.op('act', lambda e: e.activation(out=out, in_=in_, func=func, **kw), reads, writes)

    def tt(self, out, in0, in1, op, reads, writes, eng='dve'):
        self.k.op(eng, lambda e: e.tensor_tensor(out=out, in0=in0, in1=in1, op=op), reads, writes)

    def ts(self, out, in0, s1, s2, op0, op1, reads, writes, eng='dve'):
        if op1 is None:
            self.k.op(eng, lambda e: e.tensor_scalar(out=out, in0=in0, scalar1=s1, scalar2=None, op0=op0), reads, writes)
        else:
            self.k.op(eng, lambda e: e.tensor_scalar(out=out, in0=in0, scalar1=s1, scalar2=s2, op0=op0, op1=op1), reads, writes)

    def stt(self, out, in0, scalar, in1, op0, op1, reads, writes):
        self.k.op('dve', lambda e: e.scalar_tensor_tensor(out=out, in0=in0, scalar=scalar, in1=in1, op0=op0, op1=op1), reads, writes)

    def cp(self, out, in_, reads, writes, eng='dve'):
        self.k.op(eng, lambda e: e.tensor_copy(out=out, in_=in_), reads, writes)

    def recip(self, ap, t):
        self.k.op('dve', lambda e: e.reciprocal(out=ap, in_=ap), [t], [t])

    def memset(self, ap, val, writes, eng='pool'):
        self.k.op(eng, lambda e: e.memset(ap, val), (), writes)

    def pa_next(self):
        self._pai = (self._pai + 1) % len(self.pa)
        return self.pa[self._pai]

    def pb_next(self):
        self._pbi = (self._pbi + 1) % len(self.pb)
        return self.pb[self._pbi]

    def rstd_from_ss(self, ss_ap, n, cnt, reads_t, out_t):
        self.act(out_t[:, 0:n], ss_ap, AF.Sqrt, list(reads_t) + [self.eps_t], [out_t], scale=1.0 / cnt, bias=self.eps_t[:, 0:1])
        self.recip(out_t[:, 0:n], out_t)

    def headnorm(self, src, H, Dh, reads):
        sq, ss, rs = self.hn_sq, self.hn_ss, self.hn_rs
        n = H * Dh
        self.act(sq[:, 0:n], src, AF.Square, reads, [sq])
        self.k.op('dve', lambda e: e.tensor_reduce(out=ss[:, 0:H], in_=sq[:, 0:n].rearrange('p (h d) -> p h d', h=H), axis=AX.X, op=ALU.add), [sq], [ss])
        self.rstd_from_ss(ss[:, 0:H], H, Dh, [ss], rs)
        self.tt(sq[:, 0:n].rearrange('p (h d) -> p h d', h=H), src.rearrange('p (h d) -> p h d', h=H),
                rs[:, 0:H].unsqueeze(2).to_broadcast([128, H, Dh]), ALU.mult, list(reads) + [rs], [sq])
        return sq

    def gelu_tanh(self, out, x, tmp, reads, writes, tmp_t):
        self.tt(tmp, x, x, ALU.mult, reads, [tmp_t])
        self.ts(tmp, tmp, 0.044715, 1.0, ALU.mult, ALU.add, [tmp_t], [tmp_t])
        self.tt(tmp, tmp, x, ALU.mult, list(reads) + [tmp_t], [tmp_t])
        self.act(tmp, tmp, AF.Sigmoid, [tmp_t], [tmp_t], scale=1.5957691216057308)
        self.tt(out, tmp, x, ALU.mult, list(reads) + [tmp_t], writes)

    def declare_io(self):
        k = self.k
        ns_ = self.n_samp
        i = {}
        i['xloc'] = k.din('xloc', [64 * 128, D])
        i['xs'] = k.din('xs', [ns_, D])
        i['cache_kv'] = k.din('cache_kv', [(2560 if self.do_sample else 2) * 128, 512])
        i['cache_win'] = k.din('cache_win', [ns_, 512, 256])
        i['state_gla'] = k.din('state_gla', [ns_, 4, 64, 128])
        i['state_conv'] = k.din('state_conv', [ns_, 2, DFF])
        i['cache_mem'] = k.din('cache_mem', [ns_, 256, 1024])
        i['page_table'] = k.din('page_table', [ns_, 64], I32)
        i['mem'] = k.din('mem', [256, D])
        i['g_mix'] = k.din('g_mix', [D]); i['g_mem'] = k.din('g_mem', [D]); i['g_ffn'] = k.din('g_ffn', [D])
        i['w_in'] = k.din('w_in', [D, 6440])
        i['g_nsa_q'] = k.din('g_nsa_q', [64]); i['g_nsa_k'] = k.din('g_nsa_k', [3, 64])
        for nm in ('k', 'v'):
            i['cmp_%s_pe' % nm] = k.din('cmp_%s_pe' % nm, [32, 64])
            i['cmp_%s_w1' % nm] = k.din('cmp_%s_w1' % nm, [32, 64, 64])
            i['cmp_%s_w2' % nm] = k.din('cmp_%s_w2' % nm, [64, 64])
        i['rel_bias'] = k.din('rel_bias', [32, 8])
        i['w_gla_gate'] = k.din('w_gla_gate', [16, 256]); i['b_gla_gate'] = k.din('b_gla_gate', [1, 256])
        i['g_gla_o'] = k.din('g_gla_o', [128]); i['g_x_q'] = k.din('g_x_q', [128]); i['g_x_k'] = k.din('g_x_k', [128])
        i['w_mem_kv'] = k.din('w_mem_kv', [D, 1024])
        i['w_nsa_out'] = k.din('w_nsa_out', [512, D]); i['w_gla_out'] = k.din('w_gla_out', [512, D]); i['w_x_out'] = k.din('w_x_out', [512, D])
        i['w_o'] = k.din('w_o', [D, D])
        i['w_up'] = k.din('w_up', [D, 2 * DFF])
        i['conv_w'] = k.din('conv_w', [3, DFF]); i['conv_b'] = k.din('conv_b', [1, DFF])
        i['w_down'] = k.din('w_down', [DFF, D])
        i['c_ident'] = k.din('c_ident', [128, 128]); i['c_U'] = k.din('c_U', [128, 128]); i['c_L'] = k.din('c_L', [128, 128])
        i['c_onesbd'] = k.din('c_onesbd', [128, 128])
        i['c_ohdr'] = k.din('c_ohdr', [33, ND]); i['c_ohda'] = k.din('c_ohda', [33, ND])
        i['c_relb'] = k.din('c_relb', [128, 3]); i['c_edge'] = k.din('c_edge', [128, 128], BF16)
        i['c_e0'] = k.din('c_e0', [128, 2])
        i['m_slotbias'] = k.din('m_slotbias', [2, 128, NSLOT])
        i['m_pm'] = k.din('m_pm', [2, 128, 136])
        i['m_cmask'] = k.din('m_cmask', [2, 1, 528])
        i['m_fbabs'] = k.din('m_fbabs', [2, 128, 136]); i['m_hs'] = k.din('m_hs', [128, 1])
        self.i = i
        o = {}
        o['y_p'] = k.dout('y_p', [16 * 128, D])
        o['rows_p'] = k.dout('rows_p', [16 * 128, 512])
        o['win_p'] = k.dout('win_p', [512, 256])
        o['gla_p'] = k.dout('gla_p', [4, 64, 128])
        o['conv_p'] = k.dout('conv_p', [2, DFF])
        o['memkv_p'] = k.dout('memkv_p', [256, 1024])
        o['y_s'] = k.dout('y_s', [ns_, D])
        o['rows_s'] = k.dout('rows_s', [ns_, 512])
        o['win_s'] = k.dout('win_s', [ns_, 512, 256])
        o['gla_s'] = k.dout('gla_s', [ns_, 4, 64, 128])
        o['conv_s'] = k.dout('conv_s', [ns_, 2, DFF])
        self.o = o
        s = {}
        s['w_in'] = k.dint('s_w_in', [128, 8, 6440], BF16)
        s['w_mem_kv'] = k.dint('s_w_mem_kv', [128, 8, 1024], BF16)
        s['w_nsa_out'] = k.dint('s_w_nsa_out', [128, 4, D], BF16)
        s['w_gla_out'] = k.dint('s_w_gla_out', [128, 4, D], BF16)
        s['w_x_out'] = k.dint('s_w_x_out', [128, 4, D], BF16)
        s['w_o'] = k.dint('s_w_o', [128, 8, D], BF16)
        s['w_up'] = k.dint('s_w_up', [128, 8, 2 * DFF], BF16)
        s['w_down'] = k.dint('s_w_down', [128, NFF, D], BF16)
        self.s = s
        self.sbuf_w = {nm: Buf() for nm in s}

    def alloc(self):
        k = self.k
        sb, ps = k.sb, k.ps
        self.pa = [ps([128, 512]) for _ in range(4)]
        self.pacc = [ps([128, 512]) for _ in range(2)]
        self.pb = [ps([128, 1024], BF16) for _ in range(2)]
        self._pai = self._pbi = 0
        self.ident_f = sb([128, 128]); self.ident_b = sb([128, 128], BF16)
        self.U = sb([128, 128]); self.L = sb([128, 128]); self.onesbd = sb([128, 128])
        self.I4 = sb([128, 4, 128], BF16)
        self.rbp = sb([33, 8]); self.rb31 = sb([32, 8])
        self.relb = sb([128, 3]); self.e0 = sb([128, 2])
        self.ones_row = sb([1, 128], BF16)
        self.ones_col = sb([128, 1])
        self.eps_t = sb([128, 1])
        self.slotbias = [sb([128, NSLOT]) for _ in range(2)]
        self.pm = [sb([128, 136]) for _ in range(2)]
        self.cmask = [sb([1, 528], BF16) for _ in range(2)]
        self.cmask_f = sb([1, 528])
        self.fbabs = [sb([128, 136]) for _ in range(2)]; self.hs = sb([128, 1])
        self.BT = sb([128, 2, 2, 512])
        self.BT512T = sb([128, 128], BF16)
        self.BC = sb([128, 8, 16])
        self.gains = Buf()
        self.gcol = sb([128, 3, 8])
        self.gq = sb([128, 64]); self.gk = sb([128, 3, 64]); self.ggo = sb([128, 128]); self.gxq = sb([128, 128]); self.gxk = sb([128, 128])
        self.gk0col = sb([128, 1])
        self.convw = sb([128, NFF, 3]); self.convb = sb([128, NFF])
        self.w1bd = [sb([128, 32, 128], BF16) for _ in range(2)]
        self.w2bd = [sb([128, 128], BF16) for _ in range(2)]
        self.b1 = [sb([128, 1]) for _ in range(2)]
        self.peT = sb([128, 2, 32], BF16)
        self.wg17 = sb([17, 256], BF16)
        self.stage = sb([128, 64])
        self.k_selT = sb([128, NSLOT * 128], BF16); self.b_ksel = [Buf() for _ in range(NSLOT)]
        self.v_sel = sb([128, NSLOT, 2, 65], BF16); self.b_vsel = [Buf() for _ in range(NSLOT)]
        self.k_winT = sb([128, 8 * 128], BF16); self.v_win = sb([128, 8, 2, 65], BF16); self.b_win = [Buf() for _ in range(8)]
        self.k_cmpT = sb([128, 528], BF16); self.b_kcmp = Buf()
        self.hgvT = sb([128, 640], BF16)
        self.v_cmp = sb([128, 5, 2, 64], BF16); self.b_vcmp = Buf()
        self.rawT = [sb([128, 16 + 512], BF16) for _ in range(2)]
        self.k_memT = sb([128, 4, 256], BF16); self.v_mem = sb([128, 2, 4, 129], BF16)
        self.S = sb([64, 4, 128]); self.Sb = sb([64, 4, 128], BF16)
        self.xt = [sb([128, D]) for _ in range(1)]
        self.big = sb([128, 2048])
        def view(ap):
            t = T(ap); t.b = self.big.b
            return t
        self.pc = view(self.big.t[:, :].rearrange('p (a b) -> p a b', a=4))
        self.x1 = view(self.big.t[:, 0:1024])
        self.hn_sq = view(self.big.t[:, 1024:2048])
        self.yt = self.hn_sq; self.memrows = self.hn_sq
        self.mtmp = view(self.big.t[:, 1536:2048].rearrange('p (a b) -> p a b', a=4))
        self.BTs = view(self.big.t[:, 0:512].rearrange('p (k o n) -> p k o n', k=2, o=2))
        self.qTs = sb([128, 128], BF16); self.I4s = sb([128, 128], BF16)
        self.hn_ss = sb([128, 8]); self.hn_rs = sb([128, 8])
        self.hb = sb([128, D], BF16)
        self.hT = sb([128, 8, 128], BF16)
        self.rows = sb([128, 4, 128]); self.winrows = sb([128, 2, 128])
        self.qn = sb([128, 4, 2, 64], BF16); self.qT = sb([128, 4, 128], BF16)
        self.gates = sb([128, 24])
        self.lrn = sb([128, 16], BF16); self.lrT = sb([17, 128], BF16)
        self.gqk = sb([128, 512], BF16); self.gqkT = sb([64, 8, 128], BF16)
        self.gv = sb([128, 512], BF16); self.gr = sb([128, 512], BF16)
        self.xqn = sb([128, 512], BF16); self.xqT = sb([128, 4, 128], BF16)
        self.mgT = sb([128, 24, 128], BF16)
        self.la = sb([128, 256]); self.e1 = sb([128, 256])
        self.ecb = sb([64, 4, 128]); self.encb = sb([64, 4, 128]); self.ecbl = sb([64, 4])
        self.qeT = sb([64, 4, 128], BF16); self.keT = sb([64, 4, 128], BF16)
        self.erc = sb([128, 256]); self.kl = sb([128, 256], BF16)
        self.Am = sb([128, 4, 128], BF16)
        self.ogla = sb([128, 512], BF16); self.oglaT = sb([128, 4, 128], BF16)
        self.psum8 = sb([128, 8]); self.prc8 = sb([128, 8])
        self.pg = sb([128, 2, 528]); self.imp = sb([128, 2, 136]); self.m16 = sb([128, 2, 16]); self.wk = sb([128, 136])
        self.neg = sb([128, 2, 136], BF16)
        self.negx = [[sb([128, 512], BF16) for _ in range(2)] for _ in range(2)]
        self.pcT = sb([128, 4, 128], BF16)
        self.pT = [sb([128, 512], BF16) for _ in range(4)]; self._pTi = 0
        self.oc = sb([128, 512]); self.osel = sb([128, 8, 64], BF16); self.owin = sb([128, 8, 64], BF16); self.den = sb([128, 8])
        self.onsa = sb([128, 512], BF16); self.onsaT = sb([128, 4, 128], BF16)
        self.pxT = sb([128, 2, 4, 128], BF16); self.ox = sb([128, 512], BF16); self.oxT = sb([128, 4, 128], BF16)
        self.mrg = view(self.big.t[:, 0:1024].rearrange('p (a b) -> p a b', a=8)); self.mrgT = sb([128, 8, 128], BF16)
        self.h2T = self.hT
        self.hist = sb([128, NFF, 2]); self.gx = [sb([128, 2, 130]) for _ in range(2)]; self.ub = [sb([128, 2, 128], BF16) for _ in range(2)]
        self.cva = sb([128, 2, 128]); self.cvb = sb([128, 2, 128]); self.g0col = sb([128, NFF])
        self.actT = sb([128, NFF, 128], BF16)
        self.hg = sb([128, 32], BF16); self.hx = sb([128, 32]); self.hx2 = sb([128, 32]); self.hxs = [self.hx, sb([128, 32])]; self.hx2s = [self.hx2, sb([128, 32])]; self.kc = sb([128, 32]); self.kc2 = sb([128, 32]); self.kcr = sb([128, 32])
        self.idx = sb([128, self.n_samp * 64], I32); self.ptb = sb([128, self.n_samp * 64], I32); self.iop = sb([128, 1], I32)
        self.page = [sb([128, 512]) for _ in range(2)]
        pcf = self.big.t[:, :]
        self.ohdr = T(pcf[0:33, 0:ND]); self.ohdr.b = self.pc.b
        self.ohda = T(pcf[0:33, ND:2 * ND]); self.ohda.b = self.pc.b
        self.NW = 4
        self.WAHEAD = 2
        self.wring = [sb([128, 2048], BF16) for _ in range(self.NW)]
        self.wq = []
        self.wissued = 0
        self.wnext = 0

    def wsched(self, tag, src, shape):
        self.wq.append((tag, src, shape))

    def _wview(self, r, shape):
        n = int(np.prod(shape[1:]))
        return r.t[:, 0:n].rearrange('p (a b) -> p a b', a=shape[1])

    def wget(self, tag):
        k = self.k
        i = self.wnext
        while self.stop < 99 and self.wq[i][0] != tag:
            self.wnext += 1; i = self.wnext; self.wissued = max(self.wissued, i)
        assert self.wq[i][0] == tag, (self.wq[i][0], tag)
        upto = min(len(self.wq), i + self.WAHEAD + 1)
        while self.wissued < upto:
            j = self.wissued
            tg, src, shape = self.wq[j]
            r = self.wring[j % self.NW]
            k.dma('sp', self._wview(r, shape), src, reads=[self.sbuf_w[tg.split(':')[0]]], writes=[r])
            self.wissued += 1
        self.wnext += 1
        r = self.wring[i % self.NW]
        return self._wview(r, self.wq[i][2]), r

    def sched_cols(self, nm, tag, c0, c1, nk=8):
        j = 0
        for a in range(c0, c1, 256):
            b = min(c1, a + 256)
            self.wsched('%s:%s:%d' % (nm, tag, j), self.s[nm][:, :, a:b], [128, nk, b - a])
            j += 1

    def sched_tile(self, own, need_win=True):
        if own:
            self.sched_cols('w_in', 'c0', C0, C1)
        self.sched_cols('w_in', 'c1', C1, C2)
        if own or need_win:
            self.sched_cols('w_in', 'c2', C2, C3)
        else:
            self.sched_cols('w_in', 'c2lr', C2 + 256, C2 + 272)
        if own:
            self.sched_cols('w_in', 'c3', C3, C4)
        else:
            self.sched_cols('w_in', 'c3k', C3 + 256, C4)
        self.sched_cols('w_in', 'c4', C4, C5)
        if own:
            self.sched_cols('w_in', 'c5', C5, C6)
            self.sched_cols('w_in', 'c6', C6, CMG)
            self.sched_cols('w_in', 'mg', CMG, 6440)
            for nm in ('w_nsa_out', 'w_gla_out', 'w_x_out'):
                self.sched_cols(nm, 'o', 0, D, nk=4)
            self.sched_cols('w_o', 'o', 0, D)
            self.sched_cols('w_up', 'u', 0, 2 * DFF)
            for j in range(11):
                self.wsched('w_down:d:%d' % j, self.s['w_down'][:, 2 * j:2 * j + 2, :], [128, 2, 1024])

    def setup(self):
        k, i, s = self.k, self.i, self.s

        def castw(nm, src3, nk, ncols):
            for c in range(nk):
                for c0 in range(0, ncols, 2048):
                    c1 = min(ncols, c0 + 2048)
                    k.dma('pool', s[nm][:, c, c0:c1], src3[:, c, c0:c1], writes=[self.sbuf_w[nm]])
        castw('w_mem_kv', i['w_mem_kv'].rearrange('(c p) n -> p c n', p=128), 8, 1024)
        w_in3 = i['w_in'].rearrange('(c p) n -> p c n', p=128)
        for c in range(8):
            k.dma('pool', s['w_in'][:, c, C1:C5], w_in3[:, c, C1:C5], writes=[self.sbuf_w['w_in']])
        self.late_casts = []
        for c in range(8):
            for (a_, b_) in ((C0, C1), (C5, C5 + 2048), (C5 + 2048, 6440)):
                self.late_casts.append(('w_in', s['w_in'][:, c, a_:b_], w_in3[:, c, a_:b_]))
        def castw_late(nm, src3, nk, ncols):
            for c in range(nk):
                for c0 in range(0, ncols, 2048):
                    c1 = min(ncols, c0 + 2048)
                    self.late_casts.append((nm, s[nm][:, c, c0:c1], src3[:, c, c0:c1]))
        for nm in ('w_nsa_out', 'w_gla_out', 'w_x_out'):
            castw_late(nm, i[nm].rearrange('(c p) n -> p c n', p=128), 4, D)
        castw_late('w_o', i['w_o'].rearrange('(c p) n -> p c n', p=128), 8, D)
        castw_late('w_up', i['w_up'].rearrange('(c p) n -> p c n', p=128), 8, 2 * DFF)
        castw_late('w_down', i['w_down'].rearrange('(c p) n -> p c n', p=128), NFF, D)

        def ld(t, src):
            k.dma('sp', t[:], src, writes=[t])
        ld(self.ident_f, i['c_ident']); ld(self.U, i['c_U']); ld(self.L, i['c_L']); ld(self.onesbd, i['c_onesbd'])
        ld(self.BT512T, i['c_edge'])
        ld(self.ohdr, i['c_ohdr']); ld(self.ohda, i['c_ohda']); ld(self.relb, i['c_relb']); ld(self.e0, i['c_e0'])
        for st in range(2):
            ld(self.slotbias[st], i['m_slotbias'][st]); ld(self.pm[st], i['m_pm'][st]); ld(self.fbabs[st], i['m_fbabs'][st])
            k.dma('sp', self.cmask_f[:], i['m_cmask'][st], writes=[self.cmask_f])
            self.cp(self.cmask[st][:], self.cmask_f[:], [self.cmask_f], [self.cmask[st]])
        ld(self.hs, i['m_hs'])
        self.cp(self.ident_b[:], self.ident_f[:], [self.ident_f], [self.ident_b])
        for g in range(4):
            self.cp(self.I4[:, g, :], self.ident_f[:], [self.ident_f], [self.I4])
        self.memset(self.ones_row[:], 1.0, [self.ones_row])
        self.memset(self.ones_col[:], 1.0, [self.ones_col])
        self.memset(self.eps_t[:], EPS, [self.eps_t])
        G = self.gains
        with k.nc.allow_non_contiguous_dma(reason="small one-time transposed parameter loads"):
            for j, nm in enumerate(('g_mix', 'g_mem', 'g_ffn')):
                k.dma('sp', self.gcol[:, j, :], i[nm].rearrange('(c p) -> p c', p=128), writes=[G])
            for t_ in range(3):
                k.dma('sp', self.convw[:, :, t_], i['conv_w'][t_].rearrange('(j p) -> p j', p=128), writes=[G])
            k.dma('sp', self.convb[:], i['conv_b'].rearrange('o (j p) -> p (o j)', p=128), writes=[G])
            for a, nm in enumerate(('k', 'v')):
                for kv in range(2):
                    k.dma('sp', self.stage[kv * 64:(kv + 1) * 64, a * 32:(a + 1) * 32], i['cmp_%s_pe' % nm].rearrange('j d -> d j'), writes=[self.stage])
            for kv in range(2):
                k.dma('sp', self.gk0col[kv * 64:(kv + 1) * 64, :], i['g_nsa_k'][0].rearrange('(d o) -> d o', o=1), writes=[G])
        for t, nm in ((self.gq, 'g_nsa_q'), (self.ggo, 'g_gla_o'), (self.gxq, 'g_x_q'), (self.gxk, 'g_x_k')):
            k.dma('sp', t[:], i[nm].partition_broadcast(128), writes=[G])
        for j in range(3):
            k.dma('sp', self.gk[:, j, :], i['g_nsa_k'][j].partition_broadcast(128), writes=[G])
        self.ts(self.gq[:], self.gq[:], 0.125, None, ALU.mult, None, [G], [G])
        self.ts(self.gxq[:], self.gxq[:], 128.0 ** -0.5, None, ALU.mult, None, [G], [G])
        self.cp(self.peT[:].rearrange('p a j -> p (a j)'), self.stage[:, 0:64], [self.stage], [self.peT])
        k.dma('sp', self.rbp[0:32, :], i['rel_bias'], writes=[self.rbp])
        k.dma('sp', self.rb31[:], i['rel_bias'][31].partition_broadcast(32), writes=[self.rb31])
        self.tt(self.rbp[0:32, :], self.rbp[0:32, :], self.rb31[:], ALU.subtract, [self.rbp, self.rb31], [self.rbp])
        self.memset(self.rbp[32:33, :], NEG, [self.rbp])
        for a, nm in enumerate(('k', 'v')):
            self.memset(self.w1bd[a][:], 0.0, [self.w1bd[a]])
            self.memset(self.w2bd[a][:], 0.0, [self.w2bd[a]])
            for kv in range(2):
                k.dma('pool', self.w1bd[a][kv * 64:(kv + 1) * 64, :, kv * 64:(kv + 1) * 64], i['cmp_%s_w1' % nm].rearrange('j d e -> d j e'), writes=[self.w1bd[a]])
                k.dma('pool', self.w2bd[a][kv * 64:(kv + 1) * 64, kv * 64:(kv + 1) * 64], i['cmp_%s_w2' % nm], writes=[self.w2bd[a]])
            p = self.pa_next()
            for j in range(32):
                self.mm(p[:, 0:2], self.w1bd[a][:, j, :], self.peT[:, a, j:j + 1].to_broadcast([128, 2]), j == 0, j == 31, [self.w1bd[a], self.peT], [p])
            self.cp(self.b1[a][:], p[:, 0:1], [p], [self.b1[a]])
        k.dma('pool', self.wg17[0:16, :], i['w_gla_gate'], writes=[self.wg17])
        k.dma('pool', self.wg17[16:17, :], i['b_gla_gate'], writes=[self.wg17])
        self.memset(self.lrT[:], 1.0, [self.lrT])
        zb = Buf()
        for t in (self.k_selT, self.k_winT, self.k_cmpT, self.v_sel, self.v_win):
            self.memset(t[:], 0.0, [zb])
        self.memset(self.v_sel[:, :, :, 64:65], 1.0, [zb])
        self.memset(self.v_win[:, :, :, 64:65], 1.0, [zb])
        self.memset(self.v_cmp[:], 0.0, [zb])
        for t in (self.hgvT, self.pg, self.hist, self.S, self.Sb, self.rawT[0], self.rawT[1], self.v_mem, self.neg, self.k_memT, self.pcT, self.negx[0][0], self.negx[0][1], self.negx[1][0], self.negx[1][1]):
            self.memset(t[:], 0.0, [t])
        self.memset(self.v_mem[:, :, :, 128:129], 1.0, [self.v_mem])
        for b in self.b_ksel + self.b_vsel + self.b_win + [self.b_kcmp, self.b_vcmp]:
            b.w = zb.w
        self.build_bias_tiles()

    def build_bias_tiles(self):
        for oi, off in enumerate((0, 128)):
            for half in range(2):
                p = self.pa_next()
                for ql in range(64):
                    q = half * 64 + ql
                    j0 = DMAX - off - q
                    self.mm(p[:, ql * 8:(ql + 1) * 8], self.ohdr[:, j0:j0 + 128], self.rbp[:, :], True, True, [self.ohdr, self.rbp], [p])
                src = p[:, :].rearrange('p (q h) -> p h q', h=8)
                for kv in range(2):
                    self.cp(self.BT[:, kv, oi, :].rearrange('p (g q) -> p g q', g=4)[:, :, half * 64:(half + 1) * 64],
                            src[:, kv * 4:(kv + 1) * 4, :], [p], [self.BT])
        p = self.pa_next()
        for uu in range(16):
            i0 = (113 - 16 * uu) - DMIN
            self.mm(p[:, uu * 8:(uu + 1) * 8], self.ohda[:, i0:i0 + 128], self.rbp[:, :], True, True, [self.ohda, self.rbp], [p])
        self.cp(self.BC[:], p[:, 0:128].rearrange('p (u h) -> p h u', h=8), [p], [self.BC])

    def load_x(self, xt, src):
        self.k.dma('sp', xt[:], src, writes=[xt])

    def norm_T(self, xt, gi, outT):
        sq, ss, rs = self.hn_sq, self.hn_ss, self.hn_rs
        self.act(sq[:], xt[:], AF.Square, [xt], [sq, ss], accum_out=ss[:, 0:1])
        self.rstd_from_ss(ss[:, 0:1], 1, D, [ss], rs)
        self.ts(self.hb[:], xt[:], rs[:, 0:1], None, ALU.mult, None, [xt, rs], [self.hb])
        p = self.pb_next()
        for c in range(8):
            self.tr(p[:, c * 128:(c + 1) * 128], self.hb[:, c * 128:(c + 1) * 128], False, [self.hb], [p])
        self.tt(outT[:], p[:, :].rearrange('p (a b) -> p a b', a=8), self.gcol[:, gi, :].unsqueeze(2).to_broadcast([128, 8, 128]), ALU.mult,
                [p, self.gains], [outT])

    def proj(self, nm, tag, ncols, srcT=None):
        srcT = srcT or self.hT
        p = self.pa_next()
        j = 0
        for a in range(0, ncols, 256):
            b = min(ncols, a + 256)
            w, wb = self.wget('%s:%s:%d' % (nm, tag, j))
            for c in range(8):
                self.mm(p[:, a:b], srcT[:, c, :], w[:, c, :], c == 0, c == 7, [srcT, wb], [p])
            j += 1
        return p

    def a2_rows(self, slot, rows, gi=0):
        p = self.pa_next()
        for j, sl in enumerate((0, 1, 2)):
            self.tr(p[:, j * 128:(j + 1) * 128], rows[:, sl, :], True, [rows], [p])
        for a in range(2):
            self.cp(self.rawT[a][:, 16 + gi * 128:16 + (gi + 1) * 128], p[:, a * 128:(a + 1) * 128], [p], [self.rawT[a]])
        self.cp(self.k_selT[:, slot * 128:(slot + 1) * 128], p[:, 256:384], [p], [self.b_ksel[slot]])
        self.cp(self.v_sel[:, slot, :, 0:64], rows[:, 3, :].rearrange('p (k d) -> p k d', k=2), [rows], [self.b_vsel[slot]], eng='pool')

    def a2_win(self, slot, winrows):
        r = slot % 8
        p = self.pa_next()
        self.tr(p[:, 0:128], winrows[:, 0, :], True, [winrows], [p])
        self.cp(self.k_winT[:, r * 128:(r + 1) * 128], p[:, 0:128], [p], [self.b_win[r]])
        self.cp(self.v_win[:, r, :, 0:64], winrows[:, 1, :].rearrange('p (k d) -> p k d', k=2), [winrows], [self.b_win[r]], eng='pool')

    def compress(self, slot0, G=1):
        cc0 = 8 * slot0
        n8 = 8 * G

        def chain(a):
            hx, hx2 = self.hxs[a], self.hx2s[a]
            p = self.pa_next()
            n = 0
            for r in range(2):
                for j in range(16):
                    st = 16 * r + j
                    self.mm(p[:, 0:n8], self.w1bd[a][:, st, :], self.rawT[a][:, st:st + 16 * (n8 - 1) + 1:16], n == 0, n == 31, [self.w1bd[a], self.rawT[a]], [p])
                    n += 1
            self.cp(self.rawT[a][:, 0:16], self.rawT[a][:, G * 128:G * 128 + 16], [self.rawT[a]], [self.rawT[a]], eng='pool')
            self.act(hx[:, 0:n8], p[:, 0:n8], AF.Identity, [p, self.b1[a]], [hx], bias=self.b1[a][:, 0:1])
            yield
            x, t = hx[:, 0:n8], hx2[:, 0:n8]
            self.tt(t, x, x, ALU.mult, [hx], [hx2])
            yield
            self.ts(t, t, 0.044715, 1.0, ALU.mult, ALU.add, [hx2], [hx2])
            yield
            self.tt(t, t, x, ALU.mult, [hx, hx2], [hx2])
            yield
            self.act(t, t, AF.Sigmoid, [hx2], [hx2], scale=1.5957691216057308)
            yield
            if a == 0:
                self.tt(self.hg[:, 0:n8], t, x, ALU.mult, [hx, hx2], [self.hg])
                yield
                p2 = self.pa_next()
                self.mm(p2[:, 0:n8], self.w2bd[0][:], self.hg[:, 0:n8], True, True, [self.w2bd[0], self.hg], [p2])
                yield
                self.cp(self.kc[:, 0:n8], p2[:, 0:n8], [p2], [self.kc])
                yield
                self.tt(self.kc2[:, 0:n8], self.kc[:, 0:n8], self.kc[:, 0:n8], ALU.mult, [self.kc], [self.kc2])
                yield
                p3 = self.pa_next()
                self.mm(p3[:, 0:n8], self.onesbd[:], self.kc2[:, 0:n8], True, True, [self.onesbd, self.kc2], [p3])
                yield
                self.rstd_from_ss(p3[:, 0:n8], n8, 64, [p3], self.kcr)
                yield
                self.tt(self.kc[:, 0:n8], self.kc[:, 0:n8], self.kcr[:, 0:n8], ALU.mult, [self.kc, self.kcr], [self.kc])
                yield
                self.ts(self.k_cmpT[:, cc0:cc0 + n8], self.kc[:, 0:n8], self.gk0col[:, 0:1], None, ALU.mult, None, [self.kc, self.gains], [self.b_kcmp])
                yield
            else:
                self.tt(self.hgvT[:, cc0:cc0 + n8], t, x, ALU.mult, [hx, hx2], [self.hgvT])
                yield
                for g in range(cc0 // 128, (cc0 + n8 - 1) // 128 + 1):
                    p2 = self.pa_next()
                    self.mm(p2[:, 0:128], self.hgvT[:, g * 128:(g + 1) * 128], self.w2bd[1][:], True, True, [self.hgvT, self.w2bd[1]], [p2])
                    yield
                    self.cp(self.v_cmp[:, g, :, :], p2[:, 0:128].rearrange('p (k d) -> p k d', k=2), [p2], [self.b_vcmp])
                    yield

        gens = [chain(0), chain(1)]
        while gens:
            for g_ in list(gens):
                try:
                    next(g_)
                except StopIteration:
                    gens.remove(g_)

    def la_compute(self, rowmask_col):
        p = self.pb_next()
        self.tr(p[0:16, 0:128], self.lrn[:], False, [self.lrn], [p])
        self.cp(self.lrT[0:16, :], p[0:16, 0:128], [p], [self.lrT])
        pz = self.pa_next()
        self.mm(pz[:, 0:256], self.lrT[:], self.wg17[:], True, True, [self.lrT, self.wg17], [pz])
        self.act(self.e1[:], pz[:, 0:256], AF.Exp, [pz], [self.e1], scale=-1.0)
        self.act(self.e1[:], self.e1[:], AF.Ln, [self.e1], [self.e1], bias=1.0)
        self.ts(self.la[:], self.e1[:], rowmask_col, -1.0 / 16.0, ALU.mult, ALU.mult, [self.e1, self.e0], [self.la])

    def gla_state_update(self, S, Sb, ecbl_ready):
        prc = self.pa_next()
        self.mm(prc[:, 0:256], self.L[:], self.la[:], True, True, [self.L, self.la], [prc])
        self.act(self.erc[:], prc[:, 0:256], AF.Exp, [prc], [self.erc])
        self.tt(self.kl[:], self.gqk[:, 256:512], self.erc[:], ALU.mult, [self.gqk, self.erc], [self.kl])
        if not ecbl_ready:
            pl = self.pa_next()
            for h in range(4):
                self.mm(pl[0:64, 2 * h:2 * h + 2], self.la[:, h * 64:(h + 1) * 64], self.ones_col[:, 0:1].to_broadcast([128, 2]), True, True, [self.la, self.ones_col], [pl])
            self.act(self.ecbl[:], pl[0:64, 0:8:2], AF.Exp, [pl], [self.ecbl])
        pu = self.pa_next()
        for h in range(4):
            self.mm(pu[0:64, h * 128:(h + 1) * 128], self.kl[:, h * 64:(h + 1) * 64], self.gv[:, h * 128:(h + 1) * 128], True, True, [self.kl, self.gv], [pu])
        for h in range(4):
            self.stt(S[:, h, :], S[:, h, :], self.ecbl[:, h:h + 1], pu[0:64, h * 128:(h + 1) * 128], ALU.mult, ALU.add, [S, self.ecbl, pu], [S])
        self.cp(Sb[:], S[:], [S], [Sb])

    def stage_a(self, slot, xt, own, need_win, rows_out=None, win_out=None, gi=0, flush=True):
        k = self.k
        self.norm_T(xt, 0, self.hT)
        if own:
            p = self.proj('w_in', 'c0', 512)
            sq = self.headnorm(p[:, :], 8, 64, [p])
            self.tt(self.qn[:].rearrange('p g k d -> p k g d'), sq[:, 0:512].rearrange('p (k g d) -> p k g d', k=2, g=4),
                    self.gq[:].unsqueeze(1).unsqueeze(1).to_broadcast([128, 2, 4, 64]), ALU.mult, [sq, self.gains], [self.qn])
            pq = self.pb_next()
            for g in range(4):
                self.tr(pq[:, g * 128:(g + 1) * 128], self.qn[:, g, :, :].rearrange('p k d -> p (k d)'), False, [self.qn], [pq])
            self.cp(self.qT[:].rearrange('p a b -> p (a b)'), pq[:, 0:512], [pq], [self.qT])
        p = self.proj('w_in', 'c1', 512)
        self.act(self.rows[:, 0:2, :].rearrange('p a b -> p (a b)'), p[:, 0:256], AF.Copy, [p], [self.rows])
        self.act(self.rows[:, 3, :], p[:, 384:512], AF.Copy, [p], [self.rows])
        sq = self.headnorm(p[:, 256:384], 2, 64, [p])
        self.tt(self.rows[:, 2, :].rearrange('p (h d) -> p h d', h=2), sq[:, 0:128].rearrange('p (h d) -> p h d', h=2),
                self.gk[:, 1, :].unsqueeze(1).to_broadcast([128, 2, 64]), ALU.mult, [sq, self.gains], [self.rows])
        if rows_out is not None:
            k.dma('sp', rows_out[0], self.rows[rows_out[1], :, :].rearrange('p a b -> p (a b)'), reads=[self.rows])
        self.a2_rows(slot, self.rows, gi)
        if flush:
            self.compress(slot - gi, gi + 1)
        lr0 = 256
        if own or need_win:
            p = self.proj('w_in', 'c2', 296)
        else:
            p = self.proj('w_in', 'c2lr', 16)
            lr0 = 0
        if need_win:
            sq = self.headnorm(p[:, 0:128], 2, 64, [p])
            self.tt(self.winrows[:, 0, :].rearrange('p (h d) -> p h d', h=2), sq[:, 0:128].rearrange('p (h d) -> p h d', h=2),
                    self.gk[:, 2, :].unsqueeze(1).to_broadcast([128, 2, 64]), ALU.mult, [sq, self.gains], [self.winrows])
            self.act(self.winrows[:, 1, :], p[:, 128:256], AF.Copy, [p], [self.winrows])
            if win_out is not None:
                k.dma('sp', win_out[0], self.winrows[win_out[1], :, :].rearrange('p a b -> p (a b)'), reads=[self.winrows])
            self.a2_win(slot, self.winrows)
        self.cp(self.lrn[:], p[:, lr0:lr0 + 16], [p], [self.lrn])
        if own:
            self.act(self.gates[:], p[:, 272:296], AF.Sigmoid, [p], [self.gates])
        if own:
            p = self.proj('w_in', 'c3', 512)
            self.act(self.gqk[:, 0:256], p[:, 0:256], AF.Copy, [p], [self.gqk], scale=0.125)
            self.cp(self.gqk[:, 256:512], p[:, 256:512], [p], [self.gqk])
        else:
            p = self.proj('w_in', 'c3k', 256)
            self.cp(self.gqk[:, 256:512], p[:, 0:256], [p], [self.gqk])
        p = self.proj('w_in', 'c4', 512)
        self.act(self.gv[:], p[:, :], AF.Copy, [p], [self.gv])

    def tr4(self, src, dstT):
        p = self.pb_next()
        for j in range(4):
            self.tr(p[:, j * 128:(j + 1) * 128], src[:, j * 128:(j + 1) * 128], False, [src], [p])
        self.cp(dstT[:].rearrange('p a b -> p (a b)'), p[:, 0:512], [p], [dstT])

    def gla_tile(self, S, Sb, rowmask_col):
        self.la_compute(rowmask_col)
        p = self.pb_next()
        for j in range(8):
            self.tr(p[0:64, j * 128:(j + 1) * 128], self.gqk[:, j * 64:(j + 1) * 64], False, [self.gqk], [p])
        self.cp(self.gqkT[:].rearrange('p a b -> p (a b)'), p[0:64, :], [p], [self.gqkT])
        pc = self.pa_next()
        for h in range(4):
            self.mm(pc[0:64, h * 128:(h + 1) * 128], self.la[:, h * 64:(h + 1) * 64], self.U[:], True, True, [self.la, self.U], [pc])
        self.act(self.ecb[:].rearrange('p a b -> p (a b)'), pc[0:64, :], AF.Exp, [pc], [self.ecb])
        self.act(self.encb[:].rearrange('p a b -> p (a b)'), pc[0:64, :], AF.Exp, [pc], [self.encb], scale=-1.0)
        self.cp(self.ecbl[:], self.ecb[:, :, 127], [self.ecb], [self.ecbl])
        self.tt(self.qeT[:], self.gqkT[:, 0:4, :], self.ecb[:], ALU.mult, [self.gqkT, self.ecb], [self.qeT])
        self.tt(self.keT[:], self.gqkT[:, 4:8, :], self.encb[:], ALU.mult, [self.gqkT, self.encb], [self.keT])
        pA = self.pa_next()
        for h in range(4):
            self.mm(pA[:, h * 128:(h + 1) * 128], self.keT[:, h, :], self.qeT[:, h, :], True, True, [self.keT, self.qeT], [pA])
        self.tt(self.Am[:], pA[:, :].rearrange('p (h t) -> p h t', h=4), self.U[:].unsqueeze(1).to_broadcast([128, 4, 128]), ALU.mult, [pA, self.U], [self.Am])
        po = self.pa_next()
        for h in range(4):
            self.mm(po[:, h * 128:(h + 1) * 128], self.Am[:, h, :], self.gv[:, h * 128:(h + 1) * 128], h == 0, False, [self.Am, self.gv], [po])
            self.mm(po[:, h * 128:(h + 1) * 128], self.qeT[:, h, :], Sb[:, h, :], False, h == 3, [self.qeT, Sb], [po])
        sq = self.headnorm(po[:, :], 4, 128, [po])
        self.tt(sq[:, 0:512].rearrange('p (h d) -> p h d', h=4), sq[:, 0:512].rearrange('p (h d) -> p h d', h=4),
                self.ggo[:].unsqueeze(1).to_broadcast([128, 4, 128]), ALU.mult, [sq, self.gains], [sq])
        self.tt(self.ogla[:], sq[:, 0:512], self.gr[:], ALU.mult, [sq, self.gr], [self.ogla])
        self.tr4(self.ogla, self.oglaT)
        self.gla_state_update(S, Sb, True)

    def nsa_tile(self, sq_, mset, nq=128):
        k = self.k
        sq = sq_
        ncc = min(8 * (sq + 1), 512)
        nb = 2 * (sq + 1)
        ngrp = (ncc + 127) // 128
        pacc_c = self.pacc[0]
        for kv in range(2):
            rs = slice(kv * 64, (kv + 1) * 64)
            banks = []
            for g in range(4):
                p = self.pa_next()
                banks.append(p)
                self.mm(p[:, 0:ncc], self.qT[rs, g, :], self.k_cmpT[rs, 0:ncc], True, False, [self.qT, self.b_kcmp], [p])
                self.mm(p[:, 0:ncc], self.ones_row[:, :], self.cmask[mset][:, 0:ncc], False, True, [self.ones_row, self.cmask[mset]], [p])
            w0 = 8 * (sq - 1)
            w1 = min(8 * (sq + 1), ncc)
            for g in range(4):
                h = kv * 4 + g
                p = banks[g]
                self.tt(p[:, w0:w1], p[:, w0:w1], self.BC[:, h, 0:w1 - w0], ALU.add, [p, self.BC], [p])
                self.act(self.pc[:, g, 0:ncc], p[:, 0:ncc], AF.Exp, [p], [self.pc, self.psum8], accum_out=self.psum8[:, h:h + 1])
            hs = slice(kv * 4, kv * 4 + 4)
            self.ts(self.prc8[:, hs], self.psum8[:, hs], TINY, None, ALU.max, None, [self.psum8], [self.prc8])
            self.recip(self.prc8[:, hs], self.prc8)
            self.tt(self.pc[:, :, 0:ncc], self.pc[:, :, 0:ncc], self.prc8[:, hs].unsqueeze(2).to_broadcast([128, 4, ncc]), ALU.mult, [self.pc, self.prc8], [self.pc])
            k.op('dve', lambda e: e.tensor_reduce(out=self.pg[:, kv, 0:ncc], in_=self.pc[:, :, 0:ncc].rearrange('p g c -> p c g'), axis=AX.X, op=ALU.add), [self.pc], [self.pg])
            for g_ in range(ngrp):
                w = min(128, ncc - g_ * 128)
                p = self.pa_next()
                for g in range(4):
                    self.tr(p[0:w, g * 128:(g + 1) * 128], self.pc[:, g, g_ * 128:g_ * 128 + w], True, [self.pc], [p])
                self.cp(self.pcT[0:w, :, :].rearrange('p a b -> p (a b)'), p[0:w, 0:512], [p], [self.pcT])
                for g in range(4):
                    h = kv * 4 + g
                    self.mm(pacc_c[:, h * 64:(h + 1) * 64], self.pcT[0:w, g, :], self.v_cmp[0:w, g_, kv, :], (kv == 0 and g_ == 0 and g == 0),
                            (kv == 1 and g_ == ngrp - 1 and g == 3), [self.pcT, self.b_vcmp], [pacc_c])
        self.cp(self.oc[:], pacc_c[:, :], [pacc_c], [self.oc])
        if ncc < 528:
            self.memset(self.pg[:, :, ncc:528], 0.0, [self.pg], eng='dve')
        k.op('dve', lambda e: e.tensor_reduce(out=self.imp[:, :, 0:nb], in_=self.pg[:, :, 0:4 * nb].rearrange('p k (j f) -> p k j f', f=4), axis=AX.X, op=ALU.add), [self.pg], [self.imp])
        self.tt(self.imp[:, :, 0:nb], self.imp[:, :, 0:nb], self.pg[:, :, 4:4 * nb + 4:4], ALU.add, [self.imp, self.pg], [self.imp])
        self.tt(self.imp[:, :, 0:nb], self.imp[:, :, 0:nb], self.fbabs[mset][:, 0:nb].unsqueeze(1).to_broadcast([128, 2, nb]), ALU.add, [self.imp, self.fbabs[mset]], [self.imp])
        self.tt(self.imp[:, :, nb - 3:nb], self.imp[:, :, nb - 3:nb], self.relb[:].unsqueeze(1).to_broadcast([128, 2, 3]), ALU.add, [self.imp, self.relb], [self.imp])
        for kv in range(2):
            k.op('dve', lambda e: e.max(out=self.m16[:, kv, 0:8], in_=self.imp[:, kv, 0:nb]), [self.imp], [self.m16])
            k.op('dve', lambda e: e.match_replace(out=self.wk[:, 0:nb], in_to_replace=self.m16[:, kv, 0:8], in_values=self.imp[:, kv, 0:nb], imm_value=-3.0e38), [self.imp, self.m16], [self.wk])
            k.op('dve', lambda e: e.max(out=self.m16[:, kv, 8:16], in_=self.wk[:, 0:nb]), [self.wk], [self.m16])
            self.ts(self.wk[:, 0:nb], self.imp[:, kv, 0:nb], self.m16[:, kv, 15:16], -NEG, ALU.is_ge, ALU.mult, [self.imp, self.m16], [self.wk])
            self.tt(self.neg[:, kv, 0:nb], self.wk[:, 0:nb], self.pm[mset][:, 0:nb], ALU.add, [self.wk, self.pm[mset]], [self.neg])
        self.mg_proj()
        NQ = 4 * nq
        if nq == 128:
            qrhs = [self.qT[kv * 64:(kv + 1) * 64, :, :].rearrange('p g q -> p (g q)') for kv in range(2)]
            I4f, I4t = self.I4[:].rearrange('p g q -> p (g q)'), self.I4
            BTv = lambda kv, oi: self.BT[:, kv, oi, :]
            BTt = self.BT
        else:
            self.cp(self.qTs[:, 0:NQ].rearrange('p (g q) -> p g q', g=4), self.qT[:, :, 0:nq], [self.qT], [self.qTs])
            self.cp(self.I4s[:, 0:NQ].rearrange('p (g q) -> p g q', g=4), self.I4[:, :, 0:nq], [self.I4], [self.I4s], eng='pool')
            for kv in range(2):
                for oi in range(2):
                    self.cp(self.BTs[:, kv, oi, 0:NQ].rearrange('p (g q) -> p g q', g=4),
                            self.BT[:, kv, oi, :].rearrange('p (g q) -> p g q', g=4)[:, :, 0:nq], [self.BT], [self.BTs], eng='pool')
            qrhs = [self.qTs[kv * 64:(kv + 1) * 64, 0:NQ] for kv in range(2)]
            I4f, I4t = self.I4s[:, 0:NQ], self.I4s
            BTv = lambda kv, oi: self.BTs[:, kv, oi, 0:NQ]
            BTt = self.BTs
        for br in (1, 0):
            kts = list(range(0, sq + 1)) if br == 0 else [t for t in range(sq - 4, sq + 1) if t >= 0]
            ngx = [None, None]

            def stage1(kt):
                banks = []
                if br == 0 and kt % 4 == 0 and kt < sq:
                    nblk = min(8, 2 * sq - 2 * kt)
                    for kv in range(2):
                        t_ = self.negx[kv][(kt // 4) % 2]
                        ngx[kv] = t_
                        self.cp(t_[:, 0:nblk * 64].rearrange('p (j s) -> p j s', s=64), self.neg[:, kv, 2 * kt:2 * kt + nblk].unsqueeze(2).to_broadcast([128, nblk, 64]),
                                [self.neg], [t_], eng='pool')
                off = 128 * (sq - kt)
                masked = (br == 0 and kt < sq)
                edge = (br == 1 and off == 512)
                for kv in range(2):
                    rs = slice(kv * 64, (kv + 1) * 64)
                    p = self.pa_next()
                    if br == 0:
                        kT, kb = self.k_selT[rs, kt * 128:(kt + 1) * 128], self.b_ksel[kt]
                    else:
                        r = kt % 8
                        kT, kb = self.k_winT[rs, r * 128:(r + 1) * 128], self.b_win[r]
                    self.mm(p[:, 0:NQ], kT, qrhs[kv], True, not (masked or edge), [kb, self.qT, self.qTs], [p])
                    banks.append(p)
                for kv in range(2):
                    p = banks[kv]
                    if masked:
                        t_ = ngx[kv]
                        self.mm(p[:, 0:NQ], t_[:, (kt % 4) * 128:(kt % 4 + 1) * 128], I4f, False, True, [t_, I4t], [p])
                    if edge:
                        self.mm(p[:, 0:NQ], self.BT512T[:, :], I4f, False, True, [self.BT512T, I4t], [p])
                    if off in (0, 128):
                        self.tt(p[:, 0:NQ], p[:, 0:NQ], BTv(kv, off // 128), ALU.add, [p, BTt], [p])
                return banks

            def stage2(n, kt, banks):
                for kv in range(2):
                    p = banks[kv]
                    pT = self.pT[self._pTi]
                    self._pTi = (self._pTi + 1) % len(self.pT)
                    self.act(pT[:, 0:NQ], p[:, 0:NQ], AF.Exp, [p, self.slotbias[mset]], [pT], bias=self.slotbias[mset][:, kt:kt + 1])
                    if br == 0:
                        vv, vb = self.v_sel[:, kt, kv, :], self.b_vsel[kt]
                    else:
                        r = kt % 8
                        vv, vb = self.v_win[:, r, kv, :], self.b_win[r]
                    pacc = self.pacc[kv]
                    for g in range(4):
                        self.mm(pacc[0:nq, g * 65:(g + 1) * 65], pT[:, g * nq:(g + 1) * nq], vv, (n == 0 and g == 0), (n == len(kts) - 1 and g == 3), [pT, vb], [pacc])

            nxt = stage1(kts[0])
            for n, kt in enumerate(kts):
                cur = nxt
                if n + 1 < len(kts):
                    nxt = stage1(kts[n + 1])
                stage2(n, kt, cur)
            dst = self.osel if br == 0 else self.owin
            for kv in range(2):
                pacc = self.pacc[kv]
                acc3 = pacc[0:nq, 0:260].rearrange('p (g d) -> p g d', g=4)
                hs = slice(kv * 4, kv * 4 + 4)
                self.ts(self.den[0:nq, hs], acc3[:, :, 64], TINY, None, ALU.max, None, [pacc], [self.den])
                self.recip(self.den[0:nq, hs], self.den)
                self.tt(dst[0:nq, hs, :], acc3[:, :, 0:64], self.den[0:nq, hs].unsqueeze(2).to_broadcast([nq, 4, 64]), ALU.mult, [pacc, self.den], [dst])
        g3 = self.gates[:].rearrange('p (h i) -> p h i', i=3)
        oc3 = self.oc[:].rearrange('p (h d) -> p h d', h=8)
        bc = lambda i: g3[:, :, i].unsqueeze(2).to_broadcast([128, 8, 64])
        self.tt(oc3, oc3, bc(0), ALU.mult, [self.oc, self.gates], [self.oc])
        self.tt(self.osel[:], self.osel[:], bc(1), ALU.mult, [self.osel, self.gates], [self.osel])
        self.tt(self.owin[:], self.owin[:], bc(2), ALU.mult, [self.owin, self.gates], [self.owin])
        self.tt(oc3, oc3, self.osel[:], ALU.add, [self.oc, self.osel], [self.oc])
        self.tt(self.onsa[:].rearrange('p (h d) -> p h d', h=8), oc3, self.owin[:], ALU.add, [self.oc, self.owin], [self.onsa])
        self.tr4(self.onsa, self.onsaT)

    def xatt_tile(self):
        for mt in range(2):
            p = self.pa_next()
            for h in range(4):
                self.mm(p[:, h * 128:(h + 1) * 128], self.k_memT[:, h, mt * 128:(mt + 1) * 128], self.xqT[:, h, :], True, True, [self.k_memT, self.xqT], [p])
            self.act(self.pxT[:, mt, :, :].rearrange('p a b -> p (a b)'), p[:, :], AF.Exp, [p], [self.pxT])
        for hp in range(2):
            p = self.pa_next()
            for hh in range(2):
                h = hp * 2 + hh
                for mt in range(2):
                    self.mm(p[:, hh * 129:(hh + 1) * 129], self.pxT[:, mt, h, :], self.v_mem[:, mt, h, :], (hh == 0 and mt == 0), (hh == 1 and mt == 1), [self.pxT, self.v_mem], [p])
            a3 = p[:, 0:258].rearrange('p (h d) -> p h d', h=2)
            self.ts(self.den[:, 0:2], a3[:, :, 128], TINY, None, ALU.max, None, [p], [self.den])
            self.recip(self.den[:, 0:2], self.den)
            self.tt(self.ox[:, hp * 256:(hp + 1) * 256].rearrange('p (h d) -> p h d', h=2), a3[:, :, 0:128], self.den[:, 0:2].unsqueeze(2).to_broadcast([128, 2, 128]), ALU.mult, [p, self.den], [self.ox])
        self.tr4(self.ox, self.oxT)

    def own_rest_a(self):
        p = self.proj('w_in', 'c5', 512)
        self.act(self.gr[:], p[:, :], AF.Silu, [p], [self.gr])
        p = self.proj('w_in', 'c6', 512)
        sq = self.headnorm(p[:, :], 4, 128, [p])
        self.tt(self.xqn[:].rearrange('p (h d) -> p h d', h=4), sq[:, 0:512].rearrange('p (h d) -> p h d', h=4),
                self.gxq[:].unsqueeze(1).to_broadcast([128, 4, 128]), ALU.mult, [sq, self.gains], [self.xqn])
        self.tr4(self.xqn, self.xqT)

    def mg_proj(self):
        for j in range(6):
            p = self.pa_next()
            for hf in range(2):
                w, wb = self.wget('w_in:mg:%d' % (2 * j + hf))
                for cc in range(2):
                    o_ = (hf * 2 + cc) * 128
                    for c in range(8):
                        self.mm(p[:, o_:o_ + 128], w[:, c, cc * 128:(cc + 1) * 128], self.hT[:, c, :], c == 0, c == 7, [wb, self.hT], [p])
            self.act(self.mgT[:, 4 * j:4 * j + 4, :].rearrange('p a b -> p (a b)'), p[:, :], AF.Sigmoid, [p], [self.mgT])

    def merge_ffn(self, xt, y_out, conv_out, conv_tok0_out):
        k = self.k
        for br, (oT, nm) in enumerate(((self.onsaT, 'w_nsa_out'), (self.oglaT, 'w_gla_out'), (self.oxT, 'w_x_out'))):
            for half in range(2):
                p = self.pa_next()
                for hf in range(2):
                    w, wb = self.wget('%s:o:%d' % (nm, half * 2 + hf))
                    for cc in range(2):
                        o_ = (hf * 2 + cc) * 128
                        for c in range(4):
                            self.mm(p[:, o_:o_ + 128], w[:, c, cc * 128:(cc + 1) * 128], oT[:, c, :], c == 0, c == 3, [wb, oT], [p])
                p3 = p[:, :].rearrange('p (a b) -> p a b', a=4)
                mg = self.mgT[:, br * 8 + half * 4:br * 8 + half * 4 + 4, :]
                hs = slice(half * 4, half * 4 + 4)
                if br == 0:
                    self.tt(self.mrg[:, hs, :], p3, mg, ALU.mult, [p, self.mgT], [self.mrg])
                else:
                    self.tt(self.mtmp[:], p3, mg, ALU.mult, [p, self.mgT], [self.mtmp])
                    dst = self.mrg if br == 1 else self.mrgT
                    self.tt(dst[:, hs, :], self.mrg[:, hs, :], self.mtmp[:], ALU.add, [self.mrg, self.mtmp], [dst])
        if self.stop < 11:
            return
        for half in range(2):
            p = self.pa_next()
            for hf in range(2):
                w, wb = self.wget('w_o:o:%d' % (half * 2 + hf))
                for c in range(8):
                    self.mm(p[:, hf * 256:(hf + 1) * 256], self.mrgT[:, c, :], w[:, c, :], c == 0, c == 7, [self.mrgT, wb], [p])
            self.tt(self.x1[:, half * 512:(half + 1) * 512], p[:, :], xt[:, half * 512:(half + 1) * 512], ALU.add, [p, xt], [self.x1])
        if self.stop < 12:
            return
        self.norm_T(self.x1, 2, self.h2T)
        if self.stop < 13:
            return
        groups = [(j0, min(2, NFF - j0)) for j0 in range(0, NFF, 2)]
        for gi_, (j0, ng) in enumerate(groups):
            gx, ub = self.gx[gi_ % 2], self.ub[gi_ % 2]
            banks = []
            for jj in range(ng):
                w, wb = self.wget('w_up:u:%d' % (j0 + jj))
                if jj % 2 == 0:
                    p = self.pa_next()
                    banks.append(p)
                for cc in range(2):
                    o_ = ((jj % 2) * 2 + cc) * 128
                    for c in range(8):
                        self.mm(p[:, o_:o_ + 128], w[:, c, cc * 128:(cc + 1) * 128], self.h2T[:, c, :], c == 0, c == 7, [wb, self.h2T], [p])
            self.cp(gx[:, 0:ng, 0:2], self.hist[:, j0:j0 + ng, :], [self.hist], [gx], eng='pool')
            for bi, p in enumerate(banks):
                nb_ = min(2, ng - 2 * bi)
                p4 = p[:, 0:nb_ * 256].rearrange('p (a t b) -> p a t b', t=2, b=128)
                self.act(ub[:, 2 * bi:2 * bi + nb_, :], p4[:, :, 0, :], AF.Copy, [p], [ub])
                self.cp(gx[:, 2 * bi:2 * bi + nb_, 2:130], p4[:, :, 1, :], [p], [gx])
            self.cp(self.hist[:, j0:j0 + ng, :], gx[:, 0:ng, 128:130], [gx], [self.hist], eng='pool')
            if conv_tok0_out is not None:
                self.cp(self.g0col[:, j0:j0 + ng], gx[:, 0:ng, 2], [gx], [self.g0col], eng='pool')
            cw = lambda t: self.convw[:, j0:j0 + ng, t:t + 1].to_broadcast([128, ng, 128])
            a_, b_ = self.cva[:, 0:ng, :], self.cvb[:, 0:ng, :]
            self.tt(a_, gx[:, 0:ng, 0:128], cw(0), ALU.mult, [gx, self.gains], [self.cva])
            self.tt(b_, gx[:, 0:ng, 1:129], cw(1), ALU.mult, [gx, self.gains], [self.cvb], eng='pool')
            self.tt(a_, a_, b_, ALU.add, [self.cva, self.cvb], [self.cva])
            self.tt(b_, gx[:, 0:ng, 2:130], cw(2), ALU.mult, [gx, self.gains], [self.cvb], eng='pool')
            self.tt(a_, a_, b_, ALU.add, [self.cva, self.cvb], [self.cva])
            self.tt(a_, a_, self.convb[:, j0:j0 + ng].unsqueeze(2).to_broadcast([128, ng, 128]), ALU.add, [self.cva, self.gains], [self.cva])
            self.gelu_tanh(b_, a_, b_, [self.cva], [self.cvb], self.cvb)
            self.tt(self.actT[:, j0:j0 + ng, :], b_, ub[:, 0:ng, :], ALU.mult, [self.cvb, ub], [self.actT])
        if self.stop < 14:
            return
        with k.nc.allow_non_contiguous_dma(reason="conv state columns written transposed (<= 2 x 2816 elements)"):
            if conv_out is not None:
                for t_ in range(2):
                    k.dma('sp', conv_out[t_].rearrange('(j p) -> p j', p=128), self.hist[:, :, t_], reads=[self.hist])
            if conv_tok0_out is not None:
                k.dma('sp', conv_tok0_out.rearrange('t (j p) -> p (t j)', p=128), self.g0col[:], reads=[self.g0col])
        if self.stop < 15:
            return
        p0, p1 = self.pa_next(), self.pa_next()
        for j in range(11):
            w, wb = self.wget('w_down:d:%d' % j)
            for c in range(2):
                f = 2 * j + c
                self.mm(p0[:, :], self.actT[:, f, :], w[:, c, 0:512], f == 0, f == NFF - 1, [self.actT, wb], [p0])
                self.mm(p1[:, :], self.actT[:, f, :], w[:, c, 512:1024], f == 0, f == NFF - 1, [self.actT, wb], [p1])
        self.tt(self.yt[:, 0:512], p0[:, :], self.x1[:, 0:512], ALU.add, [p0, self.x1], [self.yt])
        self.tt(self.yt[:, 512:1024], p1[:, :], self.x1[:, 512:1024], ALU.add, [p1, self.x1], [self.yt])
        if y_out is not None:
            k.dma('sp', y_out[0], self.yt[y_out[1], :], reads=[self.yt])

    def q_tile(self, slot, xt, mset, S, Sb, rowmask_col, rows_out, win_out, y_out, conv_out, conv_tok0_out=None):
        mk_ = lambda nm: self.marks.append(('  %d:%s' % (slot, nm), dict(self.k.cnt))) if slot in (48, 63, 64) else None
        self.stage_a(slot, xt, True, True, rows_out, win_out)
        mk_('stage_a')
        if self.stop < 6:
            return
        self.own_rest_a()
        mk_('rest_a')
        if self.stop < 7:
            return
        self.nsa_tile(slot, mset, nq=(32 if slot == 64 else 128))
        mk_('nsa')
        if self.stop < 8:
            return
        self.gla_tile(S, Sb, rowmask_col)
        mk_('gla')
        if self.stop < 9:
            return
        self.xatt_tile()
        mk_('xatt')
        if self.stop < 10:
            return
        self.merge_ffn(xt, y_out, conv_out, conv_tok0_out)
        mk_('merge_ffn')

    def mem_to_resident(self, mt, memrows):
        p = self.pa_next()
        for h in range(4):
            self.tr(p[:, h * 128:(h + 1) * 128], memrows[:, h * 128:(h + 1) * 128], True, [memrows], [p])
        self.cp(self.k_memT[:, :, mt * 128:(mt + 1) * 128], p[:, :].rearrange('p (h m) -> p h m', h=4), [p], [self.k_memT])
        self.cp(self.v_mem[:, mt, :, 0:128], memrows[:, 512:1024].rearrange('p (h d) -> p h d', h=4), [memrows], [self.v_mem], eng='pool')

    def mem_kv_prompt(self):
        k = self.k
        self.sched_cols('w_mem_kv', 'm', 0, 1024)
        self.sched_cols('w_mem_kv', 'm2', 0, 1024)
        for mt in range(2):
            xt = self.xt[mt % len(self.xt)]
            self.load_x(xt, self.i['mem'][mt * 128:(mt + 1) * 128, :])
            self.norm_T(xt, 1, self.hT)
            tg = 'm' if mt == 0 else 'm2'
            for half in range(2):
                p = self.pa_next()
                for hf in range(2):
                    w, wb = self.wget('w_mem_kv:%s:%d' % (tg, half * 2 + hf))
                    for c in range(8):
                        self.mm(p[:, hf * 256:(hf + 1) * 256], self.hT[:, c, :], w[:, c, :], c == 0, c == 7, [self.hT, wb], [p])
                if half == 0:
                    sq = self.headnorm(p[:, :], 4, 128, [p])
                    self.tt(self.memrows[:, 0:512].rearrange('p (h d) -> p h d', h=4), sq[:, 0:512].rearrange('p (h d) -> p h d', h=4),
                            self.gxk[:].unsqueeze(1).to_broadcast([128, 4, 128]), ALU.mult, [sq, self.gains], [self.memrows])
                else:
                    self.act(self.memrows[:, 512:1024], p[:, :], AF.Copy, [p], [self.memrows])
            k.dma('sp', self.o['memkv_p'][mt * 128:(mt + 1) * 128, :], self.memrows[:], reads=[self.memrows])
            self.mem_to_resident(mt, self.memrows)

    def prompt_phase(self):
        k, i, o = self.k, self.i, self.o
        self.mem_kv_prompt()
        if self.stop < 3:
            return
        slots = list(range(FIRST_Q - self.n_prefix, FIRST_Q + self.n_own))
        self.p_slots = slots
        for s in slots:
            self.sched_tile(s >= FIRST_Q, s >= FIRST_Q - 4)
        if self.do_sample:
            for _ in range(self.n_samp):
                self.sched_tile(True)
        self.load_x(self.xt[0], i['xloc'][slots[0] * 128:(slots[0] + 1) * 128, :])
        self.marks = [('start', dict(self.k.cnt))]
        for n, s in enumerate(slots):
            self.marks.append(('slot%d' % s, dict(self.k.cnt)))
            xt = self.xt[n % len(self.xt)]
            if n + 1 < len(slots):
                s2 = slots[n + 1]
                nxt_x = (self.xt[(n + 1) % len(self.xt)], i['xloc'][s2 * 128:(s2 + 1) * 128, :])
                if len(self.xt) > 1:
                    self.load_x(*nxt_x)
            ncast = -(-len(self.late_casts) // max(1, (FIRST_Q - s))) if s < FIRST_Q else len(self.late_casts)
            for _ in range(ncast):
                nm_, dst_, src_ = self.late_casts.pop(0)
                k.dma('pool', dst_, src_, writes=[self.sbuf_w[nm_]])
            if s < FIRST_Q:
                gi = n % 4
                self.stage_a(s, xt, False, s >= FIRST_Q - 4, gi=gi, flush=(gi == 3 or slots[n + 1] >= FIRST_Q))
                if self.stop < 4:
                    continue
                self.la_compute(self.e0[:, 0:1])
                self.gla_state_update(self.S, self.Sb, False)
                if self.stop < 5:
                    return
            else:
                t = s - 48
                rows_out = (o['rows_p'][t * 128:(t + 1) * 128, :], slice(0, 128)) if t >= 0 else None
                win_out = (o['win_p'][(t - 12) * 128:(t - 11) * 128, :], slice(0, 128)) if t >= 12 else None
                y_out = (o['y_p'][t * 128:(t + 1) * 128, :], slice(0, 128)) if t >= 0 else None
                conv_out = o['conv_p'] if s == slots[-1] else None
                self.q_tile(s, xt, 0, self.S, self.Sb, self.e0[:, 0:1], rows_out, win_out, y_out, conv_out)
                pass
            if len(self.xt) == 1 and n + 1 < len(slots):
                self.load_x(*nxt_x)
            if s == FIRST_Q and self.stop >= 99:
                if True:
                    self.ts(self.hist[:], self.hist[:], self.hs[:, 0:1], None, ALU.mult, None, [self.hist, self.hs], [self.hist])
        k.dma('sp', o['gla_p'].rearrange('h d v -> d h v'), self.S[:], reads=[self.S])

    def sample_phase(self):
        k, i, o = self.k, self.i, self.o
        ns = self.n_samp
        xs_t = self.xt[0]
        k.dma('sp', self.ptb[:], i['page_table'].rearrange('s j -> (s j)').partition_broadcast(128), writes=[self.ptb])
        k.op('pool', lambda e: e.iota(self.iop[:], pattern=[[0, 1]], base=0, channel_multiplier=1), (), [self.iop])
        self.cp(self.e1[:, 0:1], self.iop[:], [self.iop], [self.e1])
        self.cp(self.la[:, 0:ns * 64], self.ptb[:], [self.ptb], [self.la])
        self.ts(self.la[:, 0:ns * 64], self.la[:, 0:ns * 64], 128.0, self.e1[:, 0:1], ALU.mult, ALU.add, [self.la, self.e1], [self.la])
        self.cp(self.idx[:], self.la[:, 0:ns * 64], [self.la], [self.idx])
        self.memset(xs_t[:], 0.0, [xs_t])
        for s in range(ns):
            self.marks.append(('fill%d' % s, dict(self.k.cnt)))
            def gather(j):
                pg = self.page[j % 2]
                col = s * 64 + j
                k.dma('pool', None, None, reads=[self.idx], writes=[pg],
                      fn=lambda e, pg=pg, col=col: e.indirect_dma_start(out=pg[:], out_offset=None, in_=i['cache_kv'],
                                                                        in_offset=bass.IndirectOffsetOnAxis(ap=self.idx[:, col:col + 1], axis=0)))
            gather(0)
            for j in range(64):
                pg = self.page[j % 2]
                if j + 1 < 64:
                    gather(j + 1)
                pgT = T(pg.t[:].rearrange('p (a b) -> p a b', a=4)); pgT.b = pg.b
                self.a2_rows(j, pgT, j % 4)
                if j % 4 == 3:
                    self.compress(j - 3, 4)
            for t in range(4):
                k.dma('sp', self.winrows[:].rearrange('p a b -> p (a b)'), i['cache_win'][s, t * 128:(t + 1) * 128, :], writes=[self.winrows])
                self.a2_win(60 + t, self.winrows)
            k.dma('sp', self.S[:], i['state_gla'][s].rearrange('h d v -> d h v'), writes=[self.S])
            self.cp(self.Sb[:], self.S[:], [self.S], [self.Sb])
            with k.nc.allow_non_contiguous_dma(reason="conv state loaded transposed (2 x 2816 elements)"):
                for t_ in range(2):
                    k.dma('sp', self.hist[:, :, t_], i['state_conv'][s, t_].rearrange('(j p) -> p j', p=128), writes=[self.hist])
            for mt in range(2):
                k.dma('sp', self.memrows[:], i['cache_mem'][s, mt * 128:(mt + 1) * 128, :], writes=[self.memrows])
                self.mem_to_resident(mt, self.memrows)
            k.dma('sp', xs_t[0:1, :], i['xs'][s:s + 1, :], writes=[xs_t])
            k.dma('sp', o['win_s'][s, 0:511, :], i['cache_win'][s, 1:512, :])
            k.dma('sp', o['conv_s'][s, 0:1, :], i['state_conv'][s, 1:2, :])
            self.marks.append(('sq%d' % s, dict(self.k.cnt)))
            self.q_tile(64, xs_t, 1, self.S, self.Sb, self.e0[:, 1:2],
                        (o['rows_s'][s:s + 1, :], slice(0, 1)), (o['win_s'][s, 511:512, :], slice(0, 1)),
                        (o['y_s'][s:s + 1, :], slice(0, 1)), None, conv_tok0_out=o['conv_s'][s, 1:2, :])
            k.dma('sp', o['gla_s'][s].rearrange('h d v -> d h v'), self.S[:], reads=[self.S])


def _t5_bucket_np(d):
    n = np.maximum(d, 0)
    nf = np.maximum(n, 1).astype(np.float32)
    large = 16 + (np.log(nf / np.float32(16)) / np.float32(np.log(128 / 16)) * np.float32(16)).astype(np.int32)
    return np.where(n < 16, n, np.minimum(large, 31))


def _constants():
    c = {}
    c['c_ident'] = np.eye(128, dtype=np.float32)
    s = np.arange(128)
    c['c_U'] = (s[:, None] <= s[None, :]).astype(np.float32)
    c['c_L'] = (s[:, None] > s[None, :]).astype(np.float32)
    c['c_onesbd'] = np.kron(np.eye(2), np.ones((64, 64))).astype(np.float32)
    c['c_edge'] = (NEG * (s[:, None] >= s[None, :])).astype(np.float32).astype(ml_dtypes.bfloat16)
    d = np.arange(DMIN, DMAX + 1)
    oh = np.zeros((33, ND), np.float32)
    valid = (d >= 0) & (d < 512)
    b = _t5_bucket_np(d)
    oh[b[valid], np.nonzero(valid)[0]] = 1.0
    oh[32, ~valid] = 1.0
    c['c_ohda'] = oh
    c['c_ohdr'] = np.ascontiguousarray(oh[:, ::-1])
    relb = np.zeros((128, 3), np.float32)
    relb[:64, 0] = 1e4
    relb[:, 1] = 1e4
    relb[64:, 2] = 1e4
    relb[:64, 2] = -1e30
    c['c_relb'] = relb
    e0 = np.zeros((128, 2), np.float32)
    e0[:, 0] = 1.0
    e0[0, 1] = 1.0
    c['c_e0'] = e0
    return c


def _masks(npre):
    sb = np.zeros((2, 128, NSLOT), np.float32)
    sb[0, :, :npre] = NEG
    pm = np.full((2, 128, 136), NEG, np.float32)
    pm[0, :, :2 * npre] += NEG
    cm = np.zeros((2, 1, 528), np.float32)
    cm[0, 0, :8 * npre + 1] = NEG
    cm[1, 0, 0] = NEG
    fb = np.zeros((2, 128, 136), np.float32)
    fb[0, :, 2 * npre] = 1e4
    fb[1, :, 0] = 1e4
    hs = np.full((128, 1), 0.0 if npre > FIRST_Q else 1.0, np.float32)
    return dict(m_slotbias=sb, m_pm=pm, m_cmask=cm, m_fbabs=fb, m_hs=hs)


_CACHE = {}


def _get_nc(key, **kw):
    if key not in _CACHE:
        _CACHE[key] = MK(**kw).nc
    return _CACHE[key]


def kernel(x_prompt, x_sample, cache_kv, cache_win, state_gla, state_conv, cache_mem, page_table, mem_prompt,
           g_mix, w_in, g_nsa_q, g_nsa_k, cmp_k_pe, cmp_k_w1, cmp_k_w2, cmp_v_pe, cmp_v_w1, cmp_v_w2,
           rel_bias, w_gla_gate, b_gla_gate, g_gla_o, g_mem, w_mem_kv, g_x_q, g_x_k,
           w_nsa_out, w_gla_out, w_x_out, w_o, g_ffn, w_up, conv_w, conv_b, w_down):
    f = lambda a: np.ascontiguousarray(np.asarray(a, dtype=np.float32))
    x_prompt = f(x_prompt); x_sample = f(x_sample).reshape(32, D)
    cache_kv2 = f(cache_kv).reshape(-1, 512)
    cache_win2 = f(cache_win).reshape(32, 512, 256)
    state_gla = f(state_gla); state_conv = f(state_conv)
    cache_mem2 = f(cache_mem).reshape(32, 256, 1024)
    page_table = np.ascontiguousarray(np.asarray(page_table, dtype=np.int32))
    mem_prompt = f(mem_prompt)
    w_up_p = f(w_up).reshape(D, 2, NFF, 128).transpose(0, 2, 1, 3).reshape(D, 2 * DFF)
    shared = dict(
        g_mix=f(g_mix), g_mem=f(g_mem), g_ffn=f(g_ffn), w_in=np.ascontiguousarray(f(w_in)[:, W_IN_PERM]),
        g_nsa_q=f(g_nsa_q), g_nsa_k=f(g_nsa_k), cmp_k_pe=f(cmp_k_pe), cmp_k_w1=f(cmp_k_w1), cmp_k_w2=f(cmp_k_w2),
        cmp_v_pe=f(cmp_v_pe), cmp_v_w1=f(cmp_v_w1), cmp_v_w2=f(cmp_v_w2), rel_bias=f(rel_bias),
        w_gla_gate=f(w_gla_gate), b_gla_gate=f(b_gla_gate).reshape(1, 256), g_gla_o=f(g_gla_o), g_x_q=f(g_x_q), g_x_k=f(g_x_k),
        w_mem_kv=f(w_mem_kv), w_nsa_out=f(w_nsa_out), w_gla_out=f(w_gla_out), w_x_out=f(w_x_out), w_o=f(w_o),
        w_up=np.ascontiguousarray(w_up_p), conv_w=f(conv_w), conv_b=f(conv_b).reshape(1, DFF), w_down=f(w_down),
        cache_kv=cache_kv2)
    shared.update(_constants())
    in_maps = []
    for core in range(8):
        b, c = core // 4, core % 4
        npre = 48 - 16 * c
        xloc = np.zeros((64 * 128, D), np.float32)
        xloc[npre * 128:] = x_prompt[b, :(64 - npre) * 128] if False else x_prompt[b, (16 * (c + 1) - (64 - npre)) * 128:16 * (c + 1) * 128]
        m = dict(shared)
        m.update(_masks(npre))
        sl = slice(4 * core, 4 * core + 4)
        m.update(xloc=xloc, xs=x_sample[sl], cache_win=cache_win2[sl], state_gla=state_gla[sl], state_conv=state_conv[sl],
                 cache_mem=cache_mem2[sl], page_table=page_table[sl], mem=mem_prompt[b])
        in_maps.append(m)
    nc = _get_nc('full')
    res = run_bass_kernel_spmd(nc, in_maps, core_ids=list(range(8))).results
    r = lambda core, nm: np.asarray(res[core][nm], dtype=np.float32)
    y_p = np.stack([np.concatenate([r(4 * b + c, 'y_p') for c in range(4)], 0) for b in range(2)])
    rows_p = np.stack([np.concatenate([r(4 * b + c, 'rows_p') for c in range(4)], 0) for b in range(2)]).reshape(2, 8192, 4, 2, 64)
    win_p = np.stack([r(4 * b + 3, 'win_p') for b in range(2)]).reshape(2, 512, 2, 2, 64)
    gla_p = np.stack([r(4 * b + 3, 'gla_p') for b in range(2)])
    conv_p = np.stack([r(4 * b + 3, 'conv_p') for b in range(2)])
    memkv_p = np.stack([r(4 * b, 'memkv_p') for b in range(2)]).reshape(2, 256, 2, 4, 128)
    cat = lambda nm: np.concatenate([r(core, nm) for core in range(8)], 0)
    y_s = cat('y_s').reshape(32, 1, D)
    rows_s = cat('rows_s').reshape(32, 1, 4, 2, 64)
    win_s = cat('win_s').reshape(32, 512, 2, 2, 64)
    gla_s = cat('gla_s')
    conv_s = cat('conv_s')
    return (y_p, y_s, rows_p, win_p, gla_p, conv_p, memkv_p, rows_s, win_s, gla_s, conv_s)
```

```python
import numpy as np
import ml_dtypes
from contextlib import ExitStack
import concourse.bass as bass
import concourse.mybir as mybir
from concourse.bass_utils import run_bass_kernel_spmd

F32 = mybir.dt.float32
BF16 = mybir.dt.bfloat16
I32 = mybir.dt.int32
AF = mybir.ActivationFunctionType
ALU = mybir.AluOpType
AX = mybir.AxisListType

D = 1024
DFF = 2816
NFF = 22
EPS = 1e-6
TINY = 1e-30
NEG = -30000.0
NSLOT = 65
DMIN = -160
DMAX = 700
ND = DMAX - DMIN + 1
C0, C1, C2, C3, C4, C5, C6, CMG = 0, 512, 1024, 1320, 1832, 2344, 2856, 3368
W_IN_PERM = np.concatenate([np.arange(0, 512), np.arange(512, 1024), np.arange(1024, 1280),
                            np.arange(2328, 2344), np.arange(1280, 1304), np.arange(1304, 1816),
                            np.arange(1816, 2328), np.arange(2344, 2856), np.arange(2856, 3368),
                            np.arange(3368, 6440)])
FIRST_Q = 47
PREFIX_WIN_FROM = 43


class Buf:
    __slots__ = ('w', 'r', 'excl')

    def __init__(self):
        self.w = None
        self.r = {}
        self.excl = False


class T:
    def __init__(self, t):
        self.t = t
        self.b = Buf()

    def __getitem__(self, k):
        return self.t[k]


class KB:
    def __init__(self, nds=24):
        self.nc = bass.Bass('TRN2', target_bir_lowering=False)
        nc = self.nc
        self.es = ExitStack()
        self.eng = dict(pe=nc.tensor, act=nc.scalar, dve=nc.vector, pool=nc.gpsimd, sp=nc.sync)
        self.sem = {e: self.es.enter_context(nc.semaphore('s_' + e)) for e in ('pe', 'act', 'dve', 'pool')}
        self.cnt = dict.fromkeys(self.sem, 0)
        self.NDS = nds
        self.dsem = [self.es.enter_context(nc.semaphore('d%d' % i)) for i in range(nds)]
        self.dcnt = [0] * nds
        self.dnext = 0
        self.dnext_sw = 0
        self.waited = {e: {} for e in self.eng}
        self.nbuf = 0
        self.ninstr = 0

    def sb(self, shape, dt=F32):
        self.nbuf += 1
        return T(self.es.enter_context(self.nc.sbuf_tensor('sb%d' % self.nbuf, list(shape), dt)))

    def ps(self, shape, dt=F32):
        self.nbuf += 1
        t = T(self.es.enter_context(self.nc.psum_tensor('ps%d' % self.nbuf, list(shape), dt)))
        t.b.excl = True
        return t

    def din(self, name, shape, dt=F32):
        return self.nc.dram_tensor(name, list(shape), dt, kind="ExternalInput").ap()

    def dout(self, name, shape, dt=F32):
        return self.nc.dram_tensor(name, list(shape), dt, kind="ExternalOutput").ap()

    def dint(self, name, shape, dt=F32):
        return self.nc.dram_tensor(name, list(shape), dt, kind="Internal").ap()

    def _wait(self, e, dep):
        k, v = dep
        w = self.waited[e]
        if w.get(k, 0) >= v:
            return
        w[k] = v
        sem = self.sem[k] if isinstance(k, str) else self.dsem[k]
        self.eng[e].wait_ge(sem, v)
        self.ninstr += 1

    def _deps(self, e, reads, writes):
        deps = set()
        for b in reads:
            if b.w is not None:
                deps.add(b.w)
        for b in writes:
            if b.w is not None:
                deps.add(b.w)
            deps.update(b.r.values())
        for d in deps:
            if e == 'pe' and d[0] == 'pe':
                continue
            self._wait(e, d)

    def _mark(self, me, key, reads, writes):
        for b in reads:
            b.r[key] = me
        for b in writes:
            b.w = me
            b.r = {}

    def op(self, e, fn, reads=(), writes=()):
        reads = [x.b if isinstance(x, T) else x for x in reads]
        writes = [x.b if isinstance(x, T) else x for x in writes]
        writes = writes + [b for b in reads if b.excl and b not in writes]
        self._deps(e, reads, writes)
        ins = fn(self.eng[e])
        self.cnt[e] += 1
        self.ninstr += 1
        ins.then_inc(self.sem[e], 1)
        me = (e, self.cnt[e])
        self._mark(me, e, reads, writes)
        return me

    def dma(self, q, out, in_, reads=(), writes=(), fn=None, **kw):
        reads = [x.b if isinstance(x, T) else x for x in reads]
        writes = [x.b if isinstance(x, T) else x for x in writes]
        if q == 'pool':
            i = self.NDS - 8 + self.dnext_sw
            self.dnext_sw = (self.dnext_sw + 1) % 8
        else:
            i = self.dnext
            self.dnext = (i + 1) % (self.NDS - 8)
        if self.dcnt[i] > 0:
            self._wait(q, (i, self.dcnt[i]))
        self._deps(q, reads, writes)
        if fn is None:
            ins = self.eng[q].dma_start(out=out, in_=in_, **kw)
        else:
            ins = fn(self.eng[q])
        self.dcnt[i] += 16
        self.ninstr += 1
        ins.then_inc(self.dsem[i], 16)
        me = (i, self.dcnt[i])
        self._mark(me, i, reads, writes)
        return me

    def finish(self):
        for i in range(self.NDS):
            if self.dcnt[i] > 0:
                self._wait('sp', (i, self.dcnt[i]))
        for e in self.sem:
            if self.cnt[e] > 0:
                self._wait('sp', (e, self.cnt[e]))
        self.es.close()
        return self.nc


class MK:
    def __init__(self, n_prefix=47, n_own=17, n_samp=4, do_sample=True):
        self.k = k = KB()
        self.n_prefix, self.n_own, self.n_samp, self.do_sample = n_prefix, n_own, n_samp, do_sample
        import os
        self.stop = int(os.environ.get('MK_STOP', '99'))
        self.declare_io()
        self.alloc()
        self.setup()
        if self.stop >= 2:
            self.prompt_phase()
        if do_sample and self.stop >= 90:
            self.sample_phase()
        self.nc = k.finish()

    def mm(self, out, lhsT, rhs, start, stop, reads, writes):
        self.k.op('pe', lambda e: e.matmul(out, lhsT=lhsT, rhs=rhs, start=start, stop=stop, skip_group_check=True), reads, writes)

    def tr(self, out, in_, f32, reads, writes):
        idt = self.ident_f if f32 else self.ident_b
        self.k.op('pe', lambda e: e.transpose(out=out, in_=in_, identity=idt.t[:]), list(reads) + [idt], writes)

    def act(self, out, in_, func, reads, writes, **kw):
        self.k.op('act', lambda e: e.activation(out=out, in_=in_, func=func, **kw), reads, writes)

    def tt(self, out, in0, in1, op, reads, writes, eng='dve'):
        self.k.op(eng, lambda e: e.tensor_tensor(out=out, in0=in0, in1=in1, op=op), reads, writes)

    def ts(self, out, in0, s1, s2, op0, op1, reads, writes, eng='dve'):
        if op1 is None:
            self.k.op(eng, lambda e: e.tensor_scalar(out=out, in0=in0, scalar1=s1, scalar2=None, op0=op0), reads, writes)
        else:
            self.k.op(eng, lambda e: e.tensor_scalar(out=out, in0=in0, scalar1=s1, scalar2=s2, op0=op0, op1=op1), reads, writes)

    def stt(self, out, in0, scalar, in1, op0, op1, reads, writes):
        self.k.op('dve', lambda e: e.scalar_tensor_tensor(out=out, in0=in0, scalar=scalar, in1=in1, op0=op0, op1=op1), reads, writes)

    def cp(self, out, in_, reads, writes, eng='dve'):
        self.k.op(eng, lambda e: e.tensor_copy(out=out, in_=in_), reads, writes)

    def recip(self, ap, t):
        self.k.op('dve', lambda e: e.reciprocal(out=ap, in_=ap), [t], [t])

    def memset(self, ap, val, writes, eng='pool'):
        self.k.op(eng, lambda e: e.memset(ap, val), (), writes)

    def pa_next(self):
        self._pai = (self._pai + 1) % len(self.pa)
        return self.pa[self._pai]

    def pb_next(self):
        self._pbi = (self._pbi + 1) % len(self.pb)
        return self.pb[self._pbi]

    def rstd_from_ss(self, ss_ap, n, cnt, reads_t, out_t):
        self.act(out_t[:, 0:n], ss_ap, AF.Sqrt, list(reads_t) + [self.eps_t], [out_t], scale=1.0 / cnt, bias=self.eps_t[:, 0:1])
        self.recip(out_t[:, 0:n], out_t)

    def headnorm(self, src, H, Dh, reads):
        sq, ss, rs = self.hn_sq, self.hn_ss, self.hn_rs
        n = H * Dh
        self.act(sq[:, 0:n], src, AF.Square, reads, [sq])
        self.k.op('dve', lambda e: e.tensor_reduce(out=ss[:, 0:H], in_=sq[:, 0:n].rearrange('p (h d) -> p h d', h=H), axis=AX.X, op=ALU.add), [sq], [ss])
        self.rstd_from_ss(ss[:, 0:H], H, Dh, [ss], rs)
        self.tt(sq[:, 0:n].rearrange('p (h d) -> p h d', h=H), src.rearrange('p (h d) -> p h d', h=H),
                rs[:, 0:H].unsqueeze(2).to_broadcast([128, H, Dh]), ALU.mult, list(reads) + [rs], [sq])
        return sq

    def gelu_tanh(self, out, x, tmp, reads, writes, tmp_t):
        self.tt(tmp, x, x, ALU.mult, reads, [tmp_t])
        self.ts(tmp, tmp, 0.044715, 1.0, ALU.mult, ALU.add, [tmp_t], [tmp_t])
        self.tt(tmp, tmp, x, ALU.mult, list(reads) + [tmp_t], [tmp_t])
        self.act(tmp, tmp, AF.Sigmoid, [tmp_t], [tmp_t], scale=1.5957691216057308)
        self.tt(out, tmp, x, ALU.mult, list(reads) + [tmp_t], writes)

    def declare_io(self):
        k = self.k
        ns_ = self.n_samp
        i = {}
        i['xloc'] = k.din('xloc', [64 * 128, D])
        i['xs'] = k.din('xs', [ns_, D])
        i['cache_kv'] = k.din('cache_kv', [(2560 if self.do_sample else 2) * 128, 512])
        i['cache_win'] = k.din('cache_win', [ns_, 512, 256])
        i['state_gla'] = k.din('state_gla', [ns_, 4, 64, 128])
        i['state_conv'] = k.din('state_conv', [ns_, 2, DFF])
        i['cache_mem'] = k.din('cache_mem', [ns_, 256, 1024])
        i['page_table'] = k.din('page_table', [ns_, 64], I32)
        i['mem'] = k.din('mem', [256, D])
        i['g_mix'] = k.din('g_mix', [D]); i['g_mem'] = k.din('g_mem', [D]); i['g_ffn'] = k.din('g_ffn', [D])
        i['w_in'] = k.din('w_in', [D, 6440])
        i['g_nsa_q'] = k.din('g_nsa_q', [64]); i['g_nsa_k'] = k.din('g_nsa_k', [3, 64])
        for nm in ('k', 'v'):
            i['cmp_%s_pe' % nm] = k.din('cmp_%s_pe' % nm, [32, 64])
            i['cmp_%s_w1' % nm] = k.din('cmp_%s_w1' % nm, [32, 64, 64])
            i['cmp_%s_w2' % nm] = k.din('cmp_%s_w2' % nm, [64, 64])
        i['rel_bias'] = k.din('rel_bias', [32, 8])
        i['w_gla_gate'] = k.din('w_gla_gate', [16, 256]); i['b_gla_gate'] = k.din('b_gla_gate', [1, 256])
        i['g_gla_o'] = k.din('g_gla_o', [128]); i['g_x_q'] = k.din('g_x_q', [128]); i['g_x_k'] = k.din('g_x_k', [128])
        i['w_mem_kv'] = k.din('w_mem_kv', [D, 1024])
        i['w_nsa_out'] = k.din('w_nsa_out', [512, D]); i['w_gla_out'] = k.din('w_gla_out', [512, D]); i['w_x_out'] = k.din('w_x_out', [512, D])
        i['w_o'] = k.din('w_o', [D, D])
        i['w_up'] = k.din('w_up', [D, 2 * DFF])
        i['conv_w'] = k.din('conv_w', [3, DFF]); i['conv_b'] = k.din('conv_b', [1, DFF])
        i['w_down'] = k.din('w_down', [DFF, D])
        i['c_ident'] = k.din('c_ident', [128, 128]); i['c_U'] = k.din('c_U', [128, 128]); i['c_L'] = k.din('c_L', [128, 128])
        i['c_onesbd'] = k.din('c_onesbd', [128, 128])
        i['c_ohdr'] = k.din('c_ohdr', [33, ND]); i['c_ohda'] = k.din('c_ohda', [33, ND])
        i['c_relb'] = k.din('c_relb', [128, 3]); i['c_edge'] = k.din('c_edge', [128, 128], BF16)
        i['c_e0'] = k.din('c_e0', [128, 2])
        i['m_slotbias'] = k.din('m_slotbias', [2, 128, NSLOT])
        i['m_pm'] = k.din('m_pm', [2, 128, 136])
        i['m_cmask'] = k.din('m_cmask', [2, 1, 528])
        i['m_fbabs'] = k.din('m_fbabs', [2, 128, 136]); i['m_hs'] = k.din('m_hs', [128, 1])
        self.i = i
        o = {}
        o['y_p'] = k.dout('y_p', [16 * 128, D])
        o['rows_p'] = k.dout('rows_p', [16 * 128, 512])
        o['win_p'] = k.dout('win_p', [512, 256])
        o['gla_p'] = k.dout('gla_p', [4, 64, 128])
        o['conv_p'] = k.dout('conv_p', [2, DFF])
        o['memkv_p'] = k.dout('memkv_p', [256, 1024])
        o['y_s'] = k.dout('y_s', [ns_, D])
        o['rows_s'] = k.dout('rows_s', [ns_, 512])
        o['win_s'] = k.dout('win_s', [ns_, 512, 256])
        o['gla_s'] = k.dout('gla_s', [ns_, 4, 64, 128])
        o['conv_s'] = k.dout('conv_s', [ns_, 2, DFF])
        self.o = o
        s = {}
        s['w_in'] = k.dint('s_w_in', [128, 8, 6440], BF16)
        s['w_mem_kv'] = k.dint('s_w_mem_kv', [128, 8, 1024], BF16)
        s['w_nsa_out'] = k.dint('s_w_nsa_out', [128, 4, D], BF16)
        s['w_gla_out'] = k.dint('s_w_gla_out', [128, 4, D], BF16)
        s['w_x_out'] = k.dint('s_w_x_out', [128, 4, D], BF16)
        s['w_o'] = k.dint('s_w_o', [128, 8, D], BF16)
        s['w_up'] = k.dint('s_w_up', [128, 8, 2 * DFF], BF16)
        s['w_down'] = k.dint('s_w_down', [128, NFF, D], BF16)
        self.s = s
        self.sbuf_w = {nm: Buf() for nm in s}

    def alloc(self):
        k = self.k
        sb, ps = k.sb, k.ps
        self.pa = [ps([128, 512]) for _ in range(4)]
        self.pacc = [ps([128, 512]) for _ in range(2)]
        self.pb = [ps([128, 1024], BF16) for _ in range(2)]
        self._pai = self._pbi = 0
        self.ident_f = sb([128, 128]); self.ident_b = sb([128, 128], BF16)
        self.U = sb([128, 128]); self.L = sb([128, 128]); self.onesbd = sb([128, 128])
        self.I4 = sb([128, 4, 128], BF16)
        self.rbp = sb([33, 8]); self.rb31 = sb([32, 8])
        self.relb = sb([128, 3]); self.e0 = sb([128, 2])
        self.ones_row = sb([1, 128], BF16)
        self.ones_col = sb([128, 1])
        self.eps_t = sb([128, 1])
        self.slotbias = [sb([128, NSLOT]) for _ in range(2)]
        self.pm = [sb([128, 136]) for _ in range(2)]
        self.cmask = [sb([1, 528], BF16) for _ in range(2)]
        self.cmask_f = sb([1, 528])
        self.fbabs = [sb([128, 136]) for _ in range(2)]; self.hs = sb([128, 1])
        self.BT = sb([128, 2, 2, 512])
        self.BT512T = sb([128, 128], BF16)
        self.BC = sb([128, 8, 16])
        self.gains = Buf()
        self.gcol = sb([128, 3, 8])
        self.gq = sb([128, 64]); self.gk = sb([128, 3, 64]); self.ggo = sb([128, 128]); self.gxq = sb([128, 128]); self.gxk = sb([128, 128])
        self.gk0col = sb([128, 1])
        self.convw = sb([128, NFF, 3]); self.convb = sb([128, NFF])
        self.w1bd = [sb([128, 32, 128], BF16) for _ in range(2)]
        self.w2bd = [sb([128, 128], BF16) for _ in range(2)]
        self.b1 = [sb([128, 1]) for _ in range(2)]
        self.peT = sb([128, 2, 32], BF16)
        self.wg17 = sb([17, 256], BF16)
        self.stage = sb([128, 64])
        self.k_selT = sb([128, NSLOT * 128], BF16); self.b_ksel = [Buf() for _ in range(NSLOT)]
        self.v_sel = sb([128, NSLOT, 2, 65], BF16); self.b_vsel = [Buf() for _ in range(NSLOT)]
        self.k_winT = sb([128, 8 * 128], BF16); self.v_win = sb([128, 8, 2, 65], BF16); self.b_win = [Buf() for _ in range(8)]
        self.k_cmpT = sb([128, 528], BF16); self.b_kcmp = Buf()
        self.hgvT = sb([128, 640], BF16)
        self.v_cmp = sb([128, 5, 2, 64], BF16); self.b_vcmp = Buf()
        self.rawT = [sb([128, 16 + 512], BF16) for _ in range(2)]
        self.k_memT = sb([128, 4, 256], BF16); self.v_mem = sb([128, 2, 4, 129], BF16)
        self.S = sb([64, 4, 128]); self.Sb = sb([64, 4, 128], BF16)
        self.xt = [sb([128, D]) for _ in range(1)]
        self.big = sb([128, 2048])
        def view(ap):
            t = T(ap); t.b = self.big.b
            return t
        self.pc = view(self.big.t[:, :].rearrange('p (a b) -> p a b', a=4))
        self.x1 = view(self.big.t[:, 0:1024])
        self.hn_sq = view(self.big.t[:, 1024:2048])
        self.yt = self.hn_sq; self.memrows = self.hn_sq
        self.mtmp = view(self.big.t[:, 1536:2048].rearrange('p (a b) -> p a b', a=4))
        self.BTs = view(self.big.t[:, 0:512].rearrange('p (k o n) -> p k o n', k=2, o=2))
        self.qTs = sb([128, 128], BF16); self.I4s = sb([128, 128], BF16)
        self.hn_ss = sb([128, 8]); self.hn_rs = sb([128, 8])
        self.hb = sb([128, D], BF16)
        self.hT = sb([128, 8, 128], BF16)
        self.rows = sb([128, 4, 128]); self.winrows = sb([128, 2, 128])
        self.qn = sb([128, 4, 2, 64], BF16); self.qT = sb([128, 4, 128], BF16)
        self.gates = sb([128, 24])
        self.lrn = sb([128, 16], BF16); self.lrT = sb([17, 128], BF16)
        self.gqk = sb([128, 512], BF16); self.gqkT = sb([64, 8, 128], BF16)
        self.gv = sb([128, 512], BF16); self.gr = sb([128, 512], BF16)
        self.xqn = sb([128, 512], BF16); self.xqT = sb([128, 4, 128], BF16)
        self.mgT = sb([128, 24, 128], BF16)
        self.la = sb([128, 256]); self.e1 = sb([128, 256])
        self.ecb = sb([64, 4, 128]); self.encb = sb([64, 4, 128]); self.ecbl = sb([64, 4])
        self.qeT = sb([64, 4, 128], BF16); self.keT = sb([64, 4, 128], BF16)
        self.erc = sb([128, 256]); self.kl = sb([128, 256], BF16)
        self.Am = sb([128, 4, 128], BF16)
        self.ogla = sb([128, 512], BF16); self.oglaT = sb([128, 4, 128], BF16)
        self.psum8 = sb([128, 8]); self.prc8 = sb([128, 8])
        self.pg = sb([128, 2, 528]); self.imp = sb([128, 2, 136]); self.m16 = sb([128, 2, 16]); self.wk = sb([128, 136])
        self.neg = sb([128, 2, 136], BF16)
        self.negx = [[sb([128, 512], BF16) for _ in range(2)] for _ in range(2)]
        self.pcT = sb([128, 4, 128], BF16)
        self.pT = [sb([128, 512], BF16) for _ in range(4)]; self._pTi = 0
        self.oc = sb([128, 512]); self.osel = sb([128, 8, 64], BF16); self.owin = sb([128, 8, 64], BF16); self.den = sb([128, 8])
        self.onsa = sb([128, 512], BF16); self.onsaT = sb([128, 4, 128], BF16)
        self.pxT = sb([128, 2, 4, 128], BF16); self.ox = sb([128, 512], BF16); self.oxT = sb([128, 4, 128], BF16)
        self.mrg = view(self.big.t[:, 0:1024].rearrange('p (a b) -> p a b', a=8)); self.mrgT = sb([128, 8, 128], BF16)
        self.h2T = self.hT
        self.hist = sb([128, NFF, 2]); self.gx = [sb([128, 2, 130]) for _ in range(2)]; self.ub = [sb([128, 2, 128], BF16) for _ in range(2)]
        self.cva = sb([128, 2, 128]); self.cvb = sb([128, 2, 128]); self.cvc = sb([128, 2, 128]); self.g0col = sb([128, NFF])
        self.actT = sb([128, NFF, 128], BF16)
        self.hg = sb([128, 32], BF16); self.hx = sb([128, 32]); self.hx2 = sb([128, 32]); self.hxs = [self.hx, sb([128, 32])]; self.hx2s = [self.hx2, sb([128, 32])]; self.kc = sb([128, 32]); self.kc2 = sb([128, 32]); self.kcr = sb([128, 32])
        self.idx = sb([128, self.n_samp * 64], I32); self.ptb = sb([128, self.n_samp * 64], I32); self.iop = sb([128, 1], I32)
        self.page = [sb([128, 512]) for _ in range(2)]
        pcf = self.big.t[:, :]
        self.ohdr = T(pcf[0:33, 0:ND]); self.ohdr.b = self.pc.b
        self.ohda = T(pcf[0:33, ND:2 * ND]); self.ohda.b = self.pc.b
        self.NW = 4
        self.WAHEAD = 2
        self.wring = [sb([128, 2048], BF16) for _ in range(self.NW)]
        self.wq = []
        self.wissued = 0
        self.wnext = 0

    def wsched(self, tag, src, shape):
        self.wq.append((tag, src, shape))

    def _wview(self, r, shape):
        n = int(np.prod(shape[1:]))
        return r.t[:, 0:n].rearrange('p (a b) -> p a b', a=shape[1])

    def wget(self, tag):
        k = self.k
        i = self.wnext
        while self.stop < 99 and self.wq[i][0] != tag:
            self.wnext += 1; i = self.wnext; self.wissued = max(self.wissued, i)
        assert self.wq[i][0] == tag, (self.wq[i][0], tag)
        upto = min(len(self.wq), i + self.WAHEAD + 1)
        while self.wissued < upto:
            j = self.wissued
            tg, src, shape = self.wq[j]
            r = self.wring[j % self.NW]
            k.dma('sp', self._wview(r, shape), src, reads=[self.sbuf_w[tg.split(':')[0]]], writes=[r])
            self.wissued += 1
        self.wnext += 1
        r = self.wring[i % self.NW]
        return self._wview(r, self.wq[i][2]), r

    def sched_cols(self, nm, tag, c0, c1, nk=8):
        j = 0
        for a in range(c0, c1, 256):
            b = min(c1, a + 256)
            self.wsched('%s:%s:%d' % (nm, tag, j), self.s[nm][:, :, a:b], [128, nk, b - a])
            j += 1

    def sched_tile(self, own, need_win=True):
        if own:
            self.sched_cols('w_in', 'c0', C0, C1)
        self.sched_cols('w_in', 'c1', C1, C2)
        if own or need_win:
            self.sched_cols('w_in', 'c2', C2, C3)
        else:
            self.sched_cols('w_in', 'c2lr', C2 + 256, C2 + 272)
        if own:
            self.sched_cols('w_in', 'c3', C3, C4)
        else:
            self.sched_cols('w_in', 'c3k', C3 + 256, C4)
        self.sched_cols('w_in', 'c4', C4, C5)
        if own:
            self.sched_cols('w_in', 'c5', C5, C6)
            self.sched_cols('w_in', 'c6', C6, CMG)
            self.sched_cols('w_in', 'mg', CMG, 6440)
            for nm in ('w_nsa_out', 'w_gla_out', 'w_x_out'):
                self.sched_cols(nm, 'o', 0, D, nk=4)
            self.sched_cols('w_o', 'o', 0, D)
            self.sched_cols('w_up', 'u', 0, 2 * DFF)
            for j in range(11):
                self.wsched('w_down:d:%d' % j, self.s['w_down'][:, 2 * j:2 * j + 2, :], [128, 2, 1024])

    def setup(self):
        k, i, s = self.k, self.i, self.s

        def castw(nm, src3, nk, ncols):
            for c in range(nk):
                for c0 in range(0, ncols, 2048):
                    c1 = min(ncols, c0 + 2048)
                    k.dma('pool', s[nm][:, c, c0:c1], src3[:, c, c0:c1], writes=[self.sbuf_w[nm]])
        castw('w_mem_kv', i['w_mem_kv'].rearrange('(c p) n -> p c n', p=128), 8, 1024)
        w_in3 = i['w_in'].rearrange('(c p) n -> p c n', p=128)
        for c in range(8):
            k.dma('pool', s['w_in'][:, c, C1:C5], w_in3[:, c, C1:C5], writes=[self.sbuf_w['w_in']])
        self.late_casts = []
        for c in range(8):
            for (a_, b_) in ((C0, C1), (C5, C5 + 2048), (C5 + 2048, 6440)):
                self.late_casts.append(('w_in', s['w_in'][:, c, a_:b_], w_in3[:, c, a_:b_]))
        def castw_late(nm, src3, nk, ncols):
            for c in range(nk):
                for c0 in range(0, ncols, 2048):
                    c1 = min(ncols, c0 + 2048)
                    self.late_casts.append((nm, s[nm][:, c, c0:c1], src3[:, c, c0:c1]))
        for nm in ('w_nsa_out', 'w_gla_out', 'w_x_out'):
            castw_late(nm, i[nm].rearrange('(c p) n -> p c n', p=128), 4, D)
        castw_late('w_o', i['w_o'].rearrange('(c p) n -> p c n', p=128), 8, D)
        castw_late('w_up', i['w_up'].rearrange('(c p) n -> p c n', p=128), 8, 2 * DFF)
        castw_late('w_down', i['w_down'].rearrange('(c p) n -> p c n', p=128), NFF, D)

        def ld(t, src):
            k.dma('sp', t[:], src, writes=[t])
        ld(self.ident_f, i['c_ident']); ld(self.U, i['c_U']); ld(self.L, i['c_L']); ld(self.onesbd, i['c_onesbd'])
        ld(self.BT512T, i['c_edge'])
        ld(self.ohdr, i['c_ohdr']); ld(self.ohda, i['c_ohda']); ld(self.relb, i['c_relb']); ld(self.e0, i['c_e0'])
        for st in range(2):
            ld(self.slotbias[st], i['m_slotbias'][st]); ld(self.pm[st], i['m_pm'][st]); ld(self.fbabs[st], i['m_fbabs'][st])
            k.dma('sp', self.cmask_f[:], i['m_cmask'][st], writes=[self.cmask_f])
            self.cp(self.cmask[st][:], self.cmask_f[:], [self.cmask_f], [self.cmask[st]])
        ld(self.hs, i['m_hs'])
        self.cp(self.ident_b[:], self.ident_f[:], [self.ident_f], [self.ident_b])
        for g in range(4):
            self.cp(self.I4[:, g, :], self.ident_f[:], [self.ident_f], [self.I4])
        self.memset(self.ones_row[:], 1.0, [self.ones_row])
        self.memset(self.ones_col[:], 1.0, [self.ones_col])
        self.memset(self.eps_t[:], EPS, [self.eps_t])
        G = self.gains
        with k.nc.allow_non_contiguous_dma(reason="small one-time transposed parameter loads"):
            for j, nm in enumerate(('g_mix', 'g_mem', 'g_ffn')):
                k.dma('sp', self.gcol[:, j, :], i[nm].rearrange('(c p) -> p c', p=128), writes=[G])
            for t_ in range(3):
                k.dma('sp', self.convw[:, :, t_], i['conv_w'][t_].rearrange('(j p) -> p j', p=128), writes=[G])
            k.dma('sp', self.convb[:], i['conv_b'].rearrange('o (j p) -> p (o j)', p=128), writes=[G])
            for a, nm in enumerate(('k', 'v')):
                for kv in range(2):
                    k.dma('sp', self.stage[kv * 64:(kv + 1) * 64, a * 32:(a + 1) * 32], i['cmp_%s_pe' % nm].rearrange('j d -> d j'), writes=[self.stage])
            for kv in range(2):
                k.dma('sp', self.gk0col[kv * 64:(kv + 1) * 64, :], i['g_nsa_k'][0].rearrange('(d o) -> d o', o=1), writes=[G])
        for t, nm in ((self.gq, 'g_nsa_q'), (self.ggo, 'g_gla_o'), (self.gxq, 'g_x_q'), (self.gxk, 'g_x_k')):
            k.dma('sp', t[:], i[nm].partition_broadcast(128), writes=[G])
        for j in range(3):
            k.dma('sp', self.gk[:, j, :], i['g_nsa_k'][j].partition_broadcast(128), writes=[G])
        self.ts(self.gq[:], self.gq[:], 0.125, None, ALU.mult, None, [G], [G])
        self.ts(self.gxq[:], self.gxq[:], 128.0 ** -0.5, None, ALU.mult, None, [G], [G])
        self.cp(self.peT[:].rearrange('p a j -> p (a j)'), self.stage[:, 0:64], [self.stage], [self.peT])
        k.dma('sp', self.rbp[0:32, :], i['rel_bias'], writes=[self.rbp])
        k.dma('sp', self.rb31[:], i['rel_bias'][31].partition_broadcast(32), writes=[self.rb31])
        self.tt(self.rbp[0:32, :], self.rbp[0:32, :], self.rb31[:], ALU.subtract, [self.rbp, self.rb31], [self.rbp])
        self.memset(self.rbp[32:33, :], NEG, [self.rbp])
        for a, nm in enumerate(('k', 'v')):
            self.memset(self.w1bd[a][:], 0.0, [self.w1bd[a]])
            self.memset(self.w2bd[a][:], 0.0, [self.w2bd[a]])
            for kv in range(2):
                k.dma('pool', self.w1bd[a][kv * 64:(kv + 1) * 64, :, kv * 64:(kv + 1) * 64], i['cmp_%s_w1' % nm].rearrange('j d e -> d j e'), writes=[self.w1bd[a]])
                k.dma('pool', self.w2bd[a][kv * 64:(kv + 1) * 64, kv * 64:(kv + 1) * 64], i['cmp_%s_w2' % nm], writes=[self.w2bd[a]])
            p = self.pa_next()
            for j in range(32):
                self.mm(p[:, 0:2], self.w1bd[a][:, j, :], self.peT[:, a, j:j + 1].to_broadcast([128, 2]), j == 0, j == 31, [self.w1bd[a], self.peT], [p])
            self.cp(self.b1[a][:], p[:, 0:1], [p], [self.b1[a]])
        k.dma('pool', self.wg17[0:16, :], i['w_gla_gate'], writes=[self.wg17])
        k.dma('pool', self.wg17[16:17, :], i['b_gla_gate'], writes=[self.wg17])
        self.memset(self.lrT[:], 1.0, [self.lrT])
        zb = Buf()
        for t in (self.k_selT, self.k_winT, self.k_cmpT, self.v_sel, self.v_win):
            self.memset(t[:], 0.0, [zb])
        self.memset(self.v_sel[:, :, :, 64:65], 1.0, [zb])
        self.memset(self.v_win[:, :, :, 64:65], 1.0, [zb])
        self.memset(self.v_cmp[:], 0.0, [zb])
        for t in (self.hgvT, self.pg, self.hist, self.S, self.Sb, self.rawT[0], self.rawT[1], self.v_mem, self.neg, self.k_memT, self.pcT, self.negx[0][0], self.negx[0][1], self.negx[1][0], self.negx[1][1]):
            self.memset(t[:], 0.0, [t])
        self.memset(self.v_mem[:, :, :, 128:129], 1.0, [self.v_mem])
        for b in self.b_ksel + self.b_vsel + self.b_win + [self.b_kcmp, self.b_vcmp]:
            b.w = zb.w
        self.build_bias_tiles()

    def build_bias_tiles(self):
        for oi, off in enumerate((0, 128)):
            for half in range(2):
                p = self.pa_next()
                for ql in range(64):
                    q = half * 64 + ql
                    j0 = DMAX - off - q
                    self.mm(p[:, ql * 8:(ql + 1) * 8], self.ohdr[:, j0:j0 + 128], self.rbp[:, :], True, True, [self.ohdr, self.rbp], [p])
                src = p[:, :].rearrange('p (q h) -> p h q', h=8)
                for kv in range(2):
                    self.cp(self.BT[:, kv, oi, :].rearrange('p (g q) -> p g q', g=4)[:, :, half * 64:(half + 1) * 64],
                            src[:, kv * 4:(kv + 1) * 4, :], [p], [self.BT])
        p = self.pa_next()
        for uu in range(16):
            i0 = (113 - 16 * uu) - DMIN
            self.mm(p[:, uu * 8:(uu + 1) * 8], self.ohda[:, i0:i0 + 128], self.rbp[:, :], True, True, [self.ohda, self.rbp], [p])
        self.cp(self.BC[:], p[:, 0:128].rearrange('p (u h) -> p h u', h=8), [p], [self.BC])

    def load_x(self, xt, src):
        self.k.dma('sp', xt[:], src, writes=[xt])

    def norm_T(self, xt, gi, outT):
        sq, ss, rs = self.hn_sq, self.hn_ss, self.hn_rs
        self.act(sq[:], xt[:], AF.Square, [xt], [sq, ss], accum_out=ss[:, 0:1])
        self.rstd_from_ss(ss[:, 0:1], 1, D, [ss], rs)
        self.ts(self.hb[:], xt[:], rs[:, 0:1], None, ALU.mult, None, [xt, rs], [self.hb])
        p = self.pb_next()
        for c in range(8):
            self.tr(p[:, c * 128:(c + 1) * 128], self.hb[:, c * 128:(c + 1) * 128], False, [self.hb], [p])
        self.tt(outT[:], p[:, :].rearrange('p (a b) -> p a b', a=8), self.gcol[:, gi, :].unsqueeze(2).to_broadcast([128, 8, 128]), ALU.mult,
                [p, self.gains], [outT])

    def proj(self, nm, tag, ncols, srcT=None):
        srcT = srcT or self.hT
        p = self.pa_next()
        j = 0
        for a in range(0, ncols, 256):
            b = min(ncols, a + 256)
            w, wb = self.wget('%s:%s:%d' % (nm, tag, j))
            for c in range(8):
                self.mm(p[:, a:b], srcT[:, c, :], w[:, c, :], c == 0, c == 7, [srcT, wb], [p])
            j += 1
        return p

    def a2_rows(self, slot, rows, gi=0):
        p = self.pa_next()
        for j, sl in enumerate((0, 1, 2)):
            self.tr(p[:, j * 128:(j + 1) * 128], rows[:, sl, :], True, [rows], [p])
        self.cp(self.rawT[0][:, 16 + gi * 128:16 + (gi + 1) * 128], p[:, 0:128], [p], [self.rawT[0]])
        self.act(self.rawT[1][:, 16 + gi * 128:16 + (gi + 1) * 128], p[:, 128:256], AF.Copy, [p], [self.rawT[1]])
        self.cp(self.k_selT[:, slot * 128:(slot + 1) * 128], p[:, 256:384], [p], [self.b_ksel[slot]])
        self.cp(self.v_sel[:, slot, :, 0:64], rows[:, 3, :].rearrange('p (k d) -> p k d', k=2), [rows], [self.b_vsel[slot]], eng='pool')

    def a2_win(self, slot, winrows):
        r = slot % 8
        p = self.pa_next()
        self.tr(p[:, 0:128], winrows[:, 0, :], True, [winrows], [p])
        self.cp(self.k_winT[:, r * 128:(r + 1) * 128], p[:, 0:128], [p], [self.b_win[r]])
        self.cp(self.v_win[:, r, :, 0:64], winrows[:, 1, :].rearrange('p (k d) -> p k d', k=2), [winrows], [self.b_win[r]], eng='pool')

    def compress(self, slot0, G=1):
        cc0 = 8 * slot0
        n8 = 8 * G

        def chain(a):
            hx, hx2 = self.hxs[a], self.hx2s[a]
            p = self.pa_next()
            n = 0
            for r in range(2):
                for j in range(16):
                    st = 16 * r + j
                    self.mm(p[:, 0:n8], self.w1bd[a][:, st, :], self.rawT[a][:, st:st + 16 * (n8 - 1) + 1:16], n == 0, n == 31, [self.w1bd[a], self.rawT[a]], [p])
                    n += 1
            self.cp(self.rawT[a][:, 0:16], self.rawT[a][:, G * 128:G * 128 + 16], [self.rawT[a]], [self.rawT[a]], eng='pool')
            self.act(hx[:, 0:n8], p[:, 0:n8], AF.Identity, [p, self.b1[a]], [hx], bias=self.b1[a][:, 0:1])
            yield
            x, t = hx[:, 0:n8], hx2[:, 0:n8]
            self.tt(t, x, x, ALU.mult, [hx], [hx2])
            yield
            self.ts(t, t, 0.044715, 1.0, ALU.mult, ALU.add, [hx2], [hx2])
            yield
            self.tt(t, t, x, ALU.mult, [hx, hx2], [hx2])
            yield
            self.act(t, t, AF.Sigmoid, [hx2], [hx2], scale=1.5957691216057308)
            yield
            if a == 0:
                self.tt(self.hg[:, 0:n8], t, x, ALU.mult, [hx, hx2], [self.hg])
                yield
                p2 = self.pa_next()
                self.mm(p2[:, 0:n8], self.w2bd[0][:], self.hg[:, 0:n8], True, True, [self.w2bd[0], self.hg], [p2])
                yield
                self.cp(self.kc[:, 0:n8], p2[:, 0:n8], [p2], [self.kc])
                yield
                self.tt(self.kc2[:, 0:n8], self.kc[:, 0:n8], self.kc[:, 0:n8], ALU.mult, [self.kc], [self.kc2])
                yield
                p3 = self.pa_next()
                self.mm(p3[:, 0:n8], self.onesbd[:], self.kc2[:, 0:n8], True, True, [self.onesbd, self.kc2], [p3])
                yield
                self.rstd_from_ss(p3[:, 0:n8], n8, 64, [p3], self.kcr)
                yield
                self.tt(self.kc[:, 0:n8], self.kc[:, 0:n8], self.kcr[:, 0:n8], ALU.mult, [self.kc, self.kcr], [self.kc])
                yield
                self.ts(self.k_cmpT[:, cc0:cc0 + n8], self.kc[:, 0:n8], self.gk0col[:, 0:1], None, ALU.mult, None, [self.kc, self.gains], [self.b_kcmp])
                yield
            else:
                self.tt(self.hgvT[:, cc0:cc0 + n8], t, x, ALU.mult, [hx, hx2], [self.hgvT])
                yield
                for g in range(cc0 // 128, (cc0 + n8 - 1) // 128 + 1):
                    p2 = self.pa_next()
                    self.mm(p2[:, 0:128], self.hgvT[:, g * 128:(g + 1) * 128], self.w2bd[1][:], True, True, [self.hgvT, self.w2bd[1]], [p2])
                    yield
                    self.cp(self.v_cmp[:, g, :, :], p2[:, 0:128].rearrange('p (k d) -> p k d', k=2), [p2], [self.b_vcmp])
                    yield

        gens = [chain(0), chain(1)]
        while gens:
            for g_ in list(gens):
                try:
                    next(g_)
                except StopIteration:
                    gens.remove(g_)

    def la_compute(self, rowmask_col):
        p = self.pb_next()
        self.tr(p[0:16, 0:128], self.lrn[:], False, [self.lrn], [p])
        self.cp(self.lrT[0:16, :], p[0:16, 0:128], [p], [self.lrT])
        pz = self.pa_next()
        self.mm(pz[:, 0:256], self.lrT[:], self.wg17[:], True, True, [self.lrT, self.wg17], [pz])
        self.act(self.e1[:], pz[:, 0:256], AF.Exp, [pz], [self.e1], scale=-1.0)
        self.act(self.e1[:], self.e1[:], AF.Ln, [self.e1], [self.e1], bias=1.0)
        self.ts(self.la[:], self.e1[:], rowmask_col, -1.0 / 16.0, ALU.mult, ALU.mult, [self.e1, self.e0], [self.la])

    def gla_state_update(self, S, Sb, ecbl_ready):
        prc = self.pa_next()
        self.mm(prc[:, 0:256], self.L[:], self.la[:], True, True, [self.L, self.la], [prc])
        self.act(self.erc[:], prc[:, 0:256], AF.Exp, [prc], [self.erc])
        self.tt(self.kl[:], self.gqk[:, 256:512], self.erc[:], ALU.mult, [self.gqk, self.erc], [self.kl])
        if not ecbl_ready:
            pl = self.pa_next()
            for h in range(4):
                self.mm(pl[0:64, 2 * h:2 * h + 2], self.la[:, h * 64:(h + 1) * 64], self.ones_col[:, 0:1].to_broadcast([128, 2]), True, True, [self.la, self.ones_col], [pl])
            self.act(self.ecbl[:], pl[0:64, 0:8:2], AF.Exp, [pl], [self.ecbl])
        pu = self.pa_next()
        for h in range(4):
            self.mm(pu[0:64, h * 128:(h + 1) * 128], self.kl[:, h * 64:(h + 1) * 64], self.gv[:, h * 128:(h + 1) * 128], True, True, [self.kl, self.gv], [pu])
        for h in range(4):
            self.stt(S[:, h, :], S[:, h, :], self.ecbl[:, h:h + 1], pu[0:64, h * 128:(h + 1) * 128], ALU.mult, ALU.add, [S, self.ecbl, pu], [S])
        self.cp(Sb[:], S[:], [S], [Sb])

    def stage_a(self, slot, xt, own, need_win, rows_out=None, win_out=None, gi=0, flush=True):
        k = self.k
        self.norm_T(xt, 0, self.hT)
        if own:
            p = self.proj('w_in', 'c0', 512)
            sq = self.headnorm(p[:, :], 8, 64, [p])
            self.tt(self.qn[:].rearrange('p g k d -> p k g d'), sq[:, 0:512].rearrange('p (k g d) -> p k g d', k=2, g=4),
                    self.gq[:].unsqueeze(1).unsqueeze(1).to_broadcast([128, 2, 4, 64]), ALU.mult, [sq, self.gains], [self.qn])
            pq = self.pb_next()
            for g in range(4):
                self.tr(pq[:, g * 128:(g + 1) * 128], self.qn[:, g, :, :].rearrange('p k d -> p (k d)'), False, [self.qn], [pq])
            self.cp(self.qT[:].rearrange('p a b -> p (a b)'), pq[:, 0:512], [pq], [self.qT])
        p = self.proj('w_in', 'c1', 512)
        self.act(self.rows[:, 0:2, :].rearrange('p a b -> p (a b)'), p[:, 0:256], AF.Copy, [p], [self.rows])
        self.act(self.rows[:, 3, :], p[:, 384:512], AF.Copy, [p], [self.rows])
        sq = self.headnorm(p[:, 256:384], 2, 64, [p])
        self.tt(self.rows[:, 2, :].rearrange('p (h d) -> p h d', h=2), sq[:, 0:128].rearrange('p (h d) -> p h d', h=2),
                self.gk[:, 1, :].unsqueeze(1).to_broadcast([128, 2, 64]), ALU.mult, [sq, self.gains], [self.rows])
        if rows_out is not None:
            k.dma('sp', rows_out[0], self.rows[rows_out[1], :, :].rearrange('p a b -> p (a b)'), reads=[self.rows])
        self.a2_rows(slot, self.rows, gi)
        if flush:
            self.compress(slot - gi, gi + 1)
        lr0 = 256
        if own or need_win:
            p = self.proj('w_in', 'c2', 296)
        else:
            p = self.proj('w_in', 'c2lr', 16)
            lr0 = 0
        if need_win:
            sq = self.headnorm(p[:, 0:128], 2, 64, [p])
            self.tt(self.winrows[:, 0, :].rearrange('p (h d) -> p h d', h=2), sq[:, 0:128].rearrange('p (h d) -> p h d', h=2),
                    self.gk[:, 2, :].unsqueeze(1).to_broadcast([128, 2, 64]), ALU.mult, [sq, self.gains], [self.winrows])
            self.act(self.winrows[:, 1, :], p[:, 128:256], AF.Copy, [p], [self.winrows])
            if win_out is not None:
                k.dma('sp', win_out[0], self.winrows[win_out[1], :, :].rearrange('p a b -> p (a b)'), reads=[self.winrows])
            self.a2_win(slot, self.winrows)
        self.cp(self.lrn[:], p[:, lr0:lr0 + 16], [p], [self.lrn])
        if own:
            self.act(self.gates[:], p[:, 272:296], AF.Sigmoid, [p], [self.gates])
        if own:
            p = self.proj('w_in', 'c3', 512)
            self.act(self.gqk[:, 0:256], p[:, 0:256], AF.Copy, [p], [self.gqk], scale=0.125)
            self.cp(self.gqk[:, 256:512], p[:, 256:512], [p], [self.gqk])
        else:
            p = self.proj('w_in', 'c3k', 256)
            self.cp(self.gqk[:, 256:512], p[:, 0:256], [p], [self.gqk])
        p = self.proj('w_in', 'c4', 512)
        self.act(self.gv[:], p[:, :], AF.Copy, [p], [self.gv])

    def tr4(self, src, dstT):
        p = self.pb_next()
        for j in range(4):
            self.tr(p[:, j * 128:(j + 1) * 128], src[:, j * 128:(j + 1) * 128], False, [src], [p])
        self.cp(dstT[:].rearrange('p a b -> p (a b)'), p[:, 0:512], [p], [dstT])

    def gla_tile(self, S, Sb, rowmask_col):
        self.la_compute(rowmask_col)
        p = self.pb_next()
        for j in range(8):
            self.tr(p[0:64, j * 128:(j + 1) * 128], self.gqk[:, j * 64:(j + 1) * 64], False, [self.gqk], [p])
        self.cp(self.gqkT[:].rearrange('p a b -> p (a b)'), p[0:64, :], [p], [self.gqkT])
        pc = self.pa_next()
        for h in range(4):
            self.mm(pc[0:64, h * 128:(h + 1) * 128], self.la[:, h * 64:(h + 1) * 64], self.U[:], True, True, [self.la, self.U], [pc])
        self.act(self.ecb[:].rearrange('p a b -> p (a b)'), pc[0:64, :], AF.Exp, [pc], [self.ecb])
        self.act(self.encb[:].rearrange('p a b -> p (a b)'), pc[0:64, :], AF.Exp, [pc], [self.encb], scale=-1.0)
        self.cp(self.ecbl[:], self.ecb[:, :, 127], [self.ecb], [self.ecbl])
        self.tt(self.qeT[:], self.gqkT[:, 0:4, :], self.ecb[:], ALU.mult, [self.gqkT, self.ecb], [self.qeT])
        self.tt(self.keT[:], self.gqkT[:, 4:8, :], self.encb[:], ALU.mult, [self.gqkT, self.encb], [self.keT])
        pA = self.pa_next()
        for h in range(4):
            self.mm(pA[:, h * 128:(h + 1) * 128], self.keT[:, h, :], self.qeT[:, h, :], True, True, [self.keT, self.qeT], [pA])
        self.tt(self.Am[:], pA[:, :].rearrange('p (h t) -> p h t', h=4), self.U[:].unsqueeze(1).to_broadcast([128, 4, 128]), ALU.mult, [pA, self.U], [self.Am])
        po = self.pa_next()
        for h in range(4):
            self.mm(po[:, h * 128:(h + 1) * 128], self.Am[:, h, :], self.gv[:, h * 128:(h + 1) * 128], h == 0, False, [self.Am, self.gv], [po])
            self.mm(po[:, h * 128:(h + 1) * 128], self.qeT[:, h, :], Sb[:, h, :], False, h == 3, [self.qeT, Sb], [po])
        sq = self.headnorm(po[:, :], 4, 128, [po])
        self.tt(sq[:, 0:512].rearrange('p (h d) -> p h d', h=4), sq[:, 0:512].rearrange('p (h d) -> p h d', h=4),
                self.ggo[:].unsqueeze(1).to_broadcast([128, 4, 128]), ALU.mult, [sq, self.gains], [sq])
        self.tt(self.ogla[:], sq[:, 0:512], self.gr[:], ALU.mult, [sq, self.gr], [self.ogla])
        self.tr4(self.ogla, self.oglaT)
        self.gla_state_update(S, Sb, True)

    def nsa_tile(self, sq_, mset, nq=128):
        k = self.k
        sq = sq_
        ncc = min(8 * (sq + 1), 512)
        nb = 2 * (sq + 1)
        ngrp = (ncc + 127) // 128
        pacc_c = self.pacc[0]
        for kv in range(2):
            rs = slice(kv * 64, (kv + 1) * 64)
            banks = []
            for g in range(4):
                p = self.pa_next()
                banks.append(p)
                self.mm(p[:, 0:ncc], self.qT[rs, g, :], self.k_cmpT[rs, 0:ncc], True, False, [self.qT, self.b_kcmp], [p])
                self.mm(p[:, 0:ncc], self.ones_row[:, :], self.cmask[mset][:, 0:ncc], False, True, [self.ones_row, self.cmask[mset]], [p])
            w0 = 8 * (sq - 1)
            w1 = min(8 * (sq + 1), ncc)
            for g in range(4):
                h = kv * 4 + g
                p = banks[g]
                self.tt(p[:, w0:w1], p[:, w0:w1], self.BC[:, h, 0:w1 - w0], ALU.add, [p, self.BC], [p])
                self.act(self.pc[:, g, 0:ncc], p[:, 0:ncc], AF.Exp, [p], [self.pc, self.psum8], accum_out=self.psum8[:, h:h + 1])
            hs = slice(kv * 4, kv * 4 + 4)
            self.ts(self.prc8[:, hs], self.psum8[:, hs], TINY, None, ALU.max, None, [self.psum8], [self.prc8])
            self.recip(self.prc8[:, hs], self.prc8)
            self.tt(self.pc[:, :, 0:ncc], self.pc[:, :, 0:ncc], self.prc8[:, hs].unsqueeze(2).to_broadcast([128, 4, ncc]), ALU.mult, [self.pc, self.prc8], [self.pc])
            k.op('dve', lambda e: e.tensor_reduce(out=self.pg[:, kv, 0:ncc], in_=self.pc[:, :, 0:ncc].rearrange('p g c -> p c g'), axis=AX.X, op=ALU.add), [self.pc], [self.pg])
            for g_ in range(ngrp):
                w = min(128, ncc - g_ * 128)
                p = self.pa_next()
                for g in range(4):
                    self.tr(p[0:w, g * 128:(g + 1) * 128], self.pc[:, g, g_ * 128:g_ * 128 + w], True, [self.pc], [p])
                self.cp(self.pcT[0:w, :, :].rearrange('p a b -> p (a b)'), p[0:w, 0:512], [p], [self.pcT])
                for g in range(4):
                    h = kv * 4 + g
                    self.mm(pacc_c[:, h * 64:(h + 1) * 64], self.pcT[0:w, g, :], self.v_cmp[0:w, g_, kv, :], (kv == 0 and g_ == 0 and g == 0),
                            (kv == 1 and g_ == ngrp - 1 and g == 3), [self.pcT, self.b_vcmp], [pacc_c])
        self.cp(self.oc[:], pacc_c[:, :], [pacc_c], [self.oc])
        if ncc < 528:
            self.memset(self.pg[:, :, ncc:528], 0.0, [self.pg], eng='dve')
        k.op('dve', lambda e: e.tensor_reduce(out=self.imp[:, :, 0:nb], in_=self.pg[:, :, 0:4 * nb].rearrange('p k (j f) -> p k j f', f=4), axis=AX.X, op=ALU.add), [self.pg], [self.imp])
        self.tt(self.imp[:, :, 0:nb], self.imp[:, :, 0:nb], self.pg[:, :, 4:4 * nb + 4:4], ALU.add, [self.imp, self.pg], [self.imp])
        self.tt(self.imp[:, :, 0:nb], self.imp[:, :, 0:nb], self.fbabs[mset][:, 0:nb].unsqueeze(1).to_broadcast([128, 2, nb]), ALU.add, [self.imp, self.fbabs[mset]], [self.imp])
        self.tt(self.imp[:, :, nb - 3:nb], self.imp[:, :, nb - 3:nb], self.relb[:].unsqueeze(1).to_broadcast([128, 2, 3]), ALU.add, [self.imp, self.relb], [self.imp])
        for kv in range(2):
            k.op('dve', lambda e: e.max(out=self.m16[:, kv, 0:8], in_=self.imp[:, kv, 0:nb]), [self.imp], [self.m16])
            k.op('dve', lambda e: e.match_replace(out=self.wk[:, 0:nb], in_to_replace=self.m16[:, kv, 0:8], in_values=self.imp[:, kv, 0:nb], imm_value=-3.0e38), [self.imp, self.m16], [self.wk])
            k.op('dve', lambda e: e.max(out=self.m16[:, kv, 8:16], in_=self.wk[:, 0:nb]), [self.wk], [self.m16])
            self.ts(self.wk[:, 0:nb], self.imp[:, kv, 0:nb], self.m16[:, kv, 15:16], -NEG, ALU.is_ge, ALU.mult, [self.imp, self.m16], [self.wk])
            self.tt(self.neg[:, kv, 0:nb], self.wk[:, 0:nb], self.pm[mset][:, 0:nb], ALU.add, [self.wk, self.pm[mset]], [self.neg])
        self.mg_proj()
        NQ = 4 * nq
        if nq == 128:
            qrhs = [self.qT[kv * 64:(kv + 1) * 64, :, :].rearrange('p g q -> p (g q)') for kv in range(2)]
            I4f, I4t = self.I4[:].rearrange('p g q -> p (g q)'), self.I4
            BTv = lambda kv, oi: self.BT[:, kv, oi, :]
            BTt = self.BT
        else:
            self.cp(self.qTs[:, 0:NQ].rearrange('p (g q) -> p g q', g=4), self.qT[:, :, 0:nq], [self.qT], [self.qTs])
            self.cp(self.I4s[:, 0:NQ].rearrange('p (g q) -> p g q', g=4), self.I4[:, :, 0:nq], [self.I4], [self.I4s], eng='pool')
            for kv in range(2):
                for oi in range(2):
                    self.cp(self.BTs[:, kv, oi, 0:NQ].rearrange('p (g q) -> p g q', g=4),
                            self.BT[:, kv, oi, :].rearrange('p (g q) -> p g q', g=4)[:, :, 0:nq], [self.BT], [self.BTs], eng='pool')
            qrhs = [self.qTs[kv * 64:(kv + 1) * 64, 0:NQ] for kv in range(2)]
            I4f, I4t = self.I4s[:, 0:NQ], self.I4s
            BTv = lambda kv, oi: self.BTs[:, kv, oi, 0:NQ]
            BTt = self.BTs
        for br in (1, 0):
            kts = list(range(0, sq + 1)) if br == 0 else [t for t in range(sq - 4, sq + 1) if t >= 0]
            ngx = [None, None]

            def stage1(kt):
                banks = []
                if br == 0 and kt % 4 == 0 and kt < sq:
                    nblk = min(8, 2 * sq - 2 * kt)
                    for kv in range(2):
                        t_ = self.negx[kv][(kt // 4) % 2]
                        ngx[kv] = t_
                        self.cp(t_[:, 0:nblk * 64].rearrange('p (j s) -> p j s', s=64), self.neg[:, kv, 2 * kt:2 * kt + nblk].unsqueeze(2).to_broadcast([128, nblk, 64]),
                                [self.neg], [t_], eng='pool')
                off = 128 * (sq - kt)
                masked = (br == 0 and kt < sq)
                edge = (br == 1 and off == 512)
                for kv in range(2):
                    rs = slice(kv * 64, (kv + 1) * 64)
                    p = self.pa_next()
                    if br == 0:
                        kT, kb = self.k_selT[rs, kt * 128:(kt + 1) * 128], self.b_ksel[kt]
                    else:
                        r = kt % 8
                        kT, kb = self.k_winT[rs, r * 128:(r + 1) * 128], self.b_win[r]
                    self.mm(p[:, 0:NQ], kT, qrhs[kv], True, not (masked or edge), [kb, self.qT, self.qTs], [p])
                    banks.append(p)
                for kv in range(2):
                    p = banks[kv]
                    if masked:
                        t_ = ngx[kv]
                        self.mm(p[:, 0:NQ], t_[:, (kt % 4) * 128:(kt % 4 + 1) * 128], I4f, False, True, [t_, I4t], [p])
                    if edge:
                        self.mm(p[:, 0:NQ], self.BT512T[:, :], I4f, False, True, [self.BT512T, I4t], [p])
                    if off in (0, 128):
                        self.tt(p[:, 0:NQ], p[:, 0:NQ], BTv(kv, off // 128), ALU.add, [p, BTt], [p])
                return banks

            def stage2(n, kt, banks):
                for kv in range(2):
                    p = banks[kv]
                    pT = self.pT[self._pTi]
                    self._pTi = (self._pTi + 1) % len(self.pT)
                    self.act(pT[:, 0:NQ], p[:, 0:NQ], AF.Exp, [p, self.slotbias[mset]], [pT], bias=self.slotbias[mset][:, kt:kt + 1])
                    if br == 0:
                        vv, vb = self.v_sel[:, kt, kv, :], self.b_vsel[kt]
                    else:
                        r = kt % 8
                        vv, vb = self.v_win[:, r, kv, :], self.b_win[r]
                    pacc = self.pacc[kv]
                    for g in range(4):
                        self.mm(pacc[0:nq, g * 65:(g + 1) * 65], pT[:, g * nq:(g + 1) * nq], vv, (n == 0 and g == 0), (n == len(kts) - 1 and g == 3), [pT, vb], [pacc])

            nxt = stage1(kts[0])
            for n, kt in enumerate(kts):
                cur = nxt
                if n + 1 < len(kts):
                    nxt = stage1(kts[n + 1])
                stage2(n, kt, cur)
            dst = self.osel if br == 0 else self.owin
            for kv in range(2):
                pacc = self.pacc[kv]
                acc3 = pacc[0:nq, 0:260].rearrange('p (g d) -> p g d', g=4)
                hs = slice(kv * 4, kv * 4 + 4)
                self.ts(self.den[0:nq, hs], acc3[:, :, 64], TINY, None, ALU.max, None, [pacc], [self.den])
                self.recip(self.den[0:nq, hs], self.den)
                self.tt(dst[0:nq, hs, :], acc3[:, :, 0:64], self.den[0:nq, hs].unsqueeze(2).to_broadcast([nq, 4, 64]), ALU.mult, [pacc, self.den], [dst])
        g3 = self.gates[:].rearrange('p (h i) -> p h i', i=3)
        oc3 = self.oc[:].rearrange('p (h d) -> p h d', h=8)
        bc = lambda i: g3[:, :, i].unsqueeze(2).to_broadcast([128, 8, 64])
        self.tt(oc3, oc3, bc(0), ALU.mult, [self.oc, self.gates], [self.oc])
        self.tt(self.osel[:], self.osel[:], bc(1), ALU.mult, [self.osel, self.gates], [self.osel])
        self.tt(self.owin[:], self.owin[:], bc(2), ALU.mult, [self.owin, self.gates], [self.owin])
        self.tt(oc3, oc3, self.osel[:], ALU.add, [self.oc, self.osel], [self.oc])
        self.tt(self.onsa[:].rearrange('p (h d) -> p h d', h=8), oc3, self.owin[:], ALU.add, [self.oc, self.owin], [self.onsa])
        self.tr4(self.onsa, self.onsaT)

    def xatt_tile(self):
        for mt in range(2):
            p = self.pa_next()
            for h in range(4):
                self.mm(p[:, h * 128:(h + 1) * 128], self.k_memT[:, h, mt * 128:(mt + 1) * 128], self.xqT[:, h, :], True, True, [self.k_memT, self.xqT], [p])
            self.act(self.pxT[:, mt, :, :].rearrange('p a b -> p (a b)'), p[:, :], AF.Exp, [p], [self.pxT])
        for hp in range(2):
            p = self.pa_next()
            for hh in range(2):
                h = hp * 2 + hh
                for mt in range(2):
                    self.mm(p[:, hh * 129:(hh + 1) * 129], self.pxT[:, mt, h, :], self.v_mem[:, mt, h, :], (hh == 0 and mt == 0), (hh == 1 and mt == 1), [self.pxT, self.v_mem], [p])
            a3 = p[:, 0:258].rearrange('p (h d) -> p h d', h=2)
            self.ts(self.den[:, 0:2], a3[:, :, 128], TINY, None, ALU.max, None, [p], [self.den])
            self.recip(self.den[:, 0:2], self.den)
            self.tt(self.ox[:, hp * 256:(hp + 1) * 256].rearrange('p (h d) -> p h d', h=2), a3[:, :, 0:128], self.den[:, 0:2].unsqueeze(2).to_broadcast([128, 2, 128]), ALU.mult, [p, self.den], [self.ox])
        self.tr4(self.ox, self.oxT)

    def own_rest_a(self):
        p = self.proj('w_in', 'c5', 512)
        self.act(self.gr[:], p[:, :], AF.Silu, [p], [self.gr])
        p = self.proj('w_in', 'c6', 512)
        sq = self.headnorm(p[:, :], 4, 128, [p])
        self.tt(self.xqn[:].rearrange('p (h d) -> p h d', h=4), sq[:, 0:512].rearrange('p (h d) -> p h d', h=4),
                self.gxq[:].unsqueeze(1).to_broadcast([128, 4, 128]), ALU.mult, [sq, self.gains], [self.xqn])
        self.tr4(self.xqn, self.xqT)

    def mg_proj(self):
        for j in range(6):
            p = self.pa_next()
            for hf in range(2):
                w, wb = self.wget('w_in:mg:%d' % (2 * j + hf))
                for cc in range(2):
                    o_ = (hf * 2 + cc) * 128
                    for c in range(8):
                        self.mm(p[:, o_:o_ + 128], w[:, c, cc * 128:(cc + 1) * 128], self.hT[:, c, :], c == 0, c == 7, [wb, self.hT], [p])
            self.act(self.mgT[:, 4 * j:4 * j + 4, :].rearrange('p a b -> p (a b)'), p[:, :], AF.Sigmoid, [p], [self.mgT])

    def merge_ffn(self, xt, y_out, conv_out, conv_tok0_out):
        k = self.k
        for br, (oT, nm) in enumerate(((self.onsaT, 'w_nsa_out'), (self.oglaT, 'w_gla_out'), (self.oxT, 'w_x_out'))):
            for half in range(2):
                p = self.pa_next()
                for hf in range(2):
                    w, wb = self.wget('%s:o:%d' % (nm, half * 2 + hf))
                    for cc in range(2):
                        o_ = (hf * 2 + cc) * 128
                        for c in range(4):
                            self.mm(p[:, o_:o_ + 128], w[:, c, cc * 128:(cc + 1) * 128], oT[:, c, :], c == 0, c == 3, [wb, oT], [p])
                p3 = p[:, :].rearrange('p (a b) -> p a b', a=4)
                mg = self.mgT[:, br * 8 + half * 4:br * 8 + half * 4 + 4, :]
                hs = slice(half * 4, half * 4 + 4)
                if br == 0:
                    self.tt(self.mrg[:, hs, :], p3, mg, ALU.mult, [p, self.mgT], [self.mrg])
                else:
                    self.tt(self.mtmp[:], p3, mg, ALU.mult, [p, self.mgT], [self.mtmp])
                    dst = self.mrg if br == 1 else self.mrgT
                    self.tt(dst[:, hs, :], self.mrg[:, hs, :], self.mtmp[:], ALU.add, [self.mrg, self.mtmp], [dst])
        if self.stop < 11:
            return
        for half in range(2):
            p = self.pa_next()
            for hf in range(2):
                w, wb = self.wget('w_o:o:%d' % (half * 2 + hf))
                for c in range(8):
                    self.mm(p[:, hf * 256:(hf + 1) * 256], self.mrgT[:, c, :], w[:, c, :], c == 0, c == 7, [self.mrgT, wb], [p])
            self.tt(self.x1[:, half * 512:(half + 1) * 512], p[:, :], xt[:, half * 512:(half + 1) * 512], ALU.add, [p, xt], [self.x1])
        if self.stop < 12:
            return
        self.norm_T(self.x1, 2, self.h2T)
        if self.stop < 13:
            return
        groups = [(j0, min(2, NFF - j0)) for j0 in range(0, NFF, 2)]
        for gi_, (j0, ng) in enumerate(groups):
            gx, ub = self.gx[gi_ % 2], self.ub[gi_ % 2]
            banks = []
            for jj in range(ng):
                w, wb = self.wget('w_up:u:%d' % (j0 + jj))
                if jj % 2 == 0:
                    p = self.pa_next()
                    banks.append(p)
                for cc in range(2):
                    o_ = ((jj % 2) * 2 + cc) * 128
                    for c in range(8):
                        self.mm(p[:, o_:o_ + 128], w[:, c, cc * 128:(cc + 1) * 128], self.h2T[:, c, :], c == 0, c == 7, [wb, self.h2T], [p])
            self.cp(gx[:, 0:ng, 0:2], self.hist[:, j0:j0 + ng, :], [self.hist], [gx], eng='pool')
            for bi, p in enumerate(banks):
                nb_ = min(2, ng - 2 * bi)
                p4 = p[:, 0:nb_ * 256].rearrange('p (a t b) -> p a t b', t=2, b=128)
                self.act(ub[:, 2 * bi:2 * bi + nb_, :], p4[:, :, 0, :], AF.Copy, [p], [ub])
                self.cp(gx[:, 2 * bi:2 * bi + nb_, 2:130], p4[:, :, 1, :], [p], [gx])
            self.cp(self.hist[:, j0:j0 + ng, :], gx[:, 0:ng, 128:130], [gx], [self.hist], eng='pool')
            if conv_tok0_out is not None:
                self.cp(self.g0col[:, j0:j0 + ng], gx[:, 0:ng, 2], [gx], [self.g0col], eng='pool')
            cw = lambda t: self.convw[:, j0:j0 + ng, t:t + 1].to_broadcast([128, ng, 128])
            a_, b_ = self.cva[:, 0:ng, :], self.cvb[:, 0:ng, :]
            c_ = self.cvc[:, 0:ng, :]
            self.tt(b_, gx[:, 0:ng, 1:129], cw(1), ALU.mult, [gx, self.gains], [self.cvb], eng='pool')
            self.tt(c_, gx[:, 0:ng, 2:130], cw(2), ALU.mult, [gx, self.gains], [self.cvc], eng='pool')
            self.tt(a_, gx[:, 0:ng, 0:128], cw(0), ALU.mult, [gx, self.gains], [self.cva])
            self.tt(a_, a_, b_, ALU.add, [self.cva, self.cvb], [self.cva])
            self.tt(a_, a_, c_, ALU.add, [self.cva, self.cvc], [self.cva])
            self.tt(a_, a_, self.convb[:, j0:j0 + ng].unsqueeze(2).to_broadcast([128, ng, 128]), ALU.add, [self.cva, self.gains], [self.cva])
            self.gelu_tanh(b_, a_, b_, [self.cva], [self.cvb], self.cvb)
            self.tt(self.actT[:, j0:j0 + ng, :], b_, ub[:, 0:ng, :], ALU.mult, [self.cvb, ub], [self.actT])
        if self.stop < 14:
            return
        with k.nc.allow_non_contiguous_dma(reason="conv state columns written transposed (<= 2 x 2816 elements)"):
            if conv_out is not None:
                for t_ in range(2):
                    k.dma('sp', conv_out[t_].rearrange('(j p) -> p j', p=128), self.hist[:, :, t_], reads=[self.hist])
            if conv_tok0_out is not None:
                k.dma('sp', conv_tok0_out.rearrange('t (j p) -> p (t j)', p=128), self.g0col[:], reads=[self.g0col])
        if self.stop < 15:
            return
        p0, p1 = self.pa_next(), self.pa_next()
        for j in range(11):
            w, wb = self.wget('w_down:d:%d' % j)
            for c in range(2):
                f = 2 * j + c
                self.mm(p0[:, :], self.actT[:, f, :], w[:, c, 0:512], f == 0, f == NFF - 1, [self.actT, wb], [p0])
                self.mm(p1[:, :], self.actT[:, f, :], w[:, c, 512:1024], f == 0, f == NFF - 1, [self.actT, wb], [p1])
        self.tt(self.yt[:, 0:512], p0[:, :], self.x1[:, 0:512], ALU.add, [p0, self.x1], [self.yt])
        self.tt(self.yt[:, 512:1024], p1[:, :], self.x1[:, 512:1024], ALU.add, [p1, self.x1], [self.yt])
        if y_out is not None:
            k.dma('sp', y_out[0], self.yt[y_out[1], :], reads=[self.yt])

    def q_tile(self, slot, xt, mset, S, Sb, rowmask_col, rows_out, win_out, y_out, conv_out, conv_tok0_out=None):
        mk_ = lambda nm: self.marks.append(('  %d:%s' % (slot, nm), dict(self.k.cnt))) if slot in (48, 63, 64) else None
        self.stage_a(slot, xt, True, True, rows_out, win_out)
        mk_('stage_a')
        if self.stop < 6:
            return
        self.own_rest_a()
        mk_('rest_a')
        if self.stop < 7:
            return
        self.nsa_tile(slot, mset, nq=(32 if slot == 64 else 128))
        mk_('nsa')
        if self.stop < 8:
            return
        self.gla_tile(S, Sb, rowmask_col)
        mk_('gla')
        if self.stop < 9:
            return
        self.xatt_tile()
        mk_('xatt')
        if self.stop < 10:
            return
        self.merge_ffn(xt, y_out, conv_out, conv_tok0_out)
        mk_('merge_ffn')

    def mem_to_resident(self, mt, memrows):
        p = self.pa_next()
        for h in range(4):
            self.tr(p[:, h * 128:(h + 1) * 128], memrows[:, h * 128:(h + 1) * 128], True, [memrows], [p])
        self.cp(self.k_memT[:, :, mt * 128:(mt + 1) * 128], p[:, :].rearrange('p (h m) -> p h m', h=4), [p], [self.k_memT])
        self.cp(self.v_mem[:, mt, :, 0:128], memrows[:, 512:1024].rearrange('p (h d) -> p h d', h=4), [memrows], [self.v_mem], eng='pool')

    def mem_kv_prompt(self):
        k = self.k
        self.sched_cols('w_mem_kv', 'm', 0, 1024)
        self.sched_cols('w_mem_kv', 'm2', 0, 1024)
        for mt in range(2):
            xt = self.xt[mt % len(self.xt)]
            self.load_x(xt, self.i['mem'][mt * 128:(mt + 1) * 128, :])
            self.norm_T(xt, 1, self.hT)
            tg = 'm' if mt == 0 else 'm2'
            for half in range(2):
                p = self.pa_next()
                for hf in range(2):
                    w, wb = self.wget('w_mem_kv:%s:%d' % (tg, half * 2 + hf))
                    for c in range(8):
                        self.mm(p[:, hf * 256:(hf + 1) * 256], self.hT[:, c, :], w[:, c, :], c == 0, c == 7, [self.hT, wb], [p])
                if half == 0:
                    sq = self.headnorm(p[:, :], 4, 128, [p])
                    self.tt(self.memrows[:, 0:512].rearrange('p (h d) -> p h d', h=4), sq[:, 0:512].rearrange('p (h d) -> p h d', h=4),
                            self.gxk[:].unsqueeze(1).to_broadcast([128, 4, 128]), ALU.mult, [sq, self.gains], [self.memrows])
                else:
                    self.act(self.memrows[:, 512:1024], p[:, :], AF.Copy, [p], [self.memrows])
            k.dma('sp', self.o['memkv_p'][mt * 128:(mt + 1) * 128, :], self.memrows[:], reads=[self.memrows])
            self.mem_to_resident(mt, self.memrows)

    def prompt_phase(self):
        k, i, o = self.k, self.i, self.o
        self.mem_kv_prompt()
        if self.stop < 3:
            return
        slots = list(range(FIRST_Q - self.n_prefix, FIRST_Q + self.n_own))
        self.p_slots = slots
        for s in slots:
            self.sched_tile(s >= FIRST_Q, s >= FIRST_Q - 4)
        if self.do_sample:
            for _ in range(self.n_samp):
                self.sched_tile(True)
        self.load_x(self.xt[0], i['xloc'][slots[0] * 128:(slots[0] + 1) * 128, :])
        self.marks = [('start', dict(self.k.cnt))]
        for n, s in enumerate(slots):
            self.marks.append(('slot%d' % s, dict(self.k.cnt)))
            xt = self.xt[n % len(self.xt)]
            if n + 1 < len(slots):
                s2 = slots[n + 1]
                nxt_x = (self.xt[(n + 1) % len(self.xt)], i['xloc'][s2 * 128:(s2 + 1) * 128, :])
                if len(self.xt) > 1:
                    self.load_x(*nxt_x)
            ncast = -(-len(self.late_casts) // max(1, (FIRST_Q - s))) if s < FIRST_Q else len(self.late_casts)
            for _ in range(ncast):
                nm_, dst_, src_ = self.late_casts.pop(0)
                k.dma('pool', dst_, src_, writes=[self.sbuf_w[nm_]])
            if s < FIRST_Q:
                gi = n % 4
                self.stage_a(s, xt, False, s >= FIRST_Q - 4, gi=gi, flush=(gi == 3 or slots[n + 1] >= FIRST_Q))
                if self.stop < 4:
                    continue
                self.la_compute(self.e0[:, 0:1])
                self.gla_state_update(self.S, self.Sb, False)
                if self.stop < 5:
                    return
            else:
                t = s - 48
                rows_out = (o['rows_p'][t * 128:(t + 1) * 128, :], slice(0, 128)) if t >= 0 else None
                win_out = (o['win_p'][(t - 12) * 128:(t - 11) * 128, :], slice(0, 128)) if t >= 12 else None
                y_out = (o['y_p'][t * 128:(t + 1) * 128, :], slice(0, 128)) if t >= 0 else None
                conv_out = o['conv_p'] if s == slots[-1] else None
                self.q_tile(s, xt, 0, self.S, self.Sb, self.e0[:, 0:1], rows_out, win_out, y_out, conv_out)
                pass
            if len(self.xt) == 1 and n + 1 < len(slots):
                self.load_x(*nxt_x)
            if s == FIRST_Q and self.stop >= 99:
                if True:
                    self.ts(self.hist[:], self.hist[:], self.hs[:, 0:1], None, ALU.mult, None, [self.hist, self.hs], [self.hist])
        k.dma('sp', o['gla_p'].rearrange('h d v -> d h v'), self.S[:], reads=[self.S])

    def sample_phase(self):
        k, i, o = self.k, self.i, self.o
        ns = self.n_samp
        xs_t = self.xt[0]
        k.dma('sp', self.ptb[:], i['page_table'].rearrange('s j -> (s j)').partition_broadcast(128), writes=[self.ptb])
        k.op('pool', lambda e: e.iota(self.iop[:], pattern=[[0, 1]], base=0, channel_multiplier=1), (), [self.iop])
        self.cp(self.e1[:, 0:1], self.iop[:], [self.iop], [self.e1])
        self.cp(self.la[:, 0:ns * 64], self.ptb[:], [self.ptb], [self.la])
        self.ts(self.la[:, 0:ns * 64], self.la[:, 0:ns * 64], 128.0, self.e1[:, 0:1], ALU.mult, ALU.add, [self.la, self.e1], [self.la])
        self.cp(self.idx[:], self.la[:, 0:ns * 64], [self.la], [self.idx])
        self.memset(xs_t[:], 0.0, [xs_t])
        for s in range(ns):
            self.marks.append(('fill%d' % s, dict(self.k.cnt)))
            def gather(j):
                pg = self.page[j % 2]
                col = s * 64 + j
                k.dma('pool', None, None, reads=[self.idx], writes=[pg],
                      fn=lambda e, pg=pg, col=col: e.indirect_dma_start(out=pg[:], out_offset=None, in_=i['cache_kv'],
                                                                        in_offset=bass.IndirectOffsetOnAxis(ap=self.idx[:, col:col + 1], axis=0)))
            gather(0)
            for j in range(64):
                pg = self.page[j % 2]
                if j + 1 < 64:
                    gather(j + 1)
                pgT = T(pg.t[:].rearrange('p (a b) -> p a b', a=4)); pgT.b = pg.b
                self.a2_rows(j, pgT, j % 4)
                if j % 4 == 3:
                    self.compress(j - 3, 4)
            for t in range(4):
                k.dma('sp', self.winrows[:].rearrange('p a b -> p (a b)'), i['cache_win'][s, t * 128:(t + 1) * 128, :], writes=[self.winrows])
                self.a2_win(60 + t, self.winrows)
            k.dma('sp', self.S[:], i['state_gla'][s].rearrange('h d v -> d h v'), writes=[self.S])
            self.cp(self.Sb[:], self.S[:], [self.S], [self.Sb])
            with k.nc.allow_non_contiguous_dma(reason="conv state loaded transposed (2 x 2816 elements)"):
                for t_ in range(2):
                    k.dma('sp', self.hist[:, :, t_], i['state_conv'][s, t_].rearrange('(j p) -> p j', p=128), writes=[self.hist])
            for mt in range(2):
                k.dma('sp', self.memrows[:], i['cache_mem'][s, mt * 128:(mt + 1) * 128, :], writes=[self.memrows])
                self.mem_to_resident(mt, self.memrows)
            k.dma('sp', xs_t[0:1, :], i['xs'][s:s + 1, :], writes=[xs_t])
            k.dma('sp', o['win_s'][s, 0:511, :], i['cache_win'][s, 1:512, :])
            k.dma('sp', o['conv_s'][s, 0:1, :], i['state_conv'][s, 1:2, :])
            self.marks.append(('sq%d' % s, dict(self.k.cnt)))
            self.q_tile(64, xs_t, 1, self.S, self.Sb, self.e0[:, 1:2],
                        (o['rows_s'][s:s + 1, :], slice(0, 1)), (o['win_s'][s, 511:512, :], slice(0, 1)),
                        (o['y_s'][s:s + 1, :], slice(0, 1)), None, conv_tok0_out=o['conv_s'][s, 1:2, :])
            k.dma('sp', o['gla_s'][s].rearrange('h d v -> d h v'), self.S[:], reads=[self.S])


def _t5_bucket_np(d):
    n = np.maximum(d, 0)
    nf = np.maximum(n, 1).astype(np.float32)
    large = 16 + (np.log(nf / np.float32(16)) / np.float32(np.log(128 / 16)) * np.float32(16)).astype(np.int32)
    return np.where(n < 16, n, np.minimum(large, 31))


def _constants():
    c = {}
    c['c_ident'] = np.eye(128, dtype=np.float32)
    s = np.arange(128)
    c['c_U'] = (s[:, None] <= s[None, :]).astype(np.float32)
    c['c_L'] = (s[:, None] > s[None, :]).astype(np.float32)
    c['c_onesbd'] = np.kron(np.eye(2), np.ones((64, 64))).astype(np.float32)
    c['c_edge'] = (NEG * (s[:, None] >= s[None, :])).astype(np.float32).astype(ml_dtypes.bfloat16)
    d = np.arange(DMIN, DMAX + 1)
    oh = np.zeros((33, ND), np.float32)
    valid = (d >= 0) & (d < 512)
    b = _t5_bucket_np(d)
    oh[b[valid], np.nonzero(valid)[0]] = 1.0
    oh[32, ~valid] = 1.0
    c['c_ohda'] = oh
    c['c_ohdr'] = np.ascontiguousarray(oh[:, ::-1])
    relb = np.zeros((128, 3), np.float32)
    relb[:64, 0] = 1e4
    relb[:, 1] = 1e4
    relb[64:, 2] = 1e4
    relb[:64, 2] = -1e30
    c['c_relb'] = relb
    e0 = np.zeros((128, 2), np.float32)
    e0[:, 0] = 1.0
    e0[0, 1] = 1.0
    c['c_e0'] = e0
    return c


def _masks(npre):
    sb = np.zeros((2, 128, NSLOT), np.float32)
    sb[0, :, :npre] = NEG
    pm = np.full((2, 128, 136), NEG, np.float32)
    pm[0, :, :2 * npre] += NEG
    cm = np.zeros((2, 1, 528), np.float32)
    cm[0, 0, :8 * npre + 1] = NEG
    cm[1, 0, 0] = NEG
    fb = np.zeros((2, 128, 136), np.float32)
    fb[0, :, 2 * npre] = 1e4
    fb[1, :, 0] = 1e4
    hs = np.full((128, 1), 0.0 if npre > FIRST_Q else 1.0, np.float32)
    return dict(m_slotbias=sb, m_pm=pm, m_cmask=cm, m_fbabs=fb, m_hs=hs)


_CACHE = {}


def _get_nc(key, **kw):
    if key not in _CACHE:
        _CACHE[key] = MK(**kw).nc
    return _CACHE[key]


def kernel(x_prompt, x_sample, cache_kv, cache_win, state_gla, state_conv, cache_mem, page_table, mem_prompt,
           g_mix, w_in, g_nsa_q, g_nsa_k, cmp_k_pe, cmp_k_w1, cmp_k_w2, cmp_v_pe, cmp_v_w1, cmp_v_w2,
           rel_bias, w_gla_gate, b_gla_gate, g_gla_o, g_mem, w_mem_kv, g_x_q, g_x_k,
           w_nsa_out, w_gla_out, w_x_out, w_o, g_ffn, w_up, conv_w, conv_b, w_down):
    f = lambda a: np.ascontiguousarray(np.asarray(a, dtype=np.float32))
    x_prompt = f(x_prompt); x_sample = f(x_sample).reshape(32, D)
    cache_kv2 = f(cache_kv).reshape(-1, 512)
    cache_win2 = f(cache_win).reshape(32, 512, 256)
    state_gla = f(state_gla); state_conv = f(state_conv)
    cache_mem2 = f(cache_mem).reshape(32, 256, 1024)
    page_table = np.ascontiguousarray(np.asarray(page_table, dtype=np.int32))
    mem_prompt = f(mem_prompt)
    w_up_p = f(w_up).reshape(D, 2, NFF, 128).transpose(0, 2, 1, 3).reshape(D, 2 * DFF)
    shared = dict(
        g_mix=f(g_mix), g_mem=f(g_mem), g_ffn=f(g_ffn), w_in=np.ascontiguousarray(f(w_in)[:, W_IN_PERM]),
        g_nsa_q=f(g_nsa_q), g_nsa_k=f(g_nsa_k), cmp_k_pe=f(cmp_k_pe), cmp_k_w1=f(cmp_k_w1), cmp_k_w2=f(cmp_k_w2),
        cmp_v_pe=f(cmp_v_pe), cmp_v_w1=f(cmp_v_w1), cmp_v_w2=f(cmp_v_w2), rel_bias=f(rel_bias),
        w_gla_gate=f(w_gla_gate), b_gla_gate=f(b_gla_gate).reshape(1, 256), g_gla_o=f(g_gla_o), g_x_q=f(g_x_q), g_x_k=f(g_x_k),
        w_mem_kv=f(w_mem_kv), w_nsa_out=f(w_nsa_out), w_gla_out=f(w_gla_out), w_x_out=f(w_x_out), w_o=f(w_o),
        w_up=np.ascontiguousarray(w_up_p), conv_w=f(conv_w), conv_b=f(conv_b).reshape(1, DFF), w_down=f(w_down),
        cache_kv=cache_kv2)
    shared.update(_constants())
    in_maps = []
    for core in range(8):
        b, c = core // 4, core % 4
        npre = 48 - 16 * c
        xloc = np.zeros((64 * 128, D), np.float32)
        xloc[npre * 128:] = x_prompt[b, :(64 - npre) * 128] if False else x_prompt[b, (16 * (c + 1) - (64 - npre)) * 128:16 * (c + 1) * 128]
        m = dict(shared)
        m.update(_masks(npre))
        sl = slice(4 * core, 4 * core + 4)
        m.update(xloc=xloc, xs=x_sample[sl], cache_win=cache_win2[sl], state_gla=state_gla[sl], state_conv=state_conv[sl],
                 cache_mem=cache_mem2[sl], page_table=page_table[sl], mem=mem_prompt[b])
        in_maps.append(m)
    nc = _get_nc('full')
    res = run_bass_kernel_spmd(nc, in_maps, core_ids=list(range(8))).results
    r = lambda core, nm: np.asarray(res[core][nm], dtype=np.float32)
    y_p = np.stack([np.concatenate([r(4 * b + c, 'y_p') for c in range(4)], 0) for b in range(2)])
    rows_p = np.stack([np.concatenate([r(4 * b + c, 'rows_p') for c in range(4)], 0) for b in range(2)]).reshape(2, 8192, 4, 2, 64)
    win_p = np.stack([r(4 * b + 3, 'win_p') for b in range(2)]).reshape(2, 512, 2, 2, 64)
    gla_p = np.stack([r(4 * b + 3, 'gla_p') for b in range(2)])
    conv_p = np.stack([r(4 * b + 3, 'conv_p') for b in range(2)])
    memkv_p = np.stack([r(4 * b, 'memkv_p') for b in range(2)]).reshape(2, 256, 2, 4, 128)
    cat = lambda nm: np.concatenate([r(core, nm) for core in range(8)], 0)
    y_s = cat('y_s').reshape(32, 1, D)
    rows_s = cat('rows_s').reshape(32, 1, 4, 2, 64)
    win_s = cat('win_s').reshape(32, 512, 2, 2, 64)
    gla_s = cat('gla_s')
    conv_s = cat('conv_s')
    return (y_p, y_s, rows_p, win_p, gla_p, conv_p, memkv_p, rows_s, win_s, gla_s, conv_s)
```

```python
import numpy as np
import ml_dtypes
from contextlib import ExitStack
import concourse.bass as bass
import concourse.mybir as mybir
from concourse.bass_utils import run_bass_kernel_spmd

F32 = mybir.dt.float32
BF16 = mybir.dt.bfloat16
I32 = mybir.dt.int32
AF = mybir.ActivationFunctionType
ALU = mybir.AluOpType
AX = mybir.AxisListType

D = 1024
DFF = 2816
NFF = 22
EPS = 1e-6
TINY = 1e-30
NEG = -30000.0
NSLOT = 65
DMIN = -160
DMAX = 700
ND = DMAX - DMIN + 1
C0, C1, C2, C3, C4, C5, C6, CMG = 0, 512, 1024, 1320, 1832, 2344, 2856, 3368
W_IN_PERM = np.concatenate([np.arange(0, 512), np.arange(512, 1024), np.arange(1024, 1280),
                            np.arange(2328, 2344), np.arange(1280, 1304), np.arange(1304, 1816),
                            np.arange(1816, 2328), np.arange(2344, 2856), np.arange(2856, 3368),
                            np.arange(3368, 6440)])
FIRST_Q = 47
PREFIX_WIN_FROM = 43


class Buf:
    __slots__ = ('w', 'r', 'excl')

    def __init__(self):
        self.w = None
        self.r = {}
        self.excl = False


class T:
    def __init__(self, t):
        self.t = t
        self.b = Buf()

    def __getitem__(self, k):
        return self.t[k]


class KB:
    def __init__(self, nds=24):
        self.nc = bass.Bass('TRN2', target_bir_lowering=False)
        nc = self.nc
        self.es = ExitStack()
        self.eng = dict(pe=nc.tensor, act=nc.scalar, dve=nc.vector, pool=nc.gpsimd, sp=nc.sync)
        self.sem = {e: self.es.enter_context(nc.semaphore('s_' + e)) for e in ('pe', 'act', 'dve', 'pool')}
        self.cnt = dict.fromkeys(self.sem, 0)
        self.NDS = nds
        self.dsem = [self.es.enter_context(nc.semaphore('d%d' % i)) for i in range(nds)]
        self.dcnt = [0] * nds
        self.dnext = 0
        self.dnext_sw = 0
        self.waited = {e: {} for e in self.eng}
        self.nbuf = 0
        self.ninstr = 0

    def sb(self, shape, dt=F32):
        self.nbuf += 1
        return T(self.es.enter_context(self.nc.sbuf_tensor('sb%d' % self.nbuf, list(shape), dt)))

    def ps(self, shape, dt=F32):
        self.nbuf += 1
        t = T(self.es.enter_context(self.nc.psum_tensor('ps%d' % self.nbuf, list(shape), dt)))
        t.b.excl = True
        return t

    def din(self, name, shape, dt=F32):
        return self.nc.dram_tensor(name, list(shape), dt, kind="ExternalInput").ap()

    def dout(self, name, shape, dt=F32):
        return self.nc.dram_tensor(name, list(shape), dt, kind="ExternalOutput").ap()

    def dint(self, name, shape, dt=F32):
        return self.nc.dram_tensor(name, list(shape), dt, kind="Internal").ap()

    def _wait(self, e, dep):
        k, v = dep
        w = self.waited[e]
        if w.get(k, 0) >= v:
            return
        w[k] = v
        sem = self.sem[k] if isinstance(k, str) else self.dsem[k]
        self.eng[e].wait_ge(sem, v)
        self.ninstr += 1

    def _deps(self, e, reads, writes):
        deps = set()
        for b in reads:
            if b.w is not None:
                deps.add(b.w)
        for b in writes:
            if b.w is not None:
                deps.add(b.w)
            deps.update(b.r.values())
        for d in deps:
            if e == 'pe' and d[0] == 'pe':
                continue
            self._wait(e, d)

    def _mark(self, me, key, reads, writes):
        for b in reads:
            b.r[key] = me
        for b in writes:
            b.w = me
            b.r = {}

    def op(self, e, fn, reads=(), writes=()):
        reads = [x.b if isinstance(x, T) else x for x in reads]
        writes = [x.b if isinstance(x, T) else x for x in writes]
        writes = writes + [b for b in reads if b.excl and b not in writes]
        self._deps(e, reads, writes)
        ins = fn(self.eng[e])
        self.cnt[e] += 1
        self.ninstr += 1
        ins.then_inc(self.sem[e], 1)
        me = (e, self.cnt[e])
        self._mark(me, e, reads, writes)
        return me

    def dma(self, q, out, in_, reads=(), writes=(), fn=None, **kw):
        reads = [x.b if isinstance(x, T) else x for x in reads]
        writes = [x.b if isinstance(x, T) else x for x in writes]
        if q == 'pool':
            i = self.NDS - 8 + self.dnext_sw
            self.dnext_sw = (self.dnext_sw + 1) % 8
        else:
            i = self.dnext
            self.dnext = (i + 1) % (self.NDS - 8)
        if self.dcnt[i] > 0:
            self._wait(q, (i, self.dcnt[i]))
        self._deps(q, reads, writes)
        if fn is None:
            ins = self.eng[q].dma_start(out=out, in_=in_, **kw)
        else:
            ins = fn(self.eng[q])
        self.dcnt[i] += 16
        self.ninstr += 1
        ins.then_inc(self.dsem[i], 16)
        me = (i, self.dcnt[i])
        self._mark(me, i, reads, writes)
        return me

    def finish(self):
        for i in range(self.NDS):
            if self.dcnt[i] > 0:
                self._wait('sp', (i, self.dcnt[i]))
        for e in self.sem:
            if self.cnt[e] > 0:
                self._wait('sp', (e, self.cnt[e]))
        self.es.close()
        return self.nc


class MK:
    def __init__(self, n_prefix=47, n_own=17, n_samp=4, do_sample=True):
        self.k = k = KB()
        self.n_prefix, self.n_own, self.n_samp, self.do_sample = n_prefix, n_own, n_samp, do_sample
        import os
        self.stop = int(os.environ.get('MK_STOP', '99'))
        self.declare_io()
        self.alloc()
        self.setup()
        if self.stop >= 2:
            self.prompt_phase()
        if do_sample and self.stop >= 90:
            self.sample_phase()
        self.nc = k.finish()

    def mm(self, out, lhsT, rhs, start, stop, reads, writes):
        self.k.op('pe', lambda e: e.matmul(out, lhsT=lhsT, rhs=rhs, start=start, stop=stop, skip_group_check=True), reads, writes)

    def tr(self, out, in_, f32, reads, writes):
        idt = self.ident_f if f32 else self.ident_b
        self.k.op('pe', lambda e: e.transpose(out=out, in_=in_, identity=idt.t[:]), list(reads) + [idt], writes)

    def act(self, out, in_, func, reads, writes, **kw):
        self.k.op('act', lambda e: e.activation(out=out, in_=in_, func=func, **kw), reads, writes)

    def tt(self, out, in0, in1, op, reads, writes, eng='dve'):
        self.k.op(eng, lambda e: e.tensor_tensor(out=out, in0=in0, in1=in1, op=op), reads, writes)

    def ts(self, out, in0, s1, s2, op0, op1, reads, writes, eng='dve'):
        if op1 is None:
            self.k.op(eng, lambda e: e.tensor_scalar(out=out, in0=in0, scalar1=s1, scalar2=None, op0=op0), reads, writes)
        else:
            self.k.op(eng, lambda e: e.tensor_scalar(out=out, in0=in0, scalar1=s1, scalar2=s2, op0=op0, op1=op1), reads, writes)

    def stt(self, out, in0, scalar, in1, op0, op1, reads, writes):
        self.k.op('dve', lambda e: e.scalar_tensor_tensor(out=out, in0=in0, scalar=scalar, in1=in1, op0=op0, op1=op1), reads, writes)

    def cp(self, out, in_, reads, writes, eng='dve'):
        self.k.op(eng, lambda e: e.tensor_copy(out=out, in_=in_), reads, writes)

    def recip(self, ap, t):
        self.k.op('dve', lambda e: e.reciprocal(out=ap, in_=ap), [t], [t])

    def memset(self, ap, val, writes, eng='pool'):
        self.k.op(eng, lambda e: e.memset(ap, val), (), writes)

    def pa_next(self):
        self._pai = (self._pai + 1) % len(self.pa)
        return self.pa[self._pai]

    def pb_next(self):
        self._pbi = (self._pbi + 1) % len(self.pb)
        return self.pb[self._pbi]

    def rstd_from_ss(self, ss_ap, n, cnt, reads_t, out_t):
        self.act(out_t[:, 0:n], ss_ap, AF.Sqrt, list(reads_t) + [self.eps_t], [out_t], scale=1.0 / cnt, bias=self.eps_t[:, 0:1])
        self.recip(out_t[:, 0:n], out_t)

    def headnorm(self, src, H, Dh, reads):
        sq, ss, rs = self.hn_sq, self.hn_ss, self.hn_rs
        n = H * Dh
        self.act(sq[:, 0:n], src, AF.Square, reads, [sq])
        self.k.op('dve', lambda e: e.tensor_reduce(out=ss[:, 0:H], in_=sq[:, 0:n].rearrange('p (h d) -> p h d', h=H), axis=AX.X, op=ALU.add), [sq], [ss])
        self.rstd_from_ss(ss[:, 0:H], H, Dh, [ss], rs)
        self.tt(sq[:, 0:n].rearrange('p (h d) -> p h d', h=H), src.rearrange('p (h d) -> p h d', h=H),
                rs[:, 0:H].unsqueeze(2).to_broadcast([128, H, Dh]), ALU.mult, list(reads) + [rs], [sq])
        return sq

    def gelu_tanh(self, out, x, tmp, reads, writes, tmp_t):
        self.tt(tmp, x, x, ALU.mult, reads, [tmp_t])
        self.ts(tmp, tmp, 0.044715, 1.0, ALU.mult, ALU.add, [tmp_t], [tmp_t])
        self.tt(tmp, tmp, x, ALU.mult, list(reads) + [tmp_t], [tmp_t])
        self.act(tmp, tmp, AF.Sigmoid, [tmp_t], [tmp_t], scale=1.5957691216057308)
        self.tt(out, tmp, x, ALU.mult, list(reads) + [tmp_t], writes)

    def declare_io(self):
        k = self.k
        ns_ = self.n_samp
        i = {}
        i['xloc'] = k.din('xloc', [64 * 128, D])
        i['xs'] = k.din('xs', [ns_, D])
        i['cache_kv'] = k.din('cache_kv', [(2560 if self.do_sample else 2) * 128, 512])
        i['cache_win'] = k.din('cache_win', [ns_, 512, 256])
        i['state_gla'] = k.din('state_gla', [ns_, 4, 64, 128])
        i['state_conv'] = k.din('state_conv', [ns_, 2, DFF])
        i['cache_mem'] = k.din('cache_mem', [ns_, 256, 1024])
        i['page_table'] = k.din('page_table', [ns_, 64], I32)
        i['mem'] = k.din('mem', [256, D])
        i['g_mix'] = k.din('g_mix', [D]); i['g_mem'] = k.din('g_mem', [D]); i['g_ffn'] = k.din('g_ffn', [D])
        i['w_in'] = k.din('w_in', [D, 6440])
        i['g_nsa_q'] = k.din('g_nsa_q', [64]); i['g_nsa_k'] = k.din('g_nsa_k', [3, 64])
        for nm in ('k', 'v'):
            i['cmp_%s_pe' % nm] = k.din('cmp_%s_pe' % nm, [32, 64])
            i['cmp_%s_w1' % nm] = k.din('cmp_%s_w1' % nm, [32, 64, 64])
            i['cmp_%s_w2' % nm] = k.din('cmp_%s_w2' % nm, [64, 64])
        i['rel_bias'] = k.din('rel_bias', [32, 8])
        i['w_gla_gate'] = k.din('w_gla_gate', [16, 256]); i['b_gla_gate'] = k.din('b_gla_gate', [1, 256])
        i['g_gla_o'] = k.din('g_gla_o', [128]); i['g_x_q'] = k.din('g_x_q', [128]); i['g_x_k'] = k.din('g_x_k', [128])
        i['w_mem_kv'] = k.din('w_mem_kv', [D, 1024])
        i['w_nsa_out'] = k.din('w_nsa_out', [512, D]); i['w_gla_out'] = k.din('w_gla_out', [512, D]); i['w_x_out'] = k.din('w_x_out', [512, D])
        i['w_o'] = k.din('w_o', [D, D])
        i['w_up'] = k.din('w_up', [D, 2 * DFF])
        i['conv_w'] = k.din('conv_w', [3, DFF]); i['conv_b'] = k.din('conv_b', [1, DFF])
        i['w_down'] = k.din('w_down', [DFF, D])
        i['c_ident'] = k.din('c_ident', [128, 128]); i['c_U'] = k.din('c_U', [128, 128]); i['c_L'] = k.din('c_L', [128, 128])
        i['c_onesbd'] = k.din('c_onesbd', [128, 128])
        i['c_ohdr'] = k.din('c_ohdr', [33, ND]); i['c_ohda'] = k.din('c_ohda', [33, ND])
        i['c_relb'] = k.din('c_relb', [128, 3]); i['c_edge'] = k.din('c_edge', [128, 128], BF16)
        i['c_e0'] = k.din('c_e0', [128, 2])
        i['m_slotbias'] = k.din('m_slotbias', [2, 128, NSLOT])
        i['m_pm'] = k.din('m_pm', [2, 128, 136])
        i['m_cmask'] = k.din('m_cmask', [2, 1, 528])
        i['m_fbabs'] = k.din('m_fbabs', [2, 128, 136]); i['m_hs'] = k.din('m_hs', [128, 1])
        self.i = i
        o = {}
        o['y_p'] = k.dout('y_p', [16 * 128, D])
        o['rows_p'] = k.dout('rows_p', [16 * 128, 512])
        o['win_p'] = k.dout('win_p', [512, 256])
        o['gla_p'] = k.dout('gla_p', [4, 64, 128])
        o['conv_p'] = k.dout('conv_p', [2, DFF])
        o['memkv_p'] = k.dout('memkv_p', [256, 1024])
        o['y_s'] = k.dout('y_s', [ns_, D])
        o['rows_s'] = k.dout('rows_s', [ns_, 512])
        o['win_s'] = k.dout('win_s', [ns_, 512, 256])
        o['gla_s'] = k.dout('gla_s', [ns_, 4, 64, 128])
        o['conv_s'] = k.dout('conv_s', [ns_, 2, DFF])
        self.o = o
        s = {}
        s['w_in'] = k.dint('s_w_in', [128, 8, 6440], BF16)
        s['w_mem_kv'] = k.dint('s_w_mem_kv', [128, 8, 1024], BF16)
        s['w_nsa_out'] = k.dint('s_w_nsa_out', [128, 4, D], BF16)
        s['w_gla_out'] = k.dint('s_w_gla_out', [128, 4, D], BF16)
        s['w_x_out'] = k.dint('s_w_x_out', [128, 4, D], BF16)
        s['w_o'] = k.dint('s_w_o', [128, 8, D], BF16)
        s['w_up'] = k.dint('s_w_up', [128, 8, 2 * DFF], BF16)
        s['w_down'] = k.dint('s_w_down', [128, NFF, D], BF16)
        self.s = s
        self.sbuf_w = {nm: Buf() for nm in s}

    def alloc(self):
        k = self.k
        sb, ps = k.sb, k.ps
        self.pa = [ps([128, 512]) for _ in range(4)]
        self.pacc = [ps([128, 512]) for _ in range(2)]
        self.pb = [ps([128, 1024], BF16) for _ in range(2)]
        self._pai = self._pbi = 0
        self.ident_f = sb([128, 128]); self.ident_b = sb([128, 128], BF16)
        self.U = sb([128, 128]); self.L = sb([128, 128]); self.onesbd = sb([128, 128])
        self.I4 = sb([128, 4, 128], BF16)
        self.rbp = sb([33, 8]); self.rb31 = sb([32, 8])
        self.relb = sb([128, 3]); self.e0 = sb([128, 2])
        self.ones_row = sb([1, 128], BF16)
        self.ones_col = sb([128, 1])
        self.eps_t = sb([128, 1])
        self.slotbias = [sb([128, NSLOT]) for _ in range(2)]
        self.pm = [sb([128, 136]) for _ in range(2)]
        self.cmask = [sb([1, 528], BF16) for _ in range(2)]
        self.cmask_f = sb([1, 528])
        self.fbabs = [sb([128, 136]) for _ in range(2)]; self.hs = sb([128, 1])
        self.BT = sb([128, 2, 2, 512])
        self.BT512T = sb([128, 128], BF16)
        self.BC = sb([128, 8, 16])
        self.gains = Buf()
        self.gcol = sb([128, 3, 8])
        self.gq = sb([128, 64]); self.gk = sb([128, 3, 64]); self.ggo = sb([128, 128]); self.gxq = sb([128, 128]); self.gxk = sb([128, 128])
        self.gk0col = sb([128, 1])
        self.convw = sb([128, NFF, 3]); self.convb = sb([128, NFF])
        self.w1bd = [sb([128, 32, 128], BF16) for _ in range(2)]
        self.w2bd = [sb([128, 128], BF16) for _ in range(2)]
        self.b1 = [sb([128, 1]) for _ in range(2)]
        self.peT = sb([128, 2, 32], BF16)
        self.wg17 = sb([17, 256], BF16)
        self.stage = sb([128, 64])
        self.k_selT = sb([128, NSLOT * 128], BF16); self.b_ksel = [Buf() for _ in range(NSLOT)]
        self.v_sel = sb([128, NSLOT, 2, 65], BF16); self.b_vsel = [Buf() for _ in range(NSLOT)]
        self.k_winT = sb([128, 8 * 128], BF16); self.v_win = sb([128, 8, 2, 65], BF16); self.b_win = [Buf() for _ in range(8)]
        self.k_cmpT = sb([128, 528], BF16); self.b_kcmp = Buf()
        self.hgvT = sb([128, 640], BF16)
        self.v_cmp = sb([128, 5, 2, 64], BF16); self.b_vcmp = Buf()
        self.rawT = [sb([128, 16 + 512], BF16) for _ in range(2)]
        self.k_memT = sb([128, 4, 256], BF16); self.v_mem = sb([128, 2, 4, 129], BF16)
        self.S = sb([64, 4, 128]); self.Sb = sb([64, 4, 128], BF16)
        self.xt = [sb([128, D]) for _ in range(1)]
        self.big = sb([128, 2048])
        def view(ap):
            t = T(ap); t.b = self.big.b
            return t
        self.pc = view(self.big.t[:, :].rearrange('p (a b) -> p a b', a=4))
        self.x1 = view(self.big.t[:, 0:1024])
        self.hn_sq = view(self.big.t[:, 1024:2048])
        self.yt = self.hn_sq; self.memrows = self.hn_sq
        self.mtmp = view(self.big.t[:, 1536:2048].rearrange('p (a b) -> p a b', a=4))
        self.BTs = view(self.big.t[:, 0:512].rearrange('p (k o n) -> p k o n', k=2, o=2))
        self.qTs = sb([128, 128], BF16); self.I4s = sb([128, 128], BF16)
        self.hn_ss = sb([128, 8]); self.hn_rs = sb([128, 8])
        self.hb = sb([128, D], BF16)
        self.hT = sb([128, 8, 128], BF16)
        self.rows = sb([128, 4, 128]); self.winrows = sb([128, 2, 128])
        self.qn = sb([128, 4, 2, 64], BF16); self.qT = sb([128, 4, 128], BF16)
        self.gates = sb([128, 24])
        self.lrn = sb([128, 16], BF16); self.lrT = sb([17, 128], BF16)
        self.gqk = sb([128, 512], BF16); self.gqkT = sb([64, 8, 128], BF16)
        self.gv = sb([128, 512], BF16); self.gr = sb([128, 512], BF16)
        self.xqn = sb([128, 512], BF16); self.xqT = sb([128, 4, 128], BF16)
        self.mgT = sb([128, 24, 128], BF16)
        self.la = sb([128, 256]); self.e1 = sb([128, 256])
        self.ecb = sb([64, 4, 128]); self.encb = sb([64, 4, 128]); self.ecbl = sb([64, 4])
        self.qeT = sb([64, 4, 128], BF16); self.keT = sb([64, 4, 128], BF16)
        self.erc = sb([128, 256]); self.kl = sb([128, 256], BF16)
        self.Am = sb([128, 4, 128], BF16)
        self.ogla = sb([128, 512], BF16); self.oglaT = sb([128, 4, 128], BF16)
        self.psum8 = sb([128, 8]); self.prc8 = sb([128, 8])
        self.pg = sb([128, 2, 528]); self.imp = sb([128, 2, 136]); self.m16 = sb([128, 2, 16]); self.wk = sb([128, 136])
        self.neg = sb([128, 2, 136], BF16)
        self.negx = [[sb([128, 512], BF16) for _ in range(2)] for _ in range(2)]
        self.pcT = sb([128, 4, 128], BF16)
        self.pT = [sb([128, 512], BF16) for _ in range(4)]; self._pTi = 0
        self.oc = sb([128, 512]); self.osel = sb([128, 8, 64], BF16); self.owin = sb([128, 8, 64], BF16); self.den = sb([128, 8])
        self.onsa = sb([128, 512], BF16); self.onsaT = sb([128, 4, 128], BF16)
        self.pxT = sb([128, 2, 4, 128], BF16); self.ox = sb([128, 512], BF16); self.oxT = sb([128, 4, 128], BF16)
        self.mrg = view(self.big.t[:, 0:1024].rearrange('p (a b) -> p a b', a=8)); self.mrgT = sb([128, 8, 128], BF16)
        self.h2T = self.hT
        self.hist = sb([128, NFF, 2]); self.gx = [sb([128, 2, 130]) for _ in range(2)]; self.ub = [sb([128, 2, 128], BF16) for _ in range(2)]
        self.cva = sb([128, 2, 128]); self.cvb = sb([128, 2, 128]); self.cvc = sb([128, 2, 128]); self.g0col = sb([128, NFF])
        self.actT = sb([128, NFF, 128], BF16)
        self.hg = sb([128, 32], BF16); self.hx = sb([128, 32]); self.hx2 = sb([128, 32]); self.hxs = [self.hx, sb([128, 32])]; self.hx2s = [self.hx2, sb([128, 32])]; self.kc = sb([128, 32]); self.kc2 = sb([128, 32]); self.kcr = sb([128, 32])
        self.idx = sb([128, self.n_samp * 64], I32); self.ptb = sb([128, self.n_samp * 64], I32); self.iop = sb([128, 1], I32)
        self.page = [sb([128, 512]) for _ in range(2)]
        pcf = self.big.t[:, :]
        self.ohdr = T(pcf[0:33, 0:ND]); self.ohdr.b = self.pc.b
        self.ohda = T(pcf[0:33, ND:2 * ND]); self.ohda.b = self.pc.b
        self.NW = 4
        self.WAHEAD = 2
        self.wring = [sb([128, 2048], BF16) for _ in range(self.NW)]
        self.wq = []
        self.wissued = 0
        self.wnext = 0

    def wsched(self, tag, src, shape):
        self.wq.append((tag, src, shape))

    def _wview(self, r, shape):
        n = int(np.prod(shape[1:]))
        return r.t[:, 0:n].rearrange('p (a b) -> p a b', a=shape[1])

    def wget(self, tag):
        k = self.k
        i = self.wnext
        while self.stop < 99 and self.wq[i][0] != tag:
            self.wnext += 1; i = self.wnext; self.wissued = max(self.wissued, i)
        assert self.wq[i][0] == tag, (self.wq[i][0], tag)
        upto = min(len(self.wq), i + self.WAHEAD + 1)
        while self.wissued < upto:
            j = self.wissued
            tg, src, shape = self.wq[j]
            r = self.wring[j % self.NW]
            k.dma('sp', self._wview(r, shape), src, reads=[self.sbuf_w[tg.split(':')[0]]], writes=[r])
            self.wissued += 1
        self.wnext += 1
        r = self.wring[i % self.NW]
        return self._wview(r, self.wq[i][2]), r

    def sched_cols(self, nm, tag, c0, c1, nk=8):
        j = 0
        for a in range(c0, c1, 256):
            b = min(c1, a + 256)
            self.wsched('%s:%s:%d' % (nm, tag, j), self.s[nm][:, :, a:b], [128, nk, b - a])
            j += 1

    def sched_tile(self, own, need_win=True, hist_only=False):
        if own:
            self.sched_cols('w_in', 'c0', C0, C1)
        self.sched_cols('w_in', 'c1', C1, C2)
        if own or need_win:
            self.sched_cols('w_in', 'c2', C2, C3)
        else:
            self.sched_cols('w_in', 'c2lr', C2 + 256, C2 + 272)
        if own:
            self.sched_cols('w_in', 'c3', C3, C4)
        else:
            self.sched_cols('w_in', 'c3k', C3 + 256, C4)
        self.sched_cols('w_in', 'c4', C4, C5)
        if own:
            self.sched_cols('w_in', 'c5', C5, C6)
            self.sched_cols('w_in', 'c6', C6, CMG)
            self.sched_cols('w_in', 'mg', CMG, 6440)
            for nm in ('w_nsa_out', 'w_gla_out', 'w_x_out'):
                self.sched_cols(nm, 'o', 0, D, nk=4)
            self.sched_cols('w_o', 'o', 0, D)
            self.sched_cols('w_up', 'u', 0, 2 * DFF)
            for j in range(0 if hist_only else 11):
                self.wsched('w_down:d:%d' % j, self.s['w_down'][:, 2 * j:2 * j + 2, :], [128, 2, 1024])

    def setup(self):
        k, i, s = self.k, self.i, self.s

        def castw(nm, src3, nk, ncols):
            for c in range(nk):
                for c0 in range(0, ncols, 2048):
                    c1 = min(ncols, c0 + 2048)
                    k.dma('pool', s[nm][:, c, c0:c1], src3[:, c, c0:c1], writes=[self.sbuf_w[nm]])
        castw('w_mem_kv', i['w_mem_kv'].rearrange('(c p) n -> p c n', p=128), 8, 1024)
        w_in3 = i['w_in'].rearrange('(c p) n -> p c n', p=128)
        for c in range(8):
            k.dma('pool', s['w_in'][:, c, C1:C5], w_in3[:, c, C1:C5], writes=[self.sbuf_w['w_in']])
        self.late_casts = []
        for c in range(8):
            for (a_, b_) in ((C0, C1), (C5, C5 + 2048), (C5 + 2048, 6440)):
                self.late_casts.append(('w_in', s['w_in'][:, c, a_:b_], w_in3[:, c, a_:b_]))
        def castw_late(nm, src3, nk, ncols):
            for c in range(nk):
                for c0 in range(0, ncols, 2048):
                    c1 = min(ncols, c0 + 2048)
                    self.late_casts.append((nm, s[nm][:, c, c0:c1], src3[:, c, c0:c1]))
        for nm in ('w_nsa_out', 'w_gla_out', 'w_x_out'):
            castw_late(nm, i[nm].rearrange('(c p) n -> p c n', p=128), 4, D)
        castw_late('w_o', i['w_o'].rearrange('(c p) n -> p c n', p=128), 8, D)
        castw_late('w_up', i['w_up'].rearrange('(c p) n -> p c n', p=128), 8, 2 * DFF)
        castw_late('w_down', i['w_down'].rearrange('(c p) n -> p c n', p=128), NFF, D)

        def ld(t, src):
            k.dma('sp', t[:], src, writes=[t])
        ld(self.ident_f, i['c_ident']); ld(self.U, i['c_U']); ld(self.L, i['c_L']); ld(self.onesbd, i['c_onesbd'])
        ld(self.BT512T, i['c_edge'])
        ld(self.ohdr, i['c_ohdr']); ld(self.ohda, i['c_ohda']); ld(self.relb, i['c_relb']); ld(self.e0, i['c_e0'])
        for st in range(2):
            ld(self.slotbias[st], i['m_slotbias'][st]); ld(self.pm[st], i['m_pm'][st]); ld(self.fbabs[st], i['m_fbabs'][st])
            k.dma('sp', self.cmask_f[:], i['m_cmask'][st], writes=[self.cmask_f])
            self.cp(self.cmask[st][:], self.cmask_f[:], [self.cmask_f], [self.cmask[st]])
        ld(self.hs, i['m_hs'])
        self.cp(self.ident_b[:], self.ident_f[:], [self.ident_f], [self.ident_b])
        for g in range(4):
            self.cp(self.I4[:, g, :], self.ident_f[:], [self.ident_f], [self.I4])
        self.memset(self.ones_row[:], 1.0, [self.ones_row])
        self.memset(self.ones_col[:], 1.0, [self.ones_col])
        self.memset(self.eps_t[:], EPS, [self.eps_t])
        G = self.gains
        with k.nc.allow_non_contiguous_dma(reason="small one-time transposed parameter loads"):
            for j, nm in enumerate(('g_mix', 'g_mem', 'g_ffn')):
                k.dma('sp', self.gcol[:, j, :], i[nm].rearrange('(c p) -> p c', p=128), writes=[G])
            for t_ in range(3):
                k.dma('sp', self.convw[:, :, t_], i['conv_w'][t_].rearrange('(j p) -> p j', p=128), writes=[G])
            k.dma('sp', self.convb[:], i['conv_b'].rearrange('o (j p) -> p (o j)', p=128), writes=[G])
            for a, nm in enumerate(('k', 'v')):
                for kv in range(2):
                    k.dma('sp', self.stage[kv * 64:(kv + 1) * 64, a * 32:(a + 1) * 32], i['cmp_%s_pe' % nm].rearrange('j d -> d j'), writes=[self.stage])
            for kv in range(2):
                k.dma('sp', self.gk0col[kv * 64:(kv + 1) * 64, :], i['g_nsa_k'][0].rearrange('(d o) -> d o', o=1), writes=[G])
        for t, nm in ((self.gq, 'g_nsa_q'), (self.ggo, 'g_gla_o'), (self.gxq, 'g_x_q'), (self.gxk, 'g_x_k')):
            k.dma('sp', t[:], i[nm].partition_broadcast(128), writes=[G])
        for j in range(3):
            k.dma('sp', self.gk[:, j, :], i['g_nsa_k'][j].partition_broadcast(128), writes=[G])
        self.ts(self.gq[:], self.gq[:], 0.125, None, ALU.mult, None, [G], [G])
        self.ts(self.gxq[:], self.gxq[:], 128.0 ** -0.5, None, ALU.mult, None, [G], [G])
        self.cp(self.peT[:].rearrange('p a j -> p (a j)'), self.stage[:, 0:64], [self.stage], [self.peT])
        k.dma('sp', self.rbp[0:32, :], i['rel_bias'], writes=[self.rbp])
        k.dma('sp', self.rb31[:], i['rel_bias'][31].partition_broadcast(32), writes=[self.rb31])
        self.tt(self.rbp[0:32, :], self.rbp[0:32, :], self.rb31[:], ALU.subtract, [self.rbp, self.rb31], [self.rbp])
        self.memset(self.rbp[32:33, :], NEG, [self.rbp])
        for a, nm in enumerate(('k', 'v')):
            self.memset(self.w1bd[a][:], 0.0, [self.w1bd[a]])
            self.memset(self.w2bd[a][:], 0.0, [self.w2bd[a]])
            for kv in range(2):
                k.dma('pool', self.w1bd[a][kv * 64:(kv + 1) * 64, :, kv * 64:(kv + 1) * 64], i['cmp_%s_w1' % nm].rearrange('j d e -> d j e'), writes=[self.w1bd[a]])
                k.dma('pool', self.w2bd[a][kv * 64:(kv + 1) * 64, kv * 64:(kv + 1) * 64], i['cmp_%s_w2' % nm], writes=[self.w2bd[a]])
            p = self.pa_next()
            for j in range(32):
                self.mm(p[:, 0:2], self.w1bd[a][:, j, :], self.peT[:, a, j:j + 1].to_broadcast([128, 2]), j == 0, j == 31, [self.w1bd[a], self.peT], [p])
            self.cp(self.b1[a][:], p[:, 0:1], [p], [self.b1[a]])
        k.dma('pool', self.wg17[0:16, :], i['w_gla_gate'], writes=[self.wg17])
        k.dma('pool', self.wg17[16:17, :], i['b_gla_gate'], writes=[self.wg17])
        self.memset(self.lrT[:], 1.0, [self.lrT])
        zb = Buf()
        for t in (self.k_selT, self.k_winT, self.k_cmpT, self.v_sel, self.v_win):
            self.memset(t[:], 0.0, [zb])
        self.memset(self.v_sel[:, :, :, 64:65], 1.0, [zb])
        self.memset(self.v_win[:, :, :, 64:65], 1.0, [zb])
        self.memset(self.v_cmp[:], 0.0, [zb])
        for t in (self.hgvT, self.pg, self.hist, self.S, self.Sb, self.rawT[0], self.rawT[1], self.v_mem, self.neg, self.k_memT, self.pcT, self.negx[0][0], self.negx[0][1], self.negx[1][0], self.negx[1][1]):
            self.memset(t[:], 0.0, [t])
        self.memset(self.v_mem[:, :, :, 128:129], 1.0, [self.v_mem])
        for b in self.b_ksel + self.b_vsel + self.b_win + [self.b_kcmp, self.b_vcmp]:
            b.w = zb.w
        self.build_bias_tiles()

    def build_bias_tiles(self):
        for oi, off in enumerate((0, 128)):
            for half in range(2):
                p = self.pa_next()
                for ql in range(64):
                    q = half * 64 + ql
                    j0 = DMAX - off - q
                    self.mm(p[:, ql * 8:(ql + 1) * 8], self.ohdr[:, j0:j0 + 128], self.rbp[:, :], True, True, [self.ohdr, self.rbp], [p])
                src = p[:, :].rearrange('p (q h) -> p h q', h=8)
                for kv in range(2):
                    self.cp(self.BT[:, kv, oi, :].rearrange('p (g q) -> p g q', g=4)[:, :, half * 64:(half + 1) * 64],
                            src[:, kv * 4:(kv + 1) * 4, :], [p], [self.BT])
        p = self.pa_next()
        for uu in range(16):
            i0 = (113 - 16 * uu) - DMIN
            self.mm(p[:, uu * 8:(uu + 1) * 8], self.ohda[:, i0:i0 + 128], self.rbp[:, :], True, True, [self.ohda, self.rbp], [p])
        self.cp(self.BC[:], p[:, 0:128].rearrange('p (u h) -> p h u', h=8), [p], [self.BC])

    def load_x(self, xt, src):
        self.k.dma('sp', xt[:], src, writes=[xt])

    def norm_T(self, xt, gi, outT):
        sq, ss, rs = self.hn_sq, self.hn_ss, self.hn_rs
        self.act(sq[:], xt[:], AF.Square, [xt], [sq, ss], accum_out=ss[:, 0:1])
        self.rstd_from_ss(ss[:, 0:1], 1, D, [ss], rs)
        self.ts(self.hb[:], xt[:], rs[:, 0:1], None, ALU.mult, None, [xt, rs], [self.hb])
        p = self.pb_next()
        for c in range(8):
            self.tr(p[:, c * 128:(c + 1) * 128], self.hb[:, c * 128:(c + 1) * 128], False, [self.hb], [p])
        self.tt(outT[:], p[:, :].rearrange('p (a b) -> p a b', a=8), self.gcol[:, gi, :].unsqueeze(2).to_broadcast([128, 8, 128]), ALU.mult,
                [p, self.gains], [outT])

    def proj(self, nm, tag, ncols, srcT=None):
        srcT = srcT or self.hT
        p = self.pa_next()
        j = 0
        for a in range(0, ncols, 256):
            b = min(ncols, a + 256)
            w, wb = self.wget('%s:%s:%d' % (nm, tag, j))
            for c in range(8):
                self.mm(p[:, a:b], srcT[:, c, :], w[:, c, :], c == 0, c == 7, [srcT, wb], [p])
            j += 1
        return p

    def a2_rows(self, slot, rows, gi=0):
        p = self.pa_next()
        for j, sl in enumerate((0, 1, 2)):
            self.tr(p[:, j * 128:(j + 1) * 128], rows[:, sl, :], True, [rows], [p])
        self.cp(self.rawT[0][:, 16 + gi * 128:16 + (gi + 1) * 128], p[:, 0:128], [p], [self.rawT[0]])
        self.act(self.rawT[1][:, 16 + gi * 128:16 + (gi + 1) * 128], p[:, 128:256], AF.Copy, [p], [self.rawT[1]])
        self.cp(self.k_selT[:, slot * 128:(slot + 1) * 128], p[:, 256:384], [p], [self.b_ksel[slot]])
        self.cp(self.v_sel[:, slot, :, 0:64], rows[:, 3, :].rearrange('p (k d) -> p k d', k=2), [rows], [self.b_vsel[slot]], eng='pool')

    def a2_win(self, slot, winrows):
        r = slot % 8
        p = self.pa_next()
        self.tr(p[:, 0:128], winrows[:, 0, :], True, [winrows], [p])
        self.cp(self.k_winT[:, r * 128:(r + 1) * 128], p[:, 0:128], [p], [self.b_win[r]])
        self.cp(self.v_win[:, r, :, 0:64], winrows[:, 1, :].rearrange('p (k d) -> p k d', k=2), [winrows], [self.b_win[r]], eng='pool')

    def compress(self, slot0, G=1):
        cc0 = 8 * slot0
        n8 = 8 * G

        def chain(a):
            hx, hx2 = self.hxs[a], self.hx2s[a]
            p = self.pa_next()
            n = 0
            for r in range(2):
                for j in range(16):
                    st = 16 * r + j
                    self.mm(p[:, 0:n8], self.w1bd[a][:, st, :], self.rawT[a][:, st:st + 16 * (n8 - 1) + 1:16], n == 0, n == 31, [self.w1bd[a], self.rawT[a]], [p])
                    n += 1
            self.cp(self.rawT[a][:, 0:16], self.rawT[a][:, G * 128:G * 128 + 16], [self.rawT[a]], [self.rawT[a]], eng='pool')
            self.act(hx[:, 0:n8], p[:, 0:n8], AF.Identity, [p, self.b1[a]], [hx], bias=self.b1[a][:, 0:1])
            yield
            x, t = hx[:, 0:n8], hx2[:, 0:n8]
            self.tt(t, x, x, ALU.mult, [hx], [hx2])
            yield
            self.ts(t, t, 0.044715, 1.0, ALU.mult, ALU.add, [hx2], [hx2])
            yield
            self.tt(t, t, x, ALU.mult, [hx, hx2], [hx2])
            yield
            self.act(t, t, AF.Sigmoid, [hx2], [hx2], scale=1.5957691216057308)
            yield
            if a == 0:
                self.tt(self.hg[:, 0:n8], t, x, ALU.mult, [hx, hx2], [self.hg])
                yield
                p2 = self.pa_next()
                self.mm(p2[:, 0:n8], self.w2bd[0][:], self.hg[:, 0:n8], True, True, [self.w2bd[0], self.hg], [p2])
                yield
                self.cp(self.kc[:, 0:n8], p2[:, 0:n8], [p2], [self.kc])
                yield
                self.tt(self.kc2[:, 0:n8], self.kc[:, 0:n8], self.kc[:, 0:n8], ALU.mult, [self.kc], [self.kc2])
                yield
                p3 = self.pa_next()
                self.mm(p3[:, 0:n8], self.onesbd[:], self.kc2[:, 0:n8], True, True, [self.onesbd, self.kc2], [p3])
                yield
                self.rstd_from_ss(p3[:, 0:n8], n8, 64, [p3], self.kcr)
                yield
                self.tt(self.kc[:, 0:n8], self.kc[:, 0:n8], self.kcr[:, 0:n8], ALU.mult, [self.kc, self.kcr], [self.kc])
                yield
                self.ts(self.k_cmpT[:, cc0:cc0 + n8], self.kc[:, 0:n8], self.gk0col[:, 0:1], None, ALU.mult, None, [self.kc, self.gains], [self.b_kcmp])
                yield
            else:
                self.tt(self.hgvT[:, cc0:cc0 + n8], t, x, ALU.mult, [hx, hx2], [self.hgvT])
                yield
                for g in range(cc0 // 128, (cc0 + n8 - 1) // 128 + 1):
                    p2 = self.pa_next()
                    self.mm(p2[:, 0:128], self.hgvT[:, g * 128:(g + 1) * 128], self.w2bd[1][:], True, True, [self.hgvT, self.w2bd[1]], [p2])
                    yield
                    self.cp(self.v_cmp[:, g, :, :], p2[:, 0:128].rearrange('p (k d) -> p k d', k=2), [p2], [self.b_vcmp])
                    yield

        gens = [chain(0), chain(1)]
        while gens:
            for g_ in list(gens):
                try:
                    next(g_)
                except StopIteration:
                    gens.remove(g_)

    def la_compute(self, rowmask_col):
        p = self.pb_next()
        self.tr(p[0:16, 0:128], self.lrn[:], False, [self.lrn], [p])
        self.cp(self.lrT[0:16, :], p[0:16, 0:128], [p], [self.lrT])
        pz = self.pa_next()
        self.mm(pz[:, 0:256], self.lrT[:], self.wg17[:], True, True, [self.lrT, self.wg17], [pz])
        self.act(self.e1[:], pz[:, 0:256], AF.Exp, [pz], [self.e1], scale=-1.0)
        self.act(self.e1[:], self.e1[:], AF.Ln, [self.e1], [self.e1], bias=1.0)
        self.ts(self.la[:], self.e1[:], rowmask_col, -1.0 / 16.0, ALU.mult, ALU.mult, [self.e1, self.e0], [self.la])

    def gla_state_update(self, S, Sb, ecbl_ready):
        prc = self.pa_next()
        self.mm(prc[:, 0:256], self.L[:], self.la[:], True, True, [self.L, self.la], [prc])
        self.act(self.erc[:], prc[:, 0:256], AF.Exp, [prc], [self.erc])
        self.tt(self.kl[:], self.gqk[:, 256:512], self.erc[:], ALU.mult, [self.gqk, self.erc], [self.kl])
        if not ecbl_ready:
            pl = self.pa_next()
            for h in range(4):
                self.mm(pl[0:64, 2 * h:2 * h + 2], self.la[:, h * 64:(h + 1) * 64], self.ones_col[:, 0:1].to_broadcast([128, 2]), True, True, [self.la, self.ones_col], [pl])
            self.act(self.ecbl[:], pl[0:64, 0:8:2], AF.Exp, [pl], [self.ecbl])
        pu = self.pa_next()
        for h in range(4):
            self.mm(pu[0:64, h * 128:(h + 1) * 128], self.kl[:, h * 64:(h + 1) * 64], self.gv[:, h * 128:(h + 1) * 128], True, True, [self.kl, self.gv], [pu])
        for h in range(4):
            self.stt(S[:, h, :], S[:, h, :], self.ecbl[:, h:h + 1], pu[0:64, h * 128:(h + 1) * 128], ALU.mult, ALU.add, [S, self.ecbl, pu], [S])
        self.cp(Sb[:], S[:], [S], [Sb])

    def stage_a(self, slot, xt, own, need_win, rows_out=None, win_out=None, gi=0, flush=True):
        k = self.k
        self.norm_T(xt, 0, self.hT)
        if own:
            p = self.proj('w_in', 'c0', 512)
            sq = self.headnorm(p[:, :], 8, 64, [p])
            self.tt(self.qn[:].rearrange('p g k d -> p k g d'), sq[:, 0:512].rearrange('p (k g d) -> p k g d', k=2, g=4),
                    self.gq[:].unsqueeze(1).unsqueeze(1).to_broadcast([128, 2, 4, 64]), ALU.mult, [sq, self.gains], [self.qn])
            pq = self.pb_next()
            for g in range(4):
                self.tr(pq[:, g * 128:(g + 1) * 128], self.qn[:, g, :, :].rearrange('p k d -> p (k d)'), False, [self.qn], [pq])
            self.cp(self.qT[:].rearrange('p a b -> p (a b)'), pq[:, 0:512], [pq], [self.qT])
        p = self.proj('w_in', 'c1', 512)
        self.act(self.rows[:, 0:2, :].rearrange('p a b -> p (a b)'), p[:, 0:256], AF.Copy, [p], [self.rows])
        self.act(self.rows[:, 3, :], p[:, 384:512], AF.Copy, [p], [self.rows])
        sq = self.headnorm(p[:, 256:384], 2, 64, [p])
        self.tt(self.rows[:, 2, :].rearrange('p (h d) -> p h d', h=2), sq[:, 0:128].rearrange('p (h d) -> p h d', h=2),
                self.gk[:, 1, :].unsqueeze(1).to_broadcast([128, 2, 64]), ALU.mult, [sq, self.gains], [self.rows])
        if rows_out is not None:
            k.dma('sp', rows_out[0], self.rows[rows_out[1], :, :].rearrange('p a b -> p (a b)'), reads=[self.rows])
        self.a2_rows(slot, self.rows, gi)
        if flush:
            self.compress(slot - gi, gi + 1)
        lr0 = 256
        if own or need_win:
            p = self.proj('w_in', 'c2', 296)
        else:
            p = self.proj('w_in', 'c2lr', 16)
            lr0 = 0
        if need_win:
            sq = self.headnorm(p[:, 0:128], 2, 64, [p])
            self.tt(self.winrows[:, 0, :].rearrange('p (h d) -> p h d', h=2), sq[:, 0:128].rearrange('p (h d) -> p h d', h=2),
                    self.gk[:, 2, :].unsqueeze(1).to_broadcast([128, 2, 64]), ALU.mult, [sq, self.gains], [self.winrows])
            self.act(self.winrows[:, 1, :], p[:, 128:256], AF.Copy, [p], [self.winrows])
            if win_out is not None:
                k.dma('sp', win_out[0], self.winrows[win_out[1], :, :].rearrange('p a b -> p (a b)'), reads=[self.winrows])
            self.a2_win(slot, self.winrows)
        self.cp(self.lrn[:], p[:, lr0:lr0 + 16], [p], [self.lrn])
        if own:
            self.act(self.gates[:], p[:, 272:296], AF.Sigmoid, [p], [self.gates])
        if own:
            p = self.proj('w_in', 'c3', 512)
            self.act(self.gqk[:, 0:256], p[:, 0:256], AF.Copy, [p], [self.gqk], scale=0.125)
            self.cp(self.gqk[:, 256:512], p[:, 256:512], [p], [self.gqk])
        else:
            p = self.proj('w_in', 'c3k', 256)
            self.cp(self.gqk[:, 256:512], p[:, 0:256], [p], [self.gqk])
        p = self.proj('w_in', 'c4', 512)
        self.act(self.gv[:], p[:, :], AF.Copy, [p], [self.gv])

    def tr4(self, src, dstT):
        p = self.pb_next()
        for j in range(4):
            self.tr(p[:, j * 128:(j + 1) * 128], src[:, j * 128:(j + 1) * 128], False, [src], [p])
        self.cp(dstT[:].rearrange('p a b -> p (a b)'), p[:, 0:512], [p], [dstT])

    def gla_tile(self, S, Sb, rowmask_col):
        self.la_compute(rowmask_col)
        p = self.pb_next()
        for j in range(8):
            self.tr(p[0:64, j * 128:(j + 1) * 128], self.gqk[:, j * 64:(j + 1) * 64], False, [self.gqk], [p])
        self.cp(self.gqkT[:].rearrange('p a b -> p (a b)'), p[0:64, :], [p], [self.gqkT])
        pc = self.pa_next()
        for h in range(4):
            self.mm(pc[0:64, h * 128:(h + 1) * 128], self.la[:, h * 64:(h + 1) * 64], self.U[:], True, True, [self.la, self.U], [pc])
        self.act(self.ecb[:].rearrange('p a b -> p (a b)'), pc[0:64, :], AF.Exp, [pc], [self.ecb])
        self.act(self.encb[:].rearrange('p a b -> p (a b)'), pc[0:64, :], AF.Exp, [pc], [self.encb], scale=-1.0)
        self.cp(self.ecbl[:], self.ecb[:, :, 127], [self.ecb], [self.ecbl])
        self.tt(self.qeT[:], self.gqkT[:, 0:4, :], self.ecb[:], ALU.mult, [self.gqkT, self.ecb], [self.qeT])
        self.tt(self.keT[:], self.gqkT[:, 4:8, :], self.encb[:], ALU.mult, [self.gqkT, self.encb], [self.keT])
        pA = self.pa_next()
        for h in range(4):
            self.mm(pA[:, h * 128:(h + 1) * 128], self.keT[:, h, :], self.qeT[:, h, :], True, True, [self.keT, self.qeT], [pA])
        self.tt(self.Am[:], pA[:, :].rearrange('p (h t) -> p h t', h=4), self.U[:].unsqueeze(1).to_broadcast([128, 4, 128]), ALU.mult, [pA, self.U], [self.Am])
        po = self.pa_next()
        for h in range(4):
            self.mm(po[:, h * 128:(h + 1) * 128], self.Am[:, h, :], self.gv[:, h * 128:(h + 1) * 128], h == 0, False, [self.Am, self.gv], [po])
            self.mm(po[:, h * 128:(h + 1) * 128], self.qeT[:, h, :], Sb[:, h, :], False, h == 3, [self.qeT, Sb], [po])
        sq = self.headnorm(po[:, :], 4, 128, [po])
        self.tt(sq[:, 0:512].rearrange('p (h d) -> p h d', h=4), sq[:, 0:512].rearrange('p (h d) -> p h d', h=4),
                self.ggo[:].unsqueeze(1).to_broadcast([128, 4, 128]), ALU.mult, [sq, self.gains], [sq])
        self.tt(self.ogla[:], sq[:, 0:512], self.gr[:], ALU.mult, [sq, self.gr], [self.ogla])
        self.tr4(self.ogla, self.oglaT)
        self.gla_state_update(S, Sb, True)

    def nsa_tile(self, sq_, mset, nq=128):
        k = self.k
        sq = sq_
        ncc = min(8 * (sq + 1), 512)
        nb = 2 * (sq + 1)
        ngrp = (ncc + 127) // 128
        pacc_c = self.pacc[0]
        for kv in range(2):
            rs = slice(kv * 64, (kv + 1) * 64)
            banks = []
            for g in range(4):
                p = self.pa_next()
                banks.append(p)
                self.mm(p[:, 0:ncc], self.qT[rs, g, :], self.k_cmpT[rs, 0:ncc], True, False, [self.qT, self.b_kcmp], [p])
                self.mm(p[:, 0:ncc], self.ones_row[:, :], self.cmask[mset][:, 0:ncc], False, True, [self.ones_row, self.cmask[mset]], [p])
            w0 = 8 * (sq - 1)
            w1 = min(8 * (sq + 1), ncc)
            for g in range(4):
                h = kv * 4 + g
                p = banks[g]
                self.tt(p[:, w0:w1], p[:, w0:w1], self.BC[:, h, 0:w1 - w0], ALU.add, [p, self.BC], [p])
                self.act(self.pc[:, g, 0:ncc], p[:, 0:ncc], AF.Exp, [p], [self.pc, self.psum8], accum_out=self.psum8[:, h:h + 1])
            hs = slice(kv * 4, kv * 4 + 4)
            self.ts(self.prc8[:, hs], self.psum8[:, hs], TINY, None, ALU.max, None, [self.psum8], [self.prc8])
            self.recip(self.prc8[:, hs], self.prc8)
            self.tt(self.pc[:, :, 0:ncc], self.pc[:, :, 0:ncc], self.prc8[:, hs].unsqueeze(2).to_broadcast([128, 4, ncc]), ALU.mult, [self.pc, self.prc8], [self.pc])
            k.op('dve', lambda e: e.tensor_reduce(out=self.pg[:, kv, 0:ncc], in_=self.pc[:, :, 0:ncc].rearrange('p g c -> p c g'), axis=AX.X, op=ALU.add), [self.pc], [self.pg])
            for g_ in range(ngrp):
                w = min(128, ncc - g_ * 128)
                p = self.pa_next()
                for g in range(4):
                    self.tr(p[0:w, g * 128:(g + 1) * 128], self.pc[:, g, g_ * 128:g_ * 128 + w], True, [self.pc], [p])
                self.cp(self.pcT[0:w, :, :].rearrange('p a b -> p (a b)'), p[0:w, 0:512], [p], [self.pcT])
                for g in range(4):
                    h = kv * 4 + g
                    self.mm(pacc_c[:, h * 64:(h + 1) * 64], self.pcT[0:w, g, :], self.v_cmp[0:w, g_, kv, :], (kv == 0 and g_ == 0 and g == 0),
                            (kv == 1 and g_ == ngrp - 1 and g == 3), [self.pcT, self.b_vcmp], [pacc_c])
        self.cp(self.oc[:], pacc_c[:, :], [pacc_c], [self.oc])
        if ncc < 528:
            self.memset(self.pg[:, :, ncc:528], 0.0, [self.pg], eng='dve')
        k.op('dve', lambda e: e.tensor_reduce(out=self.imp[:, :, 0:nb], in_=self.pg[:, :, 0:4 * nb].rearrange('p k (j f) -> p k j f', f=4), axis=AX.X, op=ALU.add), [self.pg], [self.imp])
        self.tt(self.imp[:, :, 0:nb], self.imp[:, :, 0:nb], self.pg[:, :, 4:4 * nb + 4:4], ALU.add, [self.imp, self.pg], [self.imp])
        self.tt(self.imp[:, :, 0:nb], self.imp[:, :, 0:nb], self.fbabs[mset][:, 0:nb].unsqueeze(1).to_broadcast([128, 2, nb]), ALU.add, [self.imp, self.fbabs[mset]], [self.imp])
        self.tt(self.imp[:, :, nb - 3:nb], self.imp[:, :, nb - 3:nb], self.relb[:].unsqueeze(1).to_broadcast([128, 2, 3]), ALU.add, [self.imp, self.relb], [self.imp])
        for kv in range(2):
            k.op('dve', lambda e: e.max(out=self.m16[:, kv, 0:8], in_=self.imp[:, kv, 0:nb]), [self.imp], [self.m16])
            k.op('dve', lambda e: e.match_replace(out=self.wk[:, 0:nb], in_to_replace=self.m16[:, kv, 0:8], in_values=self.imp[:, kv, 0:nb], imm_value=-3.0e38), [self.imp, self.m16], [self.wk])
            k.op('dve', lambda e: e.max(out=self.m16[:, kv, 8:16], in_=self.wk[:, 0:nb]), [self.wk], [self.m16])
            self.ts(self.wk[:, 0:nb], self.imp[:, kv, 0:nb], self.m16[:, kv, 15:16], -NEG, ALU.is_ge, ALU.mult, [self.imp, self.m16], [self.wk])
            self.tt(self.neg[:, kv, 0:nb], self.wk[:, 0:nb], self.pm[mset][:, 0:nb], ALU.add, [self.wk, self.pm[mset]], [self.neg])
        self.mg_proj()
        NQ = 4 * nq
        if nq == 128:
            qrhs = [self.qT[kv * 64:(kv + 1) * 64, :, :].rearrange('p g q -> p (g q)') for kv in range(2)]
            I4f, I4t = self.I4[:].rearrange('p g q -> p (g q)'), self.I4
            BTv = lambda kv, oi: self.BT[:, kv, oi, :]
            BTt = self.BT
        else:
            self.cp(self.qTs[:, 0:NQ].rearrange('p (g q) -> p g q', g=4), self.qT[:, :, 0:nq], [self.qT], [self.qTs])
            self.cp(self.I4s[:, 0:NQ].rearrange('p (g q) -> p g q', g=4), self.I4[:, :, 0:nq], [self.I4], [self.I4s], eng='pool')
            for kv in range(2):
                for oi in range(2):
                    self.cp(self.BTs[:, kv, oi, 0:NQ].rearrange('p (g q) -> p g q', g=4),
                            self.BT[:, kv, oi, :].rearrange('p (g q) -> p g q', g=4)[:, :, 0:nq], [self.BT], [self.BTs], eng='pool')
            qrhs = [self.qTs[kv * 64:(kv + 1) * 64, 0:NQ] for kv in range(2)]
            I4f, I4t = self.I4s[:, 0:NQ], self.I4s
            BTv = lambda kv, oi: self.BTs[:, kv, oi, 0:NQ]
            BTt = self.BTs
        for br in (1, 0):
            kts = list(range(0, sq + 1)) if br == 0 else [t for t in range(sq - 4, sq + 1) if t >= 0]
            ngx = [None, None]

            def stage1(kt):
                banks = []
                if br == 0 and kt % 4 == 0 and kt < sq:
                    nblk = min(8, 2 * sq - 2 * kt)
                    for kv in range(2):
                        t_ = self.negx[kv][(kt // 4) % 2]
                        ngx[kv] = t_
                        self.cp(t_[:, 0:nblk * 64].rearrange('p (j s) -> p j s', s=64), self.neg[:, kv, 2 * kt:2 * kt + nblk].unsqueeze(2).to_broadcast([128, nblk, 64]),
                                [self.neg], [t_], eng='pool')
                off = 128 * (sq - kt)
                masked = (br == 0 and kt < sq)
                edge = (br == 1 and off == 512)
                for kv in range(2):
                    rs = slice(kv * 64, (kv + 1) * 64)
                    p = self.pa_next()
                    if br == 0:
                        kT, kb = self.k_selT[rs, kt * 128:(kt + 1) * 128], self.b_ksel[kt]
                    else:
                        r = kt % 8
                        kT, kb = self.k_winT[rs, r * 128:(r + 1) * 128], self.b_win[r]
                    self.mm(p[:, 0:NQ], kT, qrhs[kv], True, not (masked or edge), [kb, self.qT, self.qTs], [p])
                    banks.append(p)
                for kv in range(2):
                    p = banks[kv]
                    if masked:
                        t_ = ngx[kv]
                        self.mm(p[:, 0:NQ], t_[:, (kt % 4) * 128:(kt % 4 + 1) * 128], I4f, False, True, [t_, I4t], [p])
                    if edge:
                        self.mm(p[:, 0:NQ], self.BT512T[:, :], I4f, False, True, [self.BT512T, I4t], [p])
                    if off in (0, 128):
                        self.tt(p[:, 0:NQ], p[:, 0:NQ], BTv(kv, off // 128), ALU.add, [p, BTt], [p])
                return banks

            def stage2(n, kt, banks):
                for kv in range(2):
                    p = banks[kv]
                    pT = self.pT[self._pTi]
                    self._pTi = (self._pTi + 1) % len(self.pT)
                    self.act(pT[:, 0:NQ], p[:, 0:NQ], AF.Exp, [p, self.slotbias[mset]], [pT], bias=self.slotbias[mset][:, kt:kt + 1])
                    if br == 0:
                        vv, vb = self.v_sel[:, kt, kv, :], self.b_vsel[kt]
                    else:
                        r = kt % 8
                        vv, vb = self.v_win[:, r, kv, :], self.b_win[r]
                    pacc = self.pacc[kv]
                    for g in range(4):
                        self.mm(pacc[0:nq, g * 65:(g + 1) * 65], pT[:, g * nq:(g + 1) * nq], vv, (n == 0 and g == 0), (n == len(kts) - 1 and g == 3), [pT, vb], [pacc])

            nxt = stage1(kts[0])
            for n, kt in enumerate(kts):
                cur = nxt
                if n + 1 < len(kts):
                    nxt = stage1(kts[n + 1])
                stage2(n, kt, cur)
            dst = self.osel if br == 0 else self.owin
            for kv in range(2):
                pacc = self.pacc[kv]
                acc3 = pacc[0:nq, 0:260].rearrange('p (g d) -> p g d', g=4)
                hs = slice(kv * 4, kv * 4 + 4)
                self.ts(self.den[0:nq, hs], acc3[:, :, 64], TINY, None, ALU.max, None, [pacc], [self.den])
                self.recip(self.den[0:nq, hs], self.den)
                self.tt(dst[0:nq, hs, :], acc3[:, :, 0:64], self.den[0:nq, hs].unsqueeze(2).to_broadcast([nq, 4, 64]), ALU.mult, [pacc, self.den], [dst])
        g3 = self.gates[:].rearrange('p (h i) -> p h i', i=3)
        oc3 = self.oc[:].rearrange('p (h d) -> p h d', h=8)
        bc = lambda i: g3[:, :, i].unsqueeze(2).to_broadcast([128, 8, 64])
        self.tt(oc3, oc3, bc(0), ALU.mult, [self.oc, self.gates], [self.oc])
        self.tt(self.osel[:], self.osel[:], bc(1), ALU.mult, [self.osel, self.gates], [self.osel])
        self.tt(self.owin[:], self.owin[:], bc(2), ALU.mult, [self.owin, self.gates], [self.owin])
        self.tt(oc3, oc3, self.osel[:], ALU.add, [self.oc, self.osel], [self.oc])
        self.tt(self.onsa[:].rearrange('p (h d) -> p h d', h=8), oc3, self.owin[:], ALU.add, [self.oc, self.owin], [self.onsa])
        self.tr4(self.onsa, self.onsaT)

    def xatt_tile(self):
        for mt in range(2):
            p = self.pa_next()
            for h in range(4):
                self.mm(p[:, h * 128:(h + 1) * 128], self.k_memT[:, h, mt * 128:(mt + 1) * 128], self.xqT[:, h, :], True, True, [self.k_memT, self.xqT], [p])
            self.act(self.pxT[:, mt, :, :].rearrange('p a b -> p (a b)'), p[:, :], AF.Exp, [p], [self.pxT])
        for hp in range(2):
            p = self.pa_next()
            for hh in range(2):
                h = hp * 2 + hh
                for mt in range(2):
                    self.mm(p[:, hh * 129:(hh + 1) * 129], self.pxT[:, mt, h, :], self.v_mem[:, mt, h, :], (hh == 0 and mt == 0), (hh == 1 and mt == 1), [self.pxT, self.v_mem], [p])
            a3 = p[:, 0:258].rearrange('p (h d) -> p h d', h=2)
            self.ts(self.den[:, 0:2], a3[:, :, 128], TINY, None, ALU.max, None, [p], [self.den])
            self.recip(self.den[:, 0:2], self.den)
            self.tt(self.ox[:, hp * 256:(hp + 1) * 256].rearrange('p (h d) -> p h d', h=2), a3[:, :, 0:128], self.den[:, 0:2].unsqueeze(2).to_broadcast([128, 2, 128]), ALU.mult, [p, self.den], [self.ox])
        self.tr4(self.ox, self.oxT)

    def own_rest_a(self):
        p = self.proj('w_in', 'c5', 512)
        self.act(self.gr[:], p[:, :], AF.Silu, [p], [self.gr])
        p = self.proj('w_in', 'c6', 512)
        sq = self.headnorm(p[:, :], 4, 128, [p])
        self.tt(self.xqn[:].rearrange('p (h d) -> p h d', h=4), sq[:, 0:512].rearrange('p (h d) -> p h d', h=4),
                self.gxq[:].unsqueeze(1).to_broadcast([128, 4, 128]), ALU.mult, [sq, self.gains], [self.xqn])
        self.tr4(self.xqn, self.xqT)

    def mg_proj(self):
        for j in range(6):
            p = self.pa_next()
            for hf in range(2):
                w, wb = self.wget('w_in:mg:%d' % (2 * j + hf))
                for cc in range(2):
                    o_ = (hf * 2 + cc) * 128
                    for c in range(8):
                        self.mm(p[:, o_:o_ + 128], w[:, c, cc * 128:(cc + 1) * 128], self.hT[:, c, :], c == 0, c == 7, [wb, self.hT], [p])
            self.act(self.mgT[:, 4 * j:4 * j + 4, :].rearrange('p a b -> p (a b)'), p[:, :], AF.Sigmoid, [p], [self.mgT])

    def merge_ffn(self, xt, y_out, conv_out, conv_tok0_out, hist_only=False):
        k = self.k
        for br, (oT, nm) in enumerate(((self.onsaT, 'w_nsa_out'), (self.oglaT, 'w_gla_out'), (self.oxT, 'w_x_out'))):
            for half in range(2):
                p = self.pa_next()
                for hf in range(2):
                    w, wb = self.wget('%s:o:%d' % (nm, half * 2 + hf))
                    for cc in range(2):
                        o_ = (hf * 2 + cc) * 128
                        for c in range(4):
                            self.mm(p[:, o_:o_ + 128], w[:, c, cc * 128:(cc + 1) * 128], oT[:, c, :], c == 0, c == 3, [wb, oT], [p])
                p3 = p[:, :].rearrange('p (a b) -> p a b', a=4)
                mg = self.mgT[:, br * 8 + half * 4:br * 8 + half * 4 + 4, :]
                hs = slice(half * 4, half * 4 + 4)
                if br == 0:
                    self.tt(self.mrg[:, hs, :], p3, mg, ALU.mult, [p, self.mgT], [self.mrg])
                else:
                    self.tt(self.mtmp[:], p3, mg, ALU.mult, [p, self.mgT], [self.mtmp])
                    dst = self.mrg if br == 1 else self.mrgT
                    self.tt(dst[:, hs, :], self.mrg[:, hs, :], self.mtmp[:], ALU.add, [self.mrg, self.mtmp], [dst])
        if self.stop < 11:
            return
        for half in range(2):
            p = self.pa_next()
            for hf in range(2):
                w, wb = self.wget('w_o:o:%d' % (half * 2 + hf))
                for c in range(8):
                    self.mm(p[:, hf * 256:(hf + 1) * 256], self.mrgT[:, c, :], w[:, c, :], c == 0, c == 7, [self.mrgT, wb], [p])
            self.tt(self.x1[:, half * 512:(half + 1) * 512], p[:, :], xt[:, half * 512:(half + 1) * 512], ALU.add, [p, xt], [self.x1])
        if self.stop < 12:
            return
        self.norm_T(self.x1, 2, self.h2T)
        if self.stop < 13:
            return
        groups = [(j0, min(2, NFF - j0)) for j0 in range(0, NFF, 2)]
        for gi_, (j0, ng) in enumerate(groups):
            gx, ub = self.gx[gi_ % 2], self.ub[gi_ % 2]
            banks = []
            for jj in range(ng):
                w, wb = self.wget('w_up:u:%d' % (j0 + jj))
                if jj % 2 == 0:
                    p = self.pa_next()
                    banks.append(p)
                for cc in ((1,) if hist_only else (0, 1)):
                    o_ = ((jj % 2) * 2 + cc) * 128
                    for c in range(8):
                        self.mm(p[:, o_:o_ + 128], w[:, c, cc * 128:(cc + 1) * 128], self.h2T[:, c, :], c == 0, c == 7, [wb, self.h2T], [p])
            if hist_only:
                p4 = p[:, 0:ng * 256].rearrange('p (a t b) -> p a t b', t=2, b=128)
                self.cp(self.hist[:, j0:j0 + ng, :], p4[:, :, 1, 126:128], [p], [self.hist])
                continue
            self.cp(gx[:, 0:ng, 0:2], self.hist[:, j0:j0 + ng, :], [self.hist], [gx], eng='pool')
            for bi, p in enumerate(banks):
                nb_ = min(2, ng - 2 * bi)
                p4 = p[:, 0:nb_ * 256].rearrange('p (a t b) -> p a t b', t=2, b=128)
                self.act(ub[:, 2 * bi:2 * bi + nb_, :], p4[:, :, 0, :], AF.Copy, [p], [ub])
                self.cp(gx[:, 2 * bi:2 * bi + nb_, 2:130], p4[:, :, 1, :], [p], [gx])
            self.cp(self.hist[:, j0:j0 + ng, :], gx[:, 0:ng, 128:130], [gx], [self.hist], eng='pool')
            if conv_tok0_out is not None:
                self.cp(self.g0col[:, j0:j0 + ng], gx[:, 0:ng, 2], [gx], [self.g0col], eng='pool')
            cw = lambda t: self.convw[:, j0:j0 + ng, t:t + 1].to_broadcast([128, ng, 128])
            a_, b_ = self.cva[:, 0:ng, :], self.cvb[:, 0:ng, :]
            c_ = self.cvc[:, 0:ng, :]
            self.tt(b_, gx[:, 0:ng, 1:129], cw(1), ALU.mult, [gx, self.gains], [self.cvb], eng='pool')
            self.tt(c_, gx[:, 0:ng, 2:130], cw(2), ALU.mult, [gx, self.gains], [self.cvc], eng='pool')
            self.tt(a_, gx[:, 0:ng, 0:128], cw(0), ALU.mult, [gx, self.gains], [self.cva])
            self.tt(a_, a_, b_, ALU.add, [self.cva, self.cvb], [self.cva])
            self.tt(a_, a_, c_, ALU.add, [self.cva, self.cvc], [self.cva])
            self.tt(a_, a_, self.convb[:, j0:j0 + ng].unsqueeze(2).to_broadcast([128, ng, 128]), ALU.add, [self.cva, self.gains], [self.cva])
            self.gelu_tanh(b_, a_, b_, [self.cva], [self.cvb], self.cvb)
            self.tt(self.actT[:, j0:j0 + ng, :], b_, ub[:, 0:ng, :], ALU.mult, [self.cvb, ub], [self.actT])
        if hist_only:
            return
        if self.stop < 14:
            return
        with k.nc.allow_non_contiguous_dma(reason="conv state columns written transposed (<= 2 x 2816 elements)"):
            if conv_out is not None:
                for t_ in range(2):
                    k.dma('sp', conv_out[t_].rearrange('(j p) -> p j', p=128), self.hist[:, :, t_], reads=[self.hist])
            if conv_tok0_out is not None:
                k.dma('sp', conv_tok0_out.rearrange('t (j p) -> p (t j)', p=128), self.g0col[:], reads=[self.g0col])
        if self.stop < 15:
            return
        p0, p1 = self.pa_next(), self.pa_next()
        for j in range(11):
            w, wb = self.wget('w_down:d:%d' % j)
            for c in range(2):
                f = 2 * j + c
                self.mm(p0[:, :], self.actT[:, f, :], w[:, c, 0:512], f == 0, f == NFF - 1, [self.actT, wb], [p0])
                self.mm(p1[:, :], self.actT[:, f, :], w[:, c, 512:1024], f == 0, f == NFF - 1, [self.actT, wb], [p1])
        self.tt(self.yt[:, 0:512], p0[:, :], self.x1[:, 0:512], ALU.add, [p0, self.x1], [self.yt])
        self.tt(self.yt[:, 512:1024], p1[:, :], self.x1[:, 512:1024], ALU.add, [p1, self.x1], [self.yt])
        if y_out is not None:
            k.dma('sp', y_out[0], self.yt[y_out[1], :], reads=[self.yt])

    def q_tile(self, slot, xt, mset, S, Sb, rowmask_col, rows_out, win_out, y_out, conv_out, conv_tok0_out=None):
        mk_ = lambda nm: self.marks.append(('  %d:%s' % (slot, nm), dict(self.k.cnt))) if slot in (48, 63, 64) else None
        self.stage_a(slot, xt, True, True, rows_out, win_out)
        mk_('stage_a')
        if self.stop < 6:
            return
        self.own_rest_a()
        mk_('rest_a')
        if self.stop < 7:
            return
        self.nsa_tile(slot, mset, nq=(32 if slot == 64 else 128))
        mk_('nsa')
        if self.stop < 8:
            return
        self.gla_tile(S, Sb, rowmask_col)
        mk_('gla')
        if self.stop < 9:
            return
        self.xatt_tile()
        mk_('xatt')
        if self.stop < 10:
            return
        self.merge_ffn(xt, y_out, conv_out, conv_tok0_out, hist_only=(slot == FIRST_Q and y_out is None))
        mk_('merge_ffn')

    def mem_to_resident(self, mt, memrows):
        p = self.pa_next()
        for h in range(4):
            self.tr(p[:, h * 128:(h + 1) * 128], memrows[:, h * 128:(h + 1) * 128], True, [memrows], [p])
        self.cp(self.k_memT[:, :, mt * 128:(mt + 1) * 128], p[:, :].rearrange('p (h m) -> p h m', h=4), [p], [self.k_memT])
        self.cp(self.v_mem[:, mt, :, 0:128], memrows[:, 512:1024].rearrange('p (h d) -> p h d', h=4), [memrows], [self.v_mem], eng='pool')

    def mem_kv_prompt(self):
        k = self.k
        self.sched_cols('w_mem_kv', 'm', 0, 1024)
        self.sched_cols('w_mem_kv', 'm2', 0, 1024)
        for mt in range(2):
            xt = self.xt[mt % len(self.xt)]
            self.load_x(xt, self.i['mem'][mt * 128:(mt + 1) * 128, :])
            self.norm_T(xt, 1, self.hT)
            tg = 'm' if mt == 0 else 'm2'
            for half in range(2):
                p = self.pa_next()
                for hf in range(2):
                    w, wb = self.wget('w_mem_kv:%s:%d' % (tg, half * 2 + hf))
                    for c in range(8):
                        self.mm(p[:, hf * 256:(hf + 1) * 256], self.hT[:, c, :], w[:, c, :], c == 0, c == 7, [self.hT, wb], [p])
                if half == 0:
                    sq = self.headnorm(p[:, :], 4, 128, [p])
                    self.tt(self.memrows[:, 0:512].rearrange('p (h d) -> p h d', h=4), sq[:, 0:512].rearrange('p (h d) -> p h d', h=4),
                            self.gxk[:].unsqueeze(1).to_broadcast([128, 4, 128]), ALU.mult, [sq, self.gains], [self.memrows])
                else:
                    self.act(self.memrows[:, 512:1024], p[:, :], AF.Copy, [p], [self.memrows])
            k.dma('sp', self.o['memkv_p'][mt * 128:(mt + 1) * 128, :], self.memrows[:], reads=[self.memrows])
            self.mem_to_resident(mt, self.memrows)

    def prompt_phase(self):
        k, i, o = self.k, self.i, self.o
        self.mem_kv_prompt()
        if self.stop < 3:
            return
        slots = list(range(FIRST_Q - self.n_prefix, FIRST_Q + self.n_own))
        self.p_slots = slots
        for s in slots:
            self.sched_tile(s >= FIRST_Q, s >= FIRST_Q - 4, hist_only=(s == FIRST_Q))
        if self.do_sample:
            for _ in range(self.n_samp):
                self.sched_tile(True)
        self.load_x(self.xt[0], i['xloc'][slots[0] * 128:(slots[0] + 1) * 128, :])
        self.marks = [('start', dict(self.k.cnt))]
        for n, s in enumerate(slots):
            self.marks.append(('slot%d' % s, dict(self.k.cnt)))
            xt = self.xt[n % len(self.xt)]
            if n + 1 < len(slots):
                s2 = slots[n + 1]
                nxt_x = (self.xt[(n + 1) % len(self.xt)], i['xloc'][s2 * 128:(s2 + 1) * 128, :])
                if len(self.xt) > 1:
                    self.load_x(*nxt_x)
            ncast = -(-len(self.late_casts) // max(1, (FIRST_Q - s))) if s < FIRST_Q else len(self.late_casts)
            for _ in range(ncast):
                nm_, dst_, src_ = self.late_casts.pop(0)
                k.dma('pool', dst_, src_, writes=[self.sbuf_w[nm_]])
            if s < FIRST_Q:
                gi = n % 4
                self.stage_a(s, xt, False, s >= FIRST_Q - 4, gi=gi, flush=(gi == 3 or slots[n + 1] >= FIRST_Q))
                if self.stop < 4:
                    continue
                self.la_compute(self.e0[:, 0:1])
                self.gla_state_update(self.S, self.Sb, False)
                if self.stop < 5:
                    return
            else:
                t = s - 48
                rows_out = (o['rows_p'][t * 128:(t + 1) * 128, :], slice(0, 128)) if t >= 0 else None
                win_out = (o['win_p'][(t - 12) * 128:(t - 11) * 128, :], slice(0, 128)) if t >= 12 else None
                y_out = (o['y_p'][t * 128:(t + 1) * 128, :], slice(0, 128)) if t >= 0 else None
                conv_out = o['conv_p'] if s == slots[-1] else None
                self.q_tile(s, xt, 0, self.S, self.Sb, self.e0[:, 0:1], rows_out, win_out, y_out, conv_out)
                pass
            if len(self.xt) == 1 and n + 1 < len(slots):
                self.load_x(*nxt_x)
            if s == FIRST_Q and self.stop >= 99:
                if True:
                    self.ts(self.hist[:], self.hist[:], self.hs[:, 0:1], None, ALU.mult, None, [self.hist, self.hs], [self.hist])
        k.dma('sp', o['gla_p'].rearrange('h d v -> d h v'), self.S[:], reads=[self.S])

    def sample_phase(self):
        k, i, o = self.k, self.i, self.o
        ns = self.n_samp
        xs_t = self.xt[0]
        k.dma('sp', self.ptb[:], i['page_table'].rearrange('s j -> (s j)').partition_broadcast(128), writes=[self.ptb])
        k.op('pool', lambda e: e.iota(self.iop[:], pattern=[[0, 1]], base=0, channel_multiplier=1), (), [self.iop])
        self.cp(self.e1[:, 0:1], self.iop[:], [self.iop], [self.e1])
        self.cp(self.la[:, 0:ns * 64], self.ptb[:], [self.ptb], [self.la])
        self.ts(self.la[:, 0:ns * 64], self.la[:, 0:ns * 64], 128.0, self.e1[:, 0:1], ALU.mult, ALU.add, [self.la, self.e1], [self.la])
        self.cp(self.idx[:], self.la[:, 0:ns * 64], [self.la], [self.idx])
        self.memset(xs_t[:], 0.0, [xs_t])
        for s in range(ns):
            self.marks.append(('fill%d' % s, dict(self.k.cnt)))
            def gather(j):
                pg = self.page[j % 2]
                col = s * 64 + j
                k.dma('pool', None, None, reads=[self.idx], writes=[pg],
                      fn=lambda e, pg=pg, col=col: e.indirect_dma_start(out=pg[:], out_offset=None, in_=i['cache_kv'],
                                                                        in_offset=bass.IndirectOffsetOnAxis(ap=self.idx[:, col:col + 1], axis=0)))
            gather(0)
            for j in range(64):
                pg = self.page[j % 2]
                if j + 1 < 64:
                    gather(j + 1)
                pgT = T(pg.t[:].rearrange('p (a b) -> p a b', a=4)); pgT.b = pg.b
                self.a2_rows(j, pgT, j % 4)
                if j % 4 == 3:
                    self.compress(j - 3, 4)
            for t in range(4):
                k.dma('sp', self.winrows[:].rearrange('p a b -> p (a b)'), i['cache_win'][s, t * 128:(t + 1) * 128, :], writes=[self.winrows])
                self.a2_win(60 + t, self.winrows)
            k.dma('sp', self.S[:], i['state_gla'][s].rearrange('h d v -> d h v'), writes=[self.S])
            self.cp(self.Sb[:], self.S[:], [self.S], [self.Sb])
            with k.nc.allow_non_contiguous_dma(reason="conv state loaded transposed (2 x 2816 elements)"):
                for t_ in range(2):
                    k.dma('sp', self.hist[:, :, t_], i['state_conv'][s, t_].rearrange('(j p) -> p j', p=128), writes=[self.hist])
            for mt in range(2):
                k.dma('sp', self.memrows[:], i['cache_mem'][s, mt * 128:(mt + 1) * 128, :], writes=[self.memrows])
                self.mem_to_resident(mt, self.memrows)
            k.dma('sp', xs_t[0:1, :], i['xs'][s:s + 1, :], writes=[xs_t])
            k.dma('sp', o['win_s'][s, 0:511, :], i['cache_win'][s, 1:512, :])
            k.dma('sp', o['conv_s'][s, 0:1, :], i['state_conv'][s, 1:2, :])
            self.marks.append(('sq%d' % s, dict(self.k.cnt)))
            self.q_tile(64, xs_t, 1, self.S, self.Sb, self.e0[:, 1:2],
                        (o['rows_s'][s:s + 1, :], slice(0, 1)), (o['win_s'][s, 511:512, :], slice(0, 1)),
                        (o['y_s'][s:s + 1, :], slice(0, 1)), None, conv_tok0_out=o['conv_s'][s, 1:2, :])
            k.dma('sp', o['gla_s'][s].rearrange('h d v -> d h v'), self.S[:], reads=[self.S])


def _t5_bucket_np(d):
    n = np.maximum(d, 0)
    nf = np.maximum(n, 1).astype(np.float32)
    large = 16 + (np.log(nf / np.float32(16)) / np.float32(np.log(128 / 16)) * np.float32(16)).astype(np.int32)
    return np.where(n < 16, n, np.minimum(large, 31))


def _constants():
    c = {}
    c['c_ident'] = np.eye(128, dtype=np.float32)
    s = np.arange(128)
    c['c_U'] = (s[:, None] <= s[None, :]).astype(np.float32)
    c['c_L'] = (s[:, None] > s[None, :]).astype(np.float32)
    c['c_onesbd'] = np.kron(np.eye(2), np.ones((64, 64))).astype(np.float32)
    c['c_edge'] = (NEG * (s[:, None] >= s[None, :])).astype(np.float32).astype(ml_dtypes.bfloat16)
    d = np.arange(DMIN, DMAX + 1)
    oh = np.zeros((33, ND), np.float32)
    valid = (d >= 0) & (d < 512)
    b = _t5_bucket_np(d)
    oh[b[valid], np.nonzero(valid)[0]] = 1.0
    oh[32, ~valid] = 1.0
    c['c_ohda'] = oh
    c['c_ohdr'] = np.ascontiguousarray(oh[:, ::-1])
    relb = np.zeros((128, 3), np.float32)
    relb[:64, 0] = 1e4
    relb[:, 1] = 1e4
    relb[64:, 2] = 1e4
    relb[:64, 2] = -1e30
    c['c_relb'] = relb
    e0 = np.zeros((128, 2), np.float32)
    e0[:, 0] = 1.0
    e0[0, 1] = 1.0
    c['c_e0'] = e0
    return c


def _masks(npre):
    sb = np.zeros((2, 128, NSLOT), np.float32)
    sb[0, :, :npre] = NEG
    pm = np.full((2, 128, 136), NEG, np.float32)
    pm[0, :, :2 * npre] += NEG
    cm = np.zeros((2, 1, 528), np.float32)
    cm[0, 0, :8 * npre + 1] = NEG
    cm[1, 0, 0] = NEG
    fb = np.zeros((2, 128, 136), np.float32)
    fb[0, :, 2 * npre] = 1e4
    fb[1, :, 0] = 1e4
    hs = np.full((128, 1), 0.0 if npre > FIRST_Q else 1.0, np.float32)
    return dict(m_slotbias=sb, m_pm=pm, m_cmask=cm, m_fbabs=fb, m_hs=hs)


_CACHE = {}


def _get_nc(key, **kw):
    if key not in _CACHE:
        _CACHE[key] = MK(**kw).nc
    return _CACHE[key]


def kernel(x_prompt, x_sample, cache_kv, cache_win, state_gla, state_conv, cache_mem, page_table, mem_prompt,
           g_mix, w_in, g_nsa_q, g_nsa_k, cmp_k_pe, cmp_k_w1, cmp_k_w2, cmp_v_pe, cmp_v_w1, cmp_v_w2,
           rel_bias, w_gla_gate, b_gla_gate, g_gla_o, g_mem, w_mem_kv, g_x_q, g_x_k,
           w_nsa_out, w_gla_out, w_x_out, w_o, g_ffn, w_up, conv_w, conv_b, w_down):
    f = lambda a: np.ascontiguousarray(np.asarray(a, dtype=np.float32))
    x_prompt = f(x_prompt); x_sample = f(x_sample).reshape(32, D)
    cache_kv2 = f(cache_kv).reshape(-1, 512)
    cache_win2 = f(cache_win).reshape(32, 512, 256)
    state_gla = f(state_gla); state_conv = f(state_conv)
    cache_mem2 = f(cache_mem).reshape(32, 256, 1024)
    page_table = np.ascontiguousarray(np.asarray(page_table, dtype=np.int32))
    mem_prompt = f(mem_prompt)
    w_up_p = f(w_up).reshape(D, 2, NFF, 128).transpose(0, 2, 1, 3).reshape(D, 2 * DFF)
    shared = dict(
        g_mix=f(g_mix), g_mem=f(g_mem), g_ffn=f(g_ffn), w_in=np.ascontiguousarray(f(w_in)[:, W_IN_PERM]),
        g_nsa_q=f(g_nsa_q), g_nsa_k=f(g_nsa_k), cmp_k_pe=f(cmp_k_pe), cmp_k_w1=f(cmp_k_w1), cmp_k_w2=f(cmp_k_w2),
        cmp_v_pe=f(cmp_v_pe), cmp_v_w1=f(cmp_v_w1), cmp_v_w2=f(cmp_v_w2), rel_bias=f(rel_bias),
        w_gla_gate=f(w_gla_gate), b_gla_gate=f(b_gla_gate).reshape(1, 256), g_gla_o=f(g_gla_o), g_x_q=f(g_x_q), g_x_k=f(g_x_k),
        w_mem_kv=f(w_mem_kv), w_nsa_out=f(w_nsa_out), w_gla_out=f(w_gla_out), w_x_out=f(w_x_out), w_o=f(w_o),
        w_up=np.ascontiguousarray(w_up_p), conv_w=f(conv_w), conv_b=f(conv_b).reshape(1, DFF), w_down=f(w_down),
        cache_kv=cache_kv2)
    shared.update(_constants())
    in_maps = []
    for core in range(8):
        b, c = core // 4, core % 4
        npre = 48 - 16 * c
        xloc = np.zeros((64 * 128, D), np.float32)
        xloc[npre * 128:] = x_prompt[b, :(64 - npre) * 128] if False else x_prompt[b, (16 * (c + 1) - (64 - npre)) * 128:16 * (c + 1) * 128]
        m = dict(shared)
        m.update(_masks(npre))
        sl = slice(4 * core, 4 * core + 4)
        m.update(xloc=xloc, xs=x_sample[sl], cache_win=cache_win2[sl], state_gla=state_gla[sl], state_conv=state_conv[sl],
                 cache_mem=cache_mem2[sl], page_table=page_table[sl], mem=mem_prompt[b])
        in_maps.append(m)
    nc = _get_nc('full')
    res = run_bass_kernel_spmd(nc, in_maps, core_ids=list(range(8))).results
    r = lambda core, nm: np.asarray(res[core][nm], dtype=np.float32)
    y_p = np.stack([np.concatenate([r(4 * b + c, 'y_p') for c in range(4)], 0) for b in range(2)])
    rows_p = np.stack([np.concatenate([r(4 * b + c, 'rows_p') for c in range(4)], 0) for b in range(2)]).reshape(2, 8192, 4, 2, 64)
    win_p = np.stack([r(4 * b + 3, 'win_p') for b in range(2)]).reshape(2, 512, 2, 2, 64)
    gla_p = np.stack([r(4 * b + 3, 'gla_p') for b in range(2)])
    conv_p = np.stack([r(4 * b + 3, 'conv_p') for b in range(2)])
    memkv_p = np.stack([r(4 * b, 'memkv_p') for b in range(2)]).reshape(2, 256, 2, 4, 128)
    cat = lambda nm: np.concatenate([r(core, nm) for core in range(8)], 0)
    y_s = cat('y_s').reshape(32, 1, D)
    rows_s = cat('rows_s').reshape(32, 1, 4, 2, 64)
    win_s = cat('win_s').reshape(32, 512, 2, 2, 64)
    gla_s = cat('gla_s')
    conv_s = cat('conv_s')
    return (y_p, y_s, rows_p, win_p, gla_p, conv_p, memkv_p, rows_s, win_s, gla_s, conv_s)
```
